# Optimizing a Trainium2 kernel written in Bass

```python
import jax, jax.numpy as jnp
from jax import lax
import numpy as np

D_MODEL = 2048
BATCH = 4
SEQ = 2048
DEPTH = 4

GRID_W = 64
CTX_LEN = 256
N_MIXERS = 3
N_RET = (DEPTH + 2) // 3
N_GMLP = (DEPTH + 1) // 3
N_CONV = DEPTH // 3

D_FF = 5632
FFN_RES_WEIGHT = 0.5

RET_HEADS = 8
RET_DK = D_MODEL // RET_HEADS
RET_DV = 2 * RET_DK
RET_QK = RET_HEADS * RET_DK
RET_V = RET_HEADS * RET_DV
RET_IN = 2 * RET_QK + 2 * RET_V
RET_CHUNK = 128
ROPE_BASE = 10000.0

GMLP_CHUNK = 128
GMLP_GROUPS = 8
GMLP_E = 3 * D_MODEL

CONV_K = 31
EPS = 1e-6

kernel_name = 'hybrid_retention_gmlp_conformer_macaron'


def standardize(x):
    x32 = x.astype(jnp.float32)
    xc = x32 - jnp.mean(x32, -1, keepdims=True)
    return xc * lax.rsqrt(jnp.mean(xc * xc, -1, keepdims=True) + EPS)


def layer_norm(x, g, b):
    return (standardize(x) * g.astype(jnp.float32) + b.astype(jnp.float32)).astype(x.dtype)


def rms_norm(x, g):
    x32 = x.astype(jnp.float32)
    y = x32 * lax.rsqrt(jnp.mean(x32 * x32, -1, keepdims=True) + EPS)
    return (y * g.astype(jnp.float32)).astype(x.dtype)


def modulate_in(t, g_pre, shift, scale):
    return rms_norm(t, g_pre) * (1 + scale) + shift


def gated_residual(t, y, g_post, gate, weight):
    return t + weight * gate * rms_norm(y, g_post)


def swiglu(h, w_in, w_out):
    a, b = jnp.split(h @ w_in, 2, axis=-1)
    return (jax.nn.silu(a) * b) @ w_out


def ffn_sublayer(t, m, g_pre, g_post, w_in, w_out, s):
    h = modulate_in(t, g_pre, m[:, :, 3 * s], m[:, :, 3 * s + 1])
    return gated_residual(t, swiglu(h, w_in, w_out), g_post, m[:, :, 3 * s + 2], FFN_RES_WEIGHT)


def rope_1d(x, pos):
    half = x.shape[-1] // 2
    inv = ROPE_BASE ** (-jnp.arange(half, dtype=jnp.float32) / half)
    ang = pos[:, None] * inv[None, :]
    cos = jnp.cos(ang).astype(x.dtype)
    sin = jnp.sin(ang).astype(x.dtype)
    x1, x2 = x[..., :half], x[..., half:]
    return jnp.concatenate([x1 * cos - x2 * sin, x1 * sin + x2 * cos], axis=-1)


def axial_rope(x, rows, cols):
    half = x.shape[-1] // 2
    return jnp.concatenate([rope_1d(x[..., :half], rows), rope_1d(x[..., half:], cols)], axis=-1)


def to_heads(t, d):
    b, l, _ = t.shape
    return t.reshape(b, l, -1, d).transpose(0, 2, 1, 3)


def ret_scan(q, k, v, log_g, s0, include_diag):
    b, h, l, _ = q.shape
    dv = v.shape[-1]
    n = l // RET_CHUNK

    def chunks(t):
        return t.reshape(b, h, n, RET_CHUNK, t.shape[-1]).transpose(2, 0, 1, 3, 4)

    pos = jnp.arange(RET_CHUNK, dtype=jnp.float32)
    diff = pos[:, None] - pos[None, :]
    mask = (diff >= (0.0 if include_diag else 1.0))[None]
    d_in = jnp.where(mask, jnp.exp(jnp.where(mask, diff[None], 0.0) * log_g[:, None, None]), 0.0).astype(q.dtype)
    q_dec = jnp.exp((pos + 1.0)[None, :] * log_g[:, None])[None, :, :, None]
    k_dec = jnp.exp((RET_CHUNK - 1.0 - pos)[None, :] * log_g[:, None])[None, :, :, None].astype(k.dtype)
    c_dec = jnp.exp(RET_CHUNK * log_g)[None, :, None, None]

    def step(s, qkv):
        qc, kc, vc = qkv
        a = jnp.einsum('bhnd,bhmd->bhnm', qc, kc) * d_in
        o = (jnp.einsum('bhnm,bhme->bhne', a, vc).astype(jnp.float32)
             + jnp.einsum('bhnd,bhde->bhne', qc.astype(jnp.float32), s) * q_dec)
        s = s * c_dec + jnp.einsum('bhmd,bhme->bhde', kc * k_dec, vc).astype(jnp.float32)
        return s, o

    s, o = lax.scan(step, s0, (chunks(q), chunks(k), chunks(v)))
    o = o.transpose(1, 2, 0, 3, 4).reshape(b, h, l, dv).astype(q.dtype)
    return o, s


def ret_final_state(k, v, log_g, reverse):
    l = k.shape[2]
    pos = jnp.arange(l, dtype=jnp.float32)
    expo = pos if reverse else (l - 1.0 - pos)
    w = jnp.exp(expo[None, :] * log_g[:, None])[None, :, :, None].astype(k.dtype)
    return jnp.einsum('bhld,bhle->bhde', k * w, v).astype(jnp.float32)


def ret_output(o, g, gn_g, w_out):
    b, _, l, _ = o.shape
    on = standardize(o.transpose(0, 2, 1, 3)).reshape(b, l, RET_V) * gn_g.astype(jnp.float32)
    return (jax.nn.silu(g) * on.astype(g.dtype)) @ w_out


def retention_mixer(h_lat, h_ctx, w_in, w_out, decay_logit, gn_g, rows, cols, ctx_out):
    log_g = jax.nn.log_sigmoid(decay_logit.astype(jnp.float32))
    k_scale = RET_DK ** -0.5
    p = h_lat @ w_in
    q = axial_rope(to_heads(p[..., :RET_QK], RET_DK), rows, cols)
    k = axial_rope(to_heads(p[..., RET_QK:2 * RET_QK], RET_DK), rows, cols) * k_scale
    v = to_heads(p[..., 2 * RET_QK:2 * RET_QK + RET_V], RET_DV)
    g = p[..., 2 * RET_QK + RET_V:]
    bsz = h_ctx.shape[0]
    if ctx_out:
        pc = h_ctx @ w_in
        qc = to_heads(pc[..., :RET_QK], RET_DK)
        kc = to_heads(pc[..., RET_QK:2 * RET_QK], RET_DK) * k_scale
        vc = to_heads(pc[..., 2 * RET_QK:2 * RET_QK + RET_V], RET_DV)
        gc = pc[..., 2 * RET_QK + RET_V:]
        s0 = jnp.zeros((bsz, RET_HEADS, RET_DK, RET_DV), jnp.float32)
        oc_f, s_f = ret_scan(qc, kc, vc, log_g[0], s0, True)
        oc_b, s_b = ret_scan(jnp.flip(qc, 2), jnp.flip(kc, 2), jnp.flip(vc, 2), log_g[1], s0, False)
        y_ctx = ret_output(oc_f + jnp.flip(oc_b, 2), gc, gn_g, w_out)
    else:
        kc = to_heads(h_ctx @ w_in[:, RET_QK:2 * RET_QK], RET_DK) * k_scale
        vc = to_heads(h_ctx @ w_in[:, 2 * RET_QK:2 * RET_QK + RET_V], RET_DV)
        s_f = ret_final_state(kc, vc, log_g[0], False)
        s_b = ret_final_state(kc, vc, log_g[1], True)
        y_ctx = None
    o_f, _ = ret_scan(q, k, v, log_g[0], s_f, True)
    o_b, _ = ret_scan(jnp.flip(q, 2), jnp.flip(k, 2), jnp.flip(v, 2), log_g[1], s_b, False)
    y_lat = ret_output(o_f + jnp.flip(o_b, 2), g, gn_g, w_out)
    return y_lat, y_ctx


def gmlp_mixer(h, w_in, ln_g, ln_b, w_s, b_s, w_out):
    z = jax.nn.gelu(h @ w_in)
    u, v = z[..., :GMLP_E], z[..., GMLP_E:]
    v = layer_norm(v, ln_g, ln_b)
    b, l, _ = v.shape
    v = v.reshape(b, l // GMLP_CHUNK, GMLP_CHUNK, GMLP_GROUPS, GMLP_E // GMLP_GROUPS)
    v = jnp.einsum('gnm,bkmge->bknge', w_s, v) + b_s.T[None, None, :, :, None]
    return (u * v.reshape(b, l, GMLP_E)) @ w_out


def conv_mixer(h, w_pw1, w_dw, b_dw, ln_g, ln_b, w_pw2):
    a, b = jnp.split(h @ w_pw1, 2, axis=-1)
    z = a * jax.nn.sigmoid(b)
    z = lax.conv_general_dilated(z, w_dw[:, None, :], (1,), [(CONV_K // 2, CONV_K // 2)],
                                 dimension_numbers=('NWC', 'WIO', 'NWC'),
                                 feature_group_count=D_MODEL) + b_dw
    z = jax.nn.silu(layer_norm(z, ln_g, ln_b))
    return z @ w_pw2


def setup_inputs(seed: int = 0) -> dict:
    key = jax.random.key(seed)
    ks = jax.random.split(key, 32)
    f32 = jnp.float32

    def nrm(k, shape, scale):
        return jax.random.normal(k, shape, f32) * scale

    base = 1.0 - 2.0 ** (-5.0 - jnp.arange(RET_HEADS, dtype=f32))
    logit = jnp.log(base) - jnp.log1p(-base)
    return {
        'x': nrm(ks[0], (BATCH, SEQ, D_MODEL), 1.0),
        'c': nrm(ks[1], (BATCH, D_MODEL), 1.0),
        'ctx': nrm(ks[2], (BATCH, CTX_LEN, D_MODEL), 1.0),
        'c_ctx': nrm(ks[3], (D_MODEL,), 1.0),
        'ada_w': nrm(ks[4], (DEPTH, D_MODEL, 9 * D_MODEL), D_MODEL ** -0.5),
        'ada_b': nrm(ks[5], (DEPTH, 9 * D_MODEL), 0.02),
        'norm_g': 1.0 + nrm(ks[6], (DEPTH, 3, 2, D_MODEL), 0.02),
        'ffn_w_in': nrm(ks[7], (DEPTH, 2, D_MODEL, 2 * D_FF), D_MODEL ** -0.5),
        'ffn_w_out': nrm(ks[8], (DEPTH, 2, D_FF, D_MODEL), D_FF ** -0.5),
        'ret_w_in': nrm(ks[9], (N_RET, D_MODEL, RET_IN), D_MODEL ** -0.5),
        'ret_w_out': nrm(ks[10], (N_RET, RET_V, D_MODEL), RET_V ** -0.5),
        'ret_decay_logit': logit + nrm(ks[11], (N_RET, 2, RET_HEADS), 0.1),
        'ret_gn_g': 1.0 + nrm(ks[12], (N_RET, RET_V), 0.02),
        'gmlp_w_in': nrm(ks[13], (N_GMLP, D_MODEL, 2 * GMLP_E), D_MODEL ** -0.5),
        'gmlp_ln_g': 1.0 + nrm(ks[14], (N_GMLP, GMLP_E), 0.02),
        'gmlp_ln_b': nrm(ks[15], (N_GMLP, GMLP_E), 0.02),
        'gmlp_w_s': nrm(ks[16], (N_GMLP, GMLP_GROUPS, GMLP_CHUNK, GMLP_CHUNK), GMLP_CHUNK ** -0.5),
        'gmlp_b_s': 1.0 + nrm(ks[17], (N_GMLP, GMLP_GROUPS, GMLP_CHUNK), 0.1),
        'gmlp_w_out': nrm(ks[18], (N_GMLP, GMLP_E, D_MODEL), GMLP_E ** -0.5),
        'conv_w_pw1': nrm(ks[19], (N_CONV, D_MODEL, 2 * D_MODEL), D_MODEL ** -0.5),
        'conv_w_dw': nrm(ks[20], (N_CONV, CONV_K, D_MODEL), CONV_K ** -0.5),
        'conv_b_dw': nrm(ks[21], (N_CONV, D_MODEL), 0.02),
        'conv_ln_g': 1.0 + nrm(ks[22], (N_CONV, D_MODEL), 0.02),
        'conv_ln_b': nrm(ks[23], (N_CONV, D_MODEL), 0.02),
        'conv_w_pw2': nrm(ks[24], (N_CONV, D_MODEL, D_MODEL), D_MODEL ** -0.5),
    }


def reference(x, c, ctx, c_ctx, ada_w, ada_b, norm_g, ffn_w_in, ffn_w_out, ret_w_in, ret_w_out,
              ret_decay_logit, ret_gn_g, gmlp_w_in, gmlp_ln_g, gmlp_ln_b, gmlp_w_s, gmlp_b_s, gmlp_w_out,
              conv_w_pw1, conv_w_dw, conv_b_dw, conv_ln_g, conv_ln_b, conv_w_pw2):
    n_tok = x.shape[1]
    rows_n = n_tok // GRID_W
    rows = jnp.repeat(jnp.arange(rows_n, dtype=jnp.float32), GRID_W)
    cols = jnp.tile(jnp.arange(GRID_W, dtype=jnp.float32), rows_n)
    sc = jax.nn.silu(c)
    sctx = jax.nn.silu(c_ctx)
    for i in range(DEPTH):
        kind = i % N_MIXERS
        inst = i // N_MIXERS
        last = i == DEPTH - 1
        ctx_out = not last
        ctx_needed = ctx_out or kind == 0
        m_lat = (sc @ ada_w[i] + ada_b[i]).reshape(sc.shape[0], 1, 9, D_MODEL)
        m_ctx = (sctx @ ada_w[i] + ada_b[i]).reshape(1, 1, 9, D_MODEL)
        g = norm_g[i]
        x = ffn_sublayer(x, m_lat, g[0, 0], g[0, 1], ffn_w_in[i, 0], ffn_w_out[i, 0], 0)
        if ctx_needed:
            ctx = ffn_sublayer(ctx, m_ctx, g[0, 0], g[0, 1], ffn_w_in[i, 0], ffn_w_out[i, 0], 0)
        h_lat = modulate_in(x, g[1, 0], m_lat[:, :, 3], m_lat[:, :, 4])
        h_ctx = modulate_in(ctx, g[1, 0], m_ctx[:, :, 3], m_ctx[:, :, 4]) if ctx_needed else None
        if kind == 0:
            y_lat, y_ctx = retention_mixer(h_lat, h_ctx, ret_w_in[inst], ret_w_out[inst],
                                           ret_decay_logit[inst], ret_gn_g[inst], rows, cols, ctx_out)
        elif kind == 1:
            gm = (gmlp_w_in[inst], gmlp_ln_g[inst], gmlp_ln_b[inst], gmlp_w_s[inst], gmlp_b_s[inst], gmlp_w_out[inst])
            y_lat = gmlp_mixer(h_lat, *gm)
            y_ctx = gmlp_mixer(h_ctx, *gm) if ctx_out else None
        else:
            cm = (conv_w_pw1[inst], conv_w_dw[inst], conv_b_dw[inst], conv_ln_g[inst], conv_ln_b[inst], conv_w_pw2[inst])
            y_lat = conv_mixer(h_lat, *cm)
            y_ctx = conv_mixer(h_ctx, *cm) if ctx_out else None
        x = gated_residual(x, y_lat, g[1, 1], m_lat[:, :, 5], 1.0)
        if ctx_out:
            ctx = gated_residual(ctx, y_ctx, g[1, 1], m_ctx[:, :, 5], 1.0)
        x = ffn_sublayer(x, m_lat, g[2, 0], g[2, 1], ffn_w_in[i, 1], ffn_w_out[i, 1], 2)
        if ctx_out:
            ctx = ffn_sublayer(ctx, m_ctx, g[2, 0], g[2, 1], ffn_w_in[i, 1], ffn_w_out[i, 1], 2)
    return x
```

```python
import numpy as np
from contextlib import ExitStack
import concourse.bass as bass
import concourse.mybir as mybir
from concourse.bass_utils import run_bass_kernel_spmd

F32 = mybir.dt.float32
BF16 = mybir.dt.bfloat16
AF = mybir.ActivationFunctionType
ALU = mybir.AluOpType

D = 2048
DC = D // 128
DEPTH = 4
NB = 4
SEQ = 2048
CTX = 256
T = CTX + SEQ
DFF = 5632
FC = DFF // 128
EPS = 1e-6
RET_H = 8
RET_DK = 256
RET_DV = 512
RET_QK = 2048
RET_V = 4096
GE = 6144
GG = 8
CONV_K = 31
N_CORES = 4

TILES3 = [(0, 256, True), (256, 512, False), (768, 384, False), (1152, 384, False), (1536, 384, False), (1920, 384, False)]
BLOCKS3 = [(0, TILES3[0:2]), (768, TILES3[2:4]), (1536, TILES3[4:6])]
TILES = TILES3
BLOCKS = BLOCKS3
BW = 768

ENGS = ("pe", "act", "dve", "pool", "sp")
N_DMA_SEMS = 8
NSLOT = 6


class Op:
    __slots__ = ("eng", "fn", "deps", "need_sig", "sigidx", "dma", "dma_sem", "dma_val")

    def __init__(self, eng, fn, dma):
        self.eng = eng
        self.fn = fn
        self.deps = []
        self.need_sig = False
        self.sigidx = 0
        self.dma = dma
        self.dma_sem = None
        self.dma_val = 0


class Gen:
    def __init__(self, nc):
        self.nc = nc
        self.ops = {e: [] for e in ENGS}
        self.lastw = {}
        self.readers = {}
        self.ndma = {e: 0 for e in ENGS}
        self.out_dmas = []

    def add(self, eng, fn, reads=(), writes=(), dma=False, is_output=False):
        op = Op(eng, fn, dma)
        deps = {}
        lastw = self.lastw
        readers = self.readers
        for k in reads:
            w = lastw.get(k)
            if w is not None:
                deps[id(w)] = w
            r = readers.get(k)
            if r is None:
                readers[k] = [op]
            else:
                r.append(op)
        for k in writes:
            w = lastw.get(k)
            if w is not None:
                deps[id(w)] = w
            r = readers.get(k)
            if r:
                for o in r:
                    if o is not op:
                        deps[id(o)] = o
            lastw[k] = op
            readers[k] = []
        if dma:
            j = self.ndma[eng]
            self.ndma[eng] = j + 1
            op.dma_sem = j % N_DMA_SEMS
            op.dma_val = 16 * (j // N_DMA_SEMS + 1)
            if is_output:
                self.out_dmas.append(op)
        for d in deps.values():
            if d.eng == eng and eng == "pe" and not d.dma:
                continue
            op.deps.append(d)
            if not d.dma:
                d.need_sig = True
        self.ops[eng].append(op)
        return op

    def emit(self, stack):
        nc = self.nc
        csem = {e: stack.enter_context(nc.semaphore("c_" + e)) for e in ENGS}
        dsem = {e: [stack.enter_context(nc.semaphore("d_%s_%d" % (e, i))) for i in range(N_DMA_SEMS)]
                for e in ENGS if self.ndma[e] > 0}
        for e in ENGS:
            n = 0
            for op in self.ops[e]:
                if op.need_sig and not op.dma:
                    n += 1
                    op.sigidx = n
        block = stack.enter_context(nc.Block())
        engobj = {"pe": "tensor", "act": "scalar", "dve": "vector", "pool": "gpsimd", "sp": "sync"}

        def make(e):
            def body(eng):
                waited = {}

                def wait(sem, key, val):
                    if waited.get(key, 0) >= val:
                        return
                    waited[key] = val
                    eng.wait_ge(sem, val)

                for op in self.ops[e]:
                    for d in op.deps:
                        if d.dma:
                            wait(dsem[d.eng][d.dma_sem], ("d", d.eng, d.dma_sem), d.dma_val)
                        else:
                            wait(csem[d.eng], ("c", d.eng), d.sigidx)
                    if op.dma:
                        if op.dma_val > 16:
                            wait(dsem[e][op.dma_sem], ("d", e, op.dma_sem), op.dma_val - 16)
                        ins = op.fn(eng)
                        ins.then_inc(dsem[e][op.dma_sem], 16)
                    else:
                        ins = op.fn(eng)
                        if op.need_sig:
                            ins.then_inc(csem[e], 1)
                for op in self.out_dmas:
                    if op.eng == e:
                        wait(dsem[e][op.dma_sem], ("d", e, op.dma_sem), op.dma_val)
            return body

        for e in ENGS:
            if self.ops[e]:
                getattr(block, engobj[e])(make(e))


class Ring:
    def __init__(self, name, views):
        self.name = name
        self.views = views
        self.i = 0

    def next(self):
        j = self.i % len(self.views)
        self.i += 1
        return self.views[j], (self.name, j)


def seg(name, a, c0, w):
    return [(name, a, s) for s in range(c0 // 256, (c0 + w + 255) // 256)]


class Prog:
    def __init__(self, nc, st, dram):
        self.nc = nc
        self.st = st
        self.dr = dram
        self.g = Gen(nc)
        sb = lambda n, s, d: st.enter_context(nc.sbuf_tensor("s_" + n, s, d))
        self.hT = sb("hT", [128, DC, BW], BF16)
        self.big = sb("big", [128, 48 * BW], BF16)
        self.wr = sb("wr", [128, NSLOT, 2048], BF16)
        self.wring = Ring("wr", [self.wr[:, i, :] for i in range(NSLOT)])
        xst = sb("xst", [128, 4, 544], F32)
        self.xst = Ring("xst", [xst[:, i, :] for i in range(4)])
        self.zin = self.xst
        yst = sb("yst", [128, 4, 512], F32)
        self.yst = Ring("yst", [yst[:, i, :] for i in range(4)])
        sq = sb("sq", [128, 3, 512], BF16)
        self.sq = Ring("sq", [sq[:, i, :] for i in range(3)])
        zbuf = sb("zbuf", [128, 2, 544], BF16)
        self.zb = Ring("zb", [zbuf[:, i, :] for i in range(2)])
        sqy = sb("sqy", [128, 3, 512], BF16)
        self.sqy = Ring("sqy", [sqy[:, i, :] for i in range(3)])
        tmp = sb("tmp", [128, 2, 512], F32)
        self.tmp = Ring("tmp", [tmp[:, i, :] for i in range(2)])
        sa = sb("sa", [128, 2, 512], F32)
        self.sa = Ring("sa", [sa[:, i, :] for i in range(2)])
        self.rstd = sb("rstd", [128, 4, 512], F32)
        self.zero = sb("zero", [128, DC, 16], F32)
        self.sfb = sb("sfb", [128, 2, 1024], BF16)
        sbin = sb("sbin", [128, 2, 1024], BF16)
        self.sbin = Ring("sbin", [sbin[:, i, :] for i in range(2)])
        self.ropest = sb("ropest", [128, 2, 512], F32)
        onst = sb("onst", [128, 2, 512], BF16)
        self.onst = Ring("onst", [onst[:, i, :] for i in range(2)])
        ogst = sb("ogst", [128, 2, 512], BF16)
        self.ogst = Ring("ogst", [ogst[:, i, :] for i in range(2)])
        self.small = sb("small", [128, 64], F32)
        self.prm_w = 1536
        self.prm = sb("prm", [128, self.prm_w], F32)
        self.ug = sb("ug", [128, 6, BW], BF16)
        vtm = sb("vtm", [128, 2, 512], BF16)
        self.vtm = Ring("vtm", [vtm[:, i, :] for i in range(2)])
        self.prmb = sb("prmb", [128, 1152], BF16)
        self.ones = sb("ones", [128, 128], BF16)
        self.mod = sb("mod", [128, 144, 2], F32)
        self.vecA = sb("vecA", [128, 2, DC, 2], F32)
        self.vecB = sb("vecB", [128, 2, DC, 2], F32)
        self.vecG = sb("vecG", [128, 2, DC, 2], F32)
        self.par = 0
        self.rstdE = sb("rstdE", [128, 2, 512], F32)
        self.bg = []
        self.normg = sb("normg", [128, DEPTH * 6, DC], F32)
        self.adab = sb("adab", [128, 144], F32)
        self.cT = sb("cT", [128, DC, 2], F32)
        self.scT = sb("scT", [128, DC, 2], BF16)
        self.epsc = sb("epsc", [128, 1], F32)
        self.ps = [st.enter_context(nc.psum_tensor("ps%d" % i, [128, 512], F32)) for i in range(8)]
        self.psring = Ring("ps", [self.ps[i] for i in range(5)])
        self.psring4 = Ring("ps", [self.ps[i] for i in range(4)])
        g = self.g
        nc_ = nc
        g.add("dve", lambda e: nc_.vector.memset(self.ones[:], 1.0), writes=["ones"])
        g.add("dve", lambda e: nc_.vector.memset(self.epsc[:], EPS), writes=["epsc"])
        g.add("dve", lambda e: nc_.vector.memset(self.zero[:], 0.0), writes=["zero"])
        g.add("sp", lambda e: e.dma_start(out=self.normg[:], in_=self.dr["normg"]), writes=["normg"], dma=True)

    def bg_step(self, n=1):
        for _ in range(n):
            if not self.bg:
                return
            try:
                next(self.bg[0][1])
            except StopIteration:
                self.bg.pop(0)

    def bg_flush(self, tag=None):
        keep = []
        for t, gen in self.bg:
            if tag is None or t == tag:
                for _ in gen:
                    pass
            else:
                keep.append((t, gen))
        self.bg = keep

    def wload(self, src_ap):
        self.bg_step(2)
        view, key = self.wring.next()
        self.g.add("pool", lambda e: e.dma_start(out=view, in_=src_ap), writes=[key], dma=True)
        return view, key

    def rstd_from(self, bank, bkey, slot, w, nfeat):
        nc, g = self.nc, self.g
        r = self.rstd[:, slot, :w]
        g.add("act", lambda e: nc.scalar.activation(out=r, in_=bank[:, :w], func=AF.Sqrt,
                                                    bias=self.epsc[:, 0:1], scale=1.0 / nfeat),
              reads=["epsc"], writes=[("rstd", slot), bkey])
        g.add("dve", lambda e: nc.vector.reciprocal(out=r, in_=r), reads=[("rstd", slot)], writes=[("rstd", slot)])
        return r

    def ada_stage(self, L):
        nc, g = self.nc, self.g
        if L == 0:
            g.add("sp", lambda e: e.dma_start(out=self.cT[:], in_=self.dr["cT"]), writes=["cT"], dma=True)
            g.add("act", lambda e: nc.scalar.activation(out=self.scT[:], in_=self.cT[:], func=AF.Silu),
                  reads=["cT"], writes=["scT"])
        g.add("sp", lambda e: e.dma_start(out=self.adab[:], in_=self.dr["adab"][L]), writes=["adab"], dma=True)
        bank, bkey = self.ps[7], ("ps", 7)
        for n in range(144):
            wv, wk = self.wload(self.dr["adaw"][L, n])
            for kc in range(DC):
                g.add("pe", lambda e, n=n, kc=kc, wv=wv: nc.tensor.matmul(
                    bank[:, 2 * n:2 * n + 2], lhsT=wv[:, kc * 128:(kc + 1) * 128], rhs=self.scT[:, kc, :],
                    start=(kc == 0), stop=(kc == DC - 1)),
                    reads=[wk, "scT"], writes=[bkey])
        pv = bank[:, 0:288].rearrange("p (n s) -> p n s", s=2)
        for s in range(2):
            g.add("dve", lambda e, s=s: nc.vector.tensor_tensor(out=self.mod[:, :, s], in0=pv[:, :, s],
                                                                in1=self.adab[:], op=ALU.add),
                  reads=["adab"], writes=["mod", bkey])

    def mod_vectors(self, L, s, weight):
        nc, g = self.nc, self.g
        self.par ^= 1
        pr = self.par
        gpre = self.normg[:, L * 6 + s * 2 + 0, :]
        gpost = self.normg[:, L * 6 + s * 2 + 1, :]
        for j in range(2):
            sh = self.mod[:, (3 * s) * DC:(3 * s + 1) * DC, j]
            sc = self.mod[:, (3 * s + 1) * DC:(3 * s + 2) * DC, j]
            gt = self.mod[:, (3 * s + 2) * DC:(3 * s + 3) * DC, j]
            g.add("dve", lambda e, j=j, sc=sc: nc.vector.scalar_tensor_tensor(
                out=self.vecA[:, pr, :, j], in0=sc, scalar=1.0, in1=gpre, op0=ALU.add, op1=ALU.mult),
                reads=["mod", "normg"], writes=[("vecA", pr)])
            g.add("dve", lambda e, j=j, sh=sh: nc.vector.tensor_copy(out=self.vecB[:, pr, :, j], in_=sh),
                  reads=["mod"], writes=[("vecB", pr)])
            g.add("dve", lambda e, j=j, gt=gt: nc.vector.scalar_tensor_tensor(
                out=self.vecG[:, pr, :, j], in0=gt, scalar=float(weight), in1=gpost, op0=ALU.mult, op1=ALU.mult),
                reads=["mod", "normg"], writes=[("vecG", pr)])

    def prologue(self, b0, tiles):
        for _ in self.prologue_gen(b0, tiles):
            pass

    def prologue_gen(self, b0, tiles):
        nc, g = self.nc, self.g
        xs = self.dr["xs"]
        pr = self.par
        self.bg_flush(tag=b0)
        sbanks = (4, 7)
        for ti, (c0, w, is_ctx) in enumerate(tiles):
            j = 1 if is_ctx else 0
            l0 = c0 - b0
            bank, bkey = self.ps[sbanks[ti]], ("ps", sbanks[ti])
            for kc in range(DC):
                xv, xk = self.xst.next()
                g.add("sp", lambda e, xv=xv, kc=kc, c0=c0, w=w: e.dma_start(out=xv[:, :w], in_=xs[kc, :, c0:c0 + w]),
                      reads=seg("xs", kc, c0, w), writes=[xk], dma=True)
                qv, qk = self.sq.next()
                g.add("act", lambda e, xv=xv, qv=qv, w=w: nc.scalar.activation(out=qv[:, :w], in_=xv[:, :w], func=AF.Square),
                      reads=[xk], writes=[qk])
                g.add("pe", lambda e, qv=qv, kc=kc, w=w, bank=bank: nc.tensor.matmul(
                    bank[:, :w], lhsT=self.ones[:], rhs=qv[:, :w], start=(kc == 0), stop=(kc == DC - 1)),
                    reads=[qk, "ones"], writes=[bkey])
                yield
            r = self.rstd_from(bank, bkey, 2 + ti, w, D)
            for kc in range(DC):
                xv, xk = self.xst.next()
                g.add("sp", lambda e, xv=xv, kc=kc, c0=c0, w=w: e.dma_start(out=xv[:, :w], in_=xs[kc, :, c0:c0 + w]),
                      reads=seg("xs", kc, c0, w), writes=[xk], dma=True)
                tv, tk = self.tmp.next()
                g.add("dve", lambda e, xv=xv, tv=tv, kc=kc, w=w, r=r, j=j: nc.vector.scalar_tensor_tensor(
                    out=tv[:, :w], in0=xv[:, :w], scalar=self.vecA[:, pr, kc, j:j + 1], in1=r, op0=ALU.mult, op1=ALU.mult),
                    reads=[xk, ("vecA", pr), ("rstd", 2 + ti)], writes=[tk])
                g.add("act", lambda e, tv=tv, kc=kc, w=w, l0=l0, j=j: nc.scalar.activation(
                    out=self.hT[:, kc, l0:l0 + w], in_=tv[:, :w], func=AF.Identity,
                    bias=self.vecB[:, pr, kc, j:j + 1], scale=1.0),
                    reads=[tk, ("vecB", pr)], writes=seg("hT", kc, l0, w))
                yield

    def epilogue(self, b0, tiles):
        self.bg_flush()
        gen = self.epilogue_gen(b0, tiles)
        next(gen)
        self.bg.append((b0, gen))

    def epilogue_gen(self, b0, tiles):
        nc, g = self.nc, self.g
        xs, ys = self.dr["xs"], self.dr["ys"]
        pr = self.par
        rs = []
        for ti, (c0, w, is_ctx) in enumerate(tiles):
            bank, bkey = self.ps[5 + ti], ("ps", 5 + ti)
            r = self.rstdE[:, ti, :w]
            g.add("act", lambda e, r=r, bank=bank, w=w: nc.scalar.activation(
                out=r, in_=bank[:, :w], func=AF.Sqrt, bias=self.epsc[:, 0:1], scale=1.0 / D),
                reads=["epsc"], writes=[("rstdE", ti), bkey])
            g.add("dve", lambda e, r=r: nc.vector.reciprocal(out=r, in_=r), writes=[("rstdE", ti)])
            rs.append(r)
        yield
        for ti, (c0, w, is_ctx) in enumerate(tiles):
            j = 1 if is_ctx else 0
            r = rs[ti]
            for d in range(DC):
                yv, yk = self.yst.next()
                g.add("sp", lambda e, yv=yv, d=d, c0=c0, w=w: e.dma_start(out=yv[:, :w], in_=ys[d, :, c0:c0 + w]),
                      reads=seg("ys", d, c0, w), writes=[yk], dma=True)
                xv, xk = self.xst.next()
                g.add("sp", lambda e, xv=xv, d=d, c0=c0, w=w: e.dma_start(out=xv[:, :w], in_=xs[d, :, c0:c0 + w]),
                      reads=seg("xs", d, c0, w), writes=[xk], dma=True)
                g.add("dve", lambda e, yv=yv, d=d, w=w, r=r, j=j: nc.vector.scalar_tensor_tensor(
                    out=yv[:, :w], in0=yv[:, :w], scalar=self.vecG[:, pr, d, j:j + 1], in1=r, op0=ALU.mult, op1=ALU.mult),
                    reads=[yk, ("vecG", pr), ("rstdE", ti)], writes=[yk])
                g.add("dve", lambda e, yv=yv, xv=xv, w=w: nc.vector.tensor_tensor(
                    out=xv[:, :w], in0=xv[:, :w], in1=yv[:, :w], op=ALU.add),
                    reads=[yk, xk], writes=[xk])
                g.add("sp", lambda e, xv=xv, d=d, c0=c0, w=w: e.dma_start(out=xs[d, :, c0:c0 + w], in_=xv[:, :w]),
                      reads=[xk], writes=seg("xs", d, c0, w), dma=True)
                yield

    def y_evac(self, bank, bkey, d, ti, c0, w, first, last, pending):
        nc, g = self.nc, self.g
        ys = self.dr["ys"]
        yv, yk = self.yst.next()
        g.add("dve", lambda e: nc.vector.tensor_copy(out=yv[:, :w], in_=bank[:, :w]), writes=[yk, bkey])
        qv, qk = self.sqy.next()
        g.add("act", lambda e: nc.scalar.activation(out=qv[:, :w], in_=yv[:, :w], func=AF.Square),
              reads=[yk], writes=[qk])
        g.add("sp", lambda e: e.dma_start(out=ys[d, :, c0:c0 + w], in_=yv[:, :w]),
              reads=[yk], writes=seg("ys", d, c0, w), dma=True)
        sbank, skey = self.ps[5 + ti], ("ps", 5 + ti)
        pending.append(lambda: g.add("pe", lambda e: nc.tensor.matmul(
            sbank[:, :w], lhsT=self.ones[:], rhs=qv[:, :w], start=first, stop=last),
            reads=[qk, "ones"], writes=[skey]))

    def out_proj(self, wname, widx, KC, b0, tiles, inkey, inT=None, psring=None, bg=False):
        nc, g = self.nc, self.g
        if inT is None:
            inT = self.big[:, :KC * BW].rearrange("p (k t) -> p k t", t=BW)
        psring = psring or self.psring
        wsrc = self.dr[wname]
        npieces = (KC + 15) // 16
        pending = []
        for d in range(DC):
            slots = []
            for pc in range(npieces):
                k0, k1 = pc * 16, min(KC, pc * 16 + 16)
                src = wsrc[tuple(widx) + (d,)][:, k0 * 128:k1 * 128]
                self.bg_step(2)
                view, key = self.wring.next()
                vv = view[:, :(k1 - k0) * 128]
                g.add("pool", lambda e, vv=vv, src=src: e.dma_start(out=vv, in_=src), writes=[key], dma=True)
                slots.append((view, key))
            for ti, (c0, w, is_ctx) in enumerate(tiles):
                l0 = c0 - b0
                bank, bkey = psring.next()
                for kc in range(KC):
                    view, key = slots[kc // 16]
                    kl = kc % 16
                    g.add("pe", lambda e, view=view, kl=kl, kc=kc, l0=l0, w=w, bank=bank: nc.tensor.matmul(
                        bank[:, :w], lhsT=view[:, kl * 128:(kl + 1) * 128], rhs=inT[:, kc, l0:l0 + w],
                        start=(kc == 0), stop=(kc == KC - 1)),
                        reads=[key] + seg(inkey, kc, l0, w), writes=[bkey])
                while pending:
                    pending.pop(0)()
                self.y_evac(bank, bkey, d, ti, c0, w, d == 0, d == DC - 1, pending)
        while pending:
            pending.pop(0)()

    def ffn_stage(self, L, which, s, do_ctx):
        nc, g = self.nc, self.g
        self.mod_vectors(L, s, 0.5)
        hid = self.big[:, :FC * BW].rearrange("p (k t) -> p k t", t=BW)
        blocks = [(b0, [t for t in tl if do_ctx or not t[2]]) for b0, tl in BLOCKS]
        self.prologue(*blocks[0])
        for bi, (b0, tiles) in enumerate(blocks):
            self.bg_flush(tag=("P", b0))
            for fc in range(FC):
                wa, ka = self.wload(self.dr["ffn_in"][L, which, fc, 0])
                wb, kb = self.wload(self.dr["ffn_in"][L, which, fc, 1])
                for ti, (c0, w, is_ctx) in enumerate(tiles):
                    l0 = c0 - b0
                    ba, bak = self.psring4.next()
                    bb, bbk = self.psring4.next()
                    for (wv, wk, bank, bk) in ((wa, ka, ba, bak), (wb, kb, bb, bbk)):
                        for kc in range(DC):
                            g.add("pe", lambda e, wv=wv, kc=kc, l0=l0, w=w, bank=bank: nc.tensor.matmul(
                                bank[:, :w], lhsT=wv[:, kc * 128:(kc + 1) * 128], rhs=self.hT[:, kc, l0:l0 + w],
                                start=(kc == 0), stop=(kc == DC - 1)),
                                reads=[wk] + seg("hT", kc, l0, w), writes=[bk])
                    sv, sk = self.sa.next()
                    g.add("act", lambda e, sv=sv, ba=ba, w=w: nc.scalar.activation(out=sv[:, :w], in_=ba[:, :w], func=AF.Silu),
                          writes=[sk, bak])
                    g.add("dve", lambda e, sv=sv, bb=bb, fc=fc, l0=l0, w=w: nc.vector.tensor_tensor(
                        out=hid[:, fc, l0:l0 + w], in0=sv[:, :w], in1=bb[:, :w], op=ALU.mult),
                        reads=[sk], writes=seg("big", fc, l0, w) + [bbk])
            self.bg_flush()
            if bi + 1 < len(blocks):
                nb0, ntiles = blocks[bi + 1]
                self.bg.append((("P", nb0), self.prologue_gen(nb0, ntiles)))
            self.out_proj("ffn_out", (L, which), FC, b0, tiles, "big", psring=self.psring4, bg=True)
            self.bg_flush(tag=("P", blocks[bi + 1][0]) if bi + 1 < len(blocks) else "none")
            self.epilogue(b0, tiles)

    def ln_stats_add(self, src, skey, ti, w, first, last, pending, bf_dst=None, bf_key=None):
        nc, g = self.nc, self.g
        if bf_dst is None:
            bv, bk = self.sq.next()
            bk = [bk]
        else:
            bv, bk = bf_dst, bf_key
        g.add("dve", lambda e: nc.vector.tensor_copy(out=bv[:, :w], in_=src), reads=skey, writes=bk)
        qv, qk = self.sq.next()
        g.add("act", lambda e: nc.scalar.activation(out=qv[:, :w], in_=src, func=AF.Square), reads=skey, writes=[qk])
        b1, k1 = self.ps[4 + 2 * ti], ("ps", 4 + 2 * ti)
        b2, k2 = self.ps[5 + 2 * ti], ("ps", 5 + 2 * ti)

        def emit():
            g.add("pe", lambda e: nc.tensor.matmul(b1[:, :w], lhsT=self.ones[:], rhs=bv[:, :w], start=first, stop=last),
                  reads=bk + ["ones"], writes=[k1])
            g.add("pe", lambda e: nc.tensor.matmul(b2[:, :w], lhsT=self.ones[:], rhs=qv[:, :w], start=first, stop=last),
                  reads=[qk, "ones"], writes=[k2])
        pending.append(emit)

    def ln_finish(self, ti, w, nfeat):
        nc, g = self.nc, self.g
        b1, k1 = self.ps[4 + 2 * ti], ("ps", 4 + 2 * ti)
        b2, k2 = self.ps[5 + 2 * ti], ("ps", 5 + 2 * ti)
        mean = self.rstd[:, 2 * ti, :w]
        rs = self.rstd[:, 2 * ti + 1, :w]
        mk, rk = ("rstd", 2 * ti), ("rstd", 2 * ti + 1)
        g.add("act", lambda e: nc.scalar.activation(out=mean, in_=b1[:, :w], func=AF.Copy, scale=1.0 / nfeat),
              writes=[mk, k1])
        g.add("dve", lambda e: nc.vector.tensor_tensor(out=rs, in0=mean, in1=mean, op=ALU.mult), reads=[mk], writes=[rk])
        g.add("dve", lambda e: nc.vector.scalar_tensor_tensor(out=rs, in0=b2[:, :w], scalar=1.0 / nfeat, in1=rs,
                                                              op0=ALU.mult, op1=ALU.subtract),
              writes=[rk, k2])
        g.add("act", lambda e: nc.scalar.activation(out=rs, in_=rs, func=AF.Sqrt, bias=self.epsc[:, 0:1], scale=1.0),
              reads=["epsc"], writes=[rk])
        g.add("dve", lambda e: nc.vector.reciprocal(out=rs, in_=rs), writes=[rk])
        return mean, rs, mk, rk

    def load_prm(self, src, off, n, bf=False):
        dst = (self.prmb if bf else self.prm)[:, off:off + n]
        if bf:
            self.g.add("pool", lambda e: e.dma_start(out=dst, in_=src), writes=["prmb"], dma=True)
        else:
            self.g.add("sp", lambda e: e.dma_start(out=dst, in_=src), writes=["prm"], dma=True)
        return dst

    def conv_stage(self, L):
        nc, g = self.nc, self.g
        dr = self.dr
        self.mod_vectors(L, 1, 1.0)
        wdw = self.load_prm(dr["conv_wdw"], 0, CONV_K * DC).rearrange("p (k c) -> p k c", c=DC)
        bdw = self.load_prm(dr["conv_vec"][0], 512, DC)
        lng = self.load_prm(dr["conv_vec"][1], 528, DC)
        lnb = self.load_prm(dr["conv_vec"][2], 544, DC)
        zs = dr["zs"]
        zsv = zs.rearrange("c p t -> p c t")
        for a in (0, 271, 286, 2349):
            g.add("sp", lambda e, a=a: e.dma_start(out=zsv[:, :, a:a + 15], in_=self.zero[:, :, 0:15]),
                  reads=["zero"], writes=["zpad"], dma=True)

        def zcol(c0, is_ctx):
            return 15 + c0 if is_ctx else 301 + (c0 - CTX)

        for b0, tiles in BLOCKS3:
            self.prologue(b0, tiles)
            for fc in range(DC):
                wa, ka = self.wload(dr["conv_pw1"][fc, 0])
                wb, kb = self.wload(dr["conv_pw1"][fc, 1])
                for ti, (c0, w, is_ctx) in enumerate(tiles):
                    l0 = c0 - b0
                    ba, bak = self.psring.next()
                    bb, bbk = self.psring.next()
                    for (wv, wk, bank, bk) in ((wa, ka, ba, bak), (wb, kb, bb, bbk)):
                        for kc in range(DC):
                            g.add("pe", lambda e, wv=wv, kc=kc, l0=l0, w=w, bank=bank: nc.tensor.matmul(
                                bank[:, :w], lhsT=wv[:, kc * 128:(kc + 1) * 128], rhs=self.hT[:, kc, l0:l0 + w],
                                start=(kc == 0), stop=(kc == DC - 1)),
                                reads=[wk] + seg("hT", kc, l0, w), writes=[bk])
                    sv, sk = self.sa.next()
                    g.add("act", lambda e, sv=sv, bb=bb, w=w: nc.scalar.activation(out=sv[:, :w], in_=bb[:, :w], func=AF.Sigmoid),
                          writes=[sk, bbk])
                    yv, yk = self.yst.next()
                    g.add("dve", lambda e, sv=sv, ba=ba, yv=yv, w=w: nc.vector.tensor_tensor(
                        out=yv[:, :w], in0=sv[:, :w], in1=ba[:, :w], op=ALU.mult), reads=[sk], writes=[yk, bak])
                    z0 = zcol(c0, is_ctx)
                    g.add("sp", lambda e, yv=yv, fc=fc, z0=z0, w=w: e.dma_start(out=zs[fc, :, z0:z0 + w], in_=yv[:, :w]),
                          reads=[yk], writes=[("zs", fc, c0)], dma=True)
        zc = self.big[:].bitcast(F32)[:, :DC * 768].rearrange("p (c t) -> p c t", t=768)
        ident = self.load_prm(dr["ident"], 0, 128, bf=True)
        dg = self.ug[:].rearrange("p a b -> p (a b)")[:, :CONV_K * 128].rearrange("p (k n) -> p k n", n=128)
        PBW = 1152
        for b0, tiles in BLOCKS3:
            pending = []
            for fc in range(DC):
                idb = bass.AP(self.prmb, 0, [[PBW, 128], [0, CONV_K], [1, 128]])
                wdb = bass.AP(self.prm, fc, [[self.prm_w, 128], [DC, CONV_K], [0, 128]])
                g.add("dve", lambda e, idb=idb, wdb=wdb: nc.vector.tensor_tensor(out=dg, in0=idb, in1=wdb, op=ALU.mult),
                      reads=["prm", "prmb"], writes=["dg"])
                for ti, (c0, w, is_ctx) in enumerate(tiles):
                    l0 = c0 - b0
                    z0 = zcol(c0, is_ctx)
                    zv, zk = self.zin.next()
                    allz = ["zpad"] + [("zs", fc, t[0]) for t in TILES3]
                    g.add("sp", lambda e, zv=zv, fc=fc, z0=z0, w=w: e.dma_start(out=zv[:, :w + 30], in_=zs[fc, :, z0 - 15:z0 + w + 15]),
                          reads=allz, writes=[zk], dma=True)
                    zb, zbk = self.zb.next()
                    g.add("act", lambda e, zv=zv, zb=zb, w=w: nc.scalar.copy(out=zb[:, :w + 30], in_=zv[:, :w + 30]),
                          reads=[zk], writes=[zbk])
                    bank, bk = self.psring4.next()
                    for k in range(CONV_K):
                        g.add("pe", lambda e, bank=bank, zb=zb, k=k, w=w: nc.tensor.matmul(
                            bank[:, :w], lhsT=dg[:, k, :], rhs=zb[:, k:k + w], start=(k == 0), stop=(k == CONV_K - 1)),
                            reads=["dg", zbk], writes=[bk])
                    acc = zc[:, fc, l0:l0 + w]
                    ak = seg("big", fc, l0, w)
                    g.add("act", lambda e, bank=bank, acc=acc, fc=fc, w=w: nc.scalar.activation(
                        out=acc, in_=bank[:, :w], func=AF.Identity, bias=bdw[:, fc:fc + 1], scale=1.0),
                        reads=["prm"], writes=ak + [bk])
                    while pending:
                        pending.pop(0)()
                    self.ln_stats_add(acc, ak, ti, w, fc == 0, fc == DC - 1, pending)
            while pending:
                pending.pop(0)()
            for ti, (c0, w, is_ctx) in enumerate(tiles):
                l0 = c0 - b0
                mean, rs, mk, rk = self.ln_finish(ti, w, D)
                for fc in range(DC):
                    acc = zc[:, fc, l0:l0 + w]
                    ak = seg("big", fc, l0, w)
                    tv, tk = self.tmp.next()
                    g.add("dve", lambda e, tv=tv, acc=acc, mean=mean, w=w: nc.vector.tensor_tensor(
                        out=tv[:, :w], in0=acc, in1=mean, op=ALU.subtract), reads=ak + [mk], writes=[tk])
                    g.add("dve", lambda e, tv=tv, fc=fc, rs=rs, w=w: nc.vector.scalar_tensor_tensor(
                        out=tv[:, :w], in0=tv[:, :w], scalar=lng[:, fc:fc + 1], in1=rs, op0=ALU.mult, op1=ALU.mult),
                        reads=[rk, "prm"], writes=[tk])
                    g.add("act", lambda e, tv=tv, fc=fc, l0=l0, w=w: nc.scalar.activation(
                        out=self.hT[:, fc, l0:l0 + w], in_=tv[:, :w], func=AF.Silu, bias=lnb[:, fc:fc + 1], scale=1.0),
                        reads=[tk, "prm"], writes=seg("hT", fc, l0, w))
            self.out_proj("conv_pw2", (), DC, b0, tiles, "hT", inT=self.hT, psring=self.psring4)
            self.epilogue(b0, tiles)

    def gmlp_stage(self, L):
        nc, g = self.nc, self.g
        dr = self.dr
        self.mod_vectors(L, 1, 1.0)
        NV = GE // 128
        lng = self.load_prm(dr["gmlp_vec"][0], 0, NV)
        lnb = self.load_prm(dr["gmlp_vec"][1], NV, NV)
        self.load_prm(dr["gmlp_bsb"], 128, GG * 128)
        wsT = self.load_prm(dr["gmlp_wsT"], 0, GG * 128, bf=True).rearrange("p (g n) -> p g n", n=128)
        ident = self.load_prm(dr["ident"], GG * 128, 128, bf=True)
        vT = self.big[:, :NV * BW].rearrange("p (k t) -> p k t", t=BW)
        ug = self.ug
        for b0, tiles in BLOCKS3:
            self.prologue(b0, tiles)
            pending = []
            for vc in range(NV):
                wv, wk = self.wload(dr["gmlp_in"][1, vc])
                for ti, (c0, w, is_ctx) in enumerate(tiles):
                    l0 = c0 - b0
                    bank, bk = self.psring4.next()
                    for kc in range(DC):
                        g.add("pe", lambda e, wv=wv, kc=kc, l0=l0, w=w, bank=bank: nc.tensor.matmul(
                            bank[:, :w], lhsT=wv[:, kc * 128:(kc + 1) * 128], rhs=self.hT[:, kc, l0:l0 + w],
                            start=(kc == 0), stop=(kc == DC - 1)),
                            reads=[wk] + seg("hT", kc, l0, w), writes=[bk])
                    while pending:
                        pending.pop(0)()
                    sv, sk = self.sa.next()
                    g.add("act", lambda e, sv=sv, bank=bank, w=w: nc.scalar.activation(
                        out=sv[:, :w], in_=bank[:, :w], func=AF.Gelu_apprx_tanh), writes=[sk, bk])
                    self.ln_stats_add(sv[:, :w], [sk], ti, w, vc == 0, vc == NV - 1, pending,
                                      bf_dst=vT[:, vc, l0:l0 + w], bf_key=seg("big", vc, l0, w))
            while pending:
                pending.pop(0)()
            for ti, (c0, w, is_ctx) in enumerate(tiles):
                l0 = c0 - b0
                mean, rs, mk, rk = self.ln_finish(ti, w, GE)
                for vc in range(NV):
                    vv = vT[:, vc, l0:l0 + w]
                    vk = seg("big", vc, l0, w)
                    tv, tk = self.tmp.next()
                    g.add("dve", lambda e, tv=tv, vv=vv, mean=mean, w=w: nc.vector.tensor_tensor(
                        out=tv[:, :w], in0=vv, in1=mean, op=ALU.subtract), reads=vk + [mk], writes=[tk])
                    g.add("dve", lambda e, tv=tv, rs=rs, w=w: nc.vector.tensor_tensor(
                        out=tv[:, :w], in0=tv[:, :w], in1=rs, op=ALU.mult), reads=[rk], writes=[tk])
                    g.add("act", lambda e, tv=tv, vv=vv, vc=vc, w=w: nc.scalar.activation(
                        out=vv, in_=tv[:, :w], func=AF.Identity, bias=lnb[:, vc:vc + 1], scale=lng[:, vc:vc + 1]),
                        reads=[tk, "prm"], writes=vk)
            bw = sum(t[1] for t in tiles)
            nck = bw // 128
            for gi in range(GG):
                for j in range(6):
                    uc = gi * 6 + j
                    wv, wk = self.wload(dr["gmlp_in"][0, uc])
                    for ti, (c0, w, is_ctx) in enumerate(tiles):
                        l0 = c0 - b0
                        bank, bk = self.psring4.next()
                        for kc in range(DC):
                            g.add("pe", lambda e, wv=wv, kc=kc, l0=l0, w=w, bank=bank: nc.tensor.matmul(
                                bank[:, :w], lhsT=wv[:, kc * 128:(kc + 1) * 128], rhs=self.hT[:, kc, l0:l0 + w],
                                start=(kc == 0), stop=(kc == DC - 1)),
                                reads=[wk] + seg("hT", kc, l0, w), writes=[bk])
                        g.add("act", lambda e, j=j, l0=l0, bank=bank, w=w: nc.scalar.activation(
                            out=ug[:, j, l0:l0 + w], in_=bank[:, :w], func=AF.Gelu_apprx_tanh),
                            writes=seg("ug", j, l0, w) + [bk])
                for j in range(6):
                    vc = gi * 6 + j
                    for ck0 in range(0, nck, 4):
                        nb = min(4, nck - ck0)
                        cl0, cw = ck0 * 128, nb * 128
                        vk = seg("big", vc, cl0, cw)
                        pb, pk = self.psring4.next()
                        pbb = pb[:].bitcast(BF16)
                        for i in range(nb):
                            g.add("pe", lambda e, pbb=pbb, i=i, vc=vc, cl0=cl0: nc.tensor.transpose(
                                pbb[:, i * 128:(i + 1) * 128], vT[:, vc, cl0 + i * 128:cl0 + (i + 1) * 128], ident),
                                reads=vk + ["prmb"], writes=[pk])
                        mv, mkk = self.vtm.next()
                        g.add("act", lambda e, mv=mv, pbb=pbb, cw=cw: nc.scalar.copy(out=mv[:, :cw], in_=pbb[:, :cw]),
                              writes=[mkk, pk])
                        mb, mbk = self.psring4.next()
                        for i in range(nb):
                            g.add("pe", lambda e, mb=mb, mv=mv, i=i, gi=gi: nc.tensor.matmul(
                                mb[:, i * 128:(i + 1) * 128], lhsT=mv[:, i * 128:(i + 1) * 128], rhs=wsT[:, gi, :],
                                start=True, stop=True), reads=[mkk, "prmb"], writes=[mbk])
                        bsv = bass.AP(self.prm, 128 + gi * 128, [[self.prm_w, 128], [0, nb], [1, 128]])
                        tv, tk = self.tmp.next()
                        g.add("dve", lambda e, tv=tv, mb=mb, bsv=bsv, nb=nb, cw=cw: nc.vector.tensor_tensor(
                            out=tv[:, :cw].rearrange("p (b n) -> p b n", n=128),
                            in0=mb[:, :cw].rearrange("p (b n) -> p b n", n=128), in1=bsv, op=ALU.add),
                            reads=["prm"], writes=[tk, mbk])
                        g.add("dve", lambda e, tv=tv, j=j, vc=vc, cl0=cl0, cw=cw: nc.vector.tensor_tensor(
                            out=vT[:, vc, cl0:cl0 + cw], in0=tv[:, :cw], in1=ug[:, j, cl0:cl0 + cw], op=ALU.mult),
                            reads=[tk] + seg("ug", j, cl0, cw), writes=vk)
            self.out_proj("gmlp_out", (), NV, b0, tiles, "big", inT=vT, psring=self.psring4)
            self.epilogue(b0, tiles)

    def prologue_to_dram(self):
        g = self.g
        hs = self.dr["hs"]
        hsv = hs.rearrange("c p t -> p c t")
        for b0, tiles in BLOCKS3:
            self.prologue(b0, tiles)
            bw = sum(t[1] for t in tiles)
            rk = [k for kc in range(DC) for k in seg("hT", kc, 0, bw)]
            g.add("sp", lambda e, b0=b0, bw=bw: e.dma_start(out=hsv[:, :, b0:b0 + bw], in_=self.hT[:, :, :bw]),
                  reads=rk, writes=[("hs", b0)], dma=True)

    def ret_stage(self, L, r, ctx_out):
        nc, g = self.nc, self.g
        dr = self.dr
        NCH = T // 128
        self.mod_vectors(L, 1, 1.0)
        P = self.prm
        PW = self.prm_w
        self.load_prm(dr["ret_logit"][r], 0, 16)
        gng = self.load_prm(dr["ret_gng"][r], 16, 32)
        self.load_prm(dr["ret_const"], 64, 770)
        ident = self.load_prm(dr["ident"], 0, 128, bf=True)
        lg = P[:, 0:16]
        posd = [P[:, 64:192], P[:, 192:320]]
        msk = [P[:, 320:448], P[:, 448:576]]
        colc = [P[:, 576:704], P[:, 704:832]]
        pcol = [P[:, 832:833], P[:, 833:834]]
        DT = [P[:, 896:1024], P[:, 1024:1152]]
        QD = P[:, 1152:1408]
        sm = self.small
        g.add("act", lambda e: nc.scalar.activation(out=lg, in_=lg, func=AF.Exp, scale=-1.0), reads=["prm"], writes=["prm"])
        g.add("act", lambda e: nc.scalar.activation(out=lg, in_=lg, func=AF.Ln, bias=1.0, scale=1.0), reads=["prm"], writes=["prm"])
        g.add("dve", lambda e: nc.vector.tensor_single_scalar(out=lg, in_=lg, scalar=-1.0, op=ALU.mult),
              reads=["prm"], writes=["prm"])
        self.prologue_to_dram()

        big = self.big
        qT = big[:, 0:2 * T].rearrange("p (c t) -> p c t", t=T)
        kT = big[:, 2 * T:4 * T].rearrange("p (c t) -> p c t", t=T)
        sgT = big[:, 4 * T:8 * T].rearrange("p (c t) -> p c t", t=T)
        vtm = big[:, 8 * T:12 * T].rearrange("p (j e) -> p j e", e=512)
        ktm = [big[:, 12 * T:14 * T].rearrange("p (j d) -> p j d", d=256),
               big[:, 14 * T:16 * T].rearrange("p (j d) -> p j d", d=256)]
        ugf = self.ug[:].rearrange("p a b -> p (a b)").bitcast(F32)
        S = [ugf[:, 0:1024].rearrange("p (c e) -> p c e", e=512), ugf[:, 1024:2048].rearrange("p (c e) -> p c e", e=512)]
        hsv = dr["hs"].rearrange("c p t -> p c t")
        sbs = dr["sbs"]
        ogs = dr["ogs"]
        rope = dr["rope"]

        for h in range(RET_H):
            for di in range(2):
                col = di * 8 + h
                g.add("act", lambda e, di=di, col=col: nc.scalar.activation(out=DT[di], in_=posd[di], func=AF.Exp,
                                                                            scale=lg[:, col:col + 1]),
                      reads=["prm"], writes=[("DT", di)])
                g.add("dve", lambda e, di=di: nc.vector.tensor_tensor(out=DT[di], in0=DT[di], in1=msk[di], op=ALU.mult),
                      reads=["prm"], writes=[("DT", di)])
                g.add("act", lambda e, di=di, col=col: nc.scalar.activation(out=QD[:, di * 128:(di + 1) * 128], in_=colc[di],
                                                                            func=AF.Exp, scale=lg[:, col:col + 1]),
                      reads=["prm"], writes=[("QD", di)])
                g.add("act", lambda e, di=di, col=col: nc.scalar.activation(out=sm[:, di:di + 1], in_=pcol[di], func=AF.Exp,
                                                                            scale=lg[:, col:col + 1]),
                      reads=["prm"], writes=[("sm", di)])
                g.add("dve", lambda e, di=di: nc.vector.tensor_single_scalar(out=sm[:, di:di + 1], in_=sm[:, di:di + 1],
                                                                             scalar=1.0 / 16.0, op=ALU.mult),
                      writes=[("sm", di)])
                g.add("act", lambda e, di=di, col=col: nc.scalar.activation(out=sm[:, 2 + di:3 + di], in_=lg[:, col:col + 1],
                                                                            func=AF.Exp, scale=128.0),
                      reads=["prm"], writes=[("sm", 2 + di)])
            for b0, tiles in BLOCKS3:
                bw = sum(t[1] for t in tiles)
                g.add("sp", lambda e, b0=b0, bw=bw: e.dma_start(out=self.hT[:, :, :bw], in_=hsv[:, :, b0:b0 + bw]),
                      reads=[("hs", b0)], writes=[k for kc in range(DC) for k in seg("hT", kc, 0, bw)], dma=True)
                for qk in range(2):
                    dstT = qT if qk == 0 else kT
                    dname = "qT" if qk == 0 else "kT"
                    for dc in range(2):
                        w1, k1 = self.wload(dr["ret_in"][r, h, qk * 4 + dc])
                        w2, k2 = self.wload(dr["ret_in"][r, h, qk * 4 + 2 + dc])
                        for ti, (c0, w, is_ctx) in enumerate(tiles):
                            l0 = c0 - b0
                            b1, bk1 = self.psring.next()
                            b2, bk2 = self.psring.next()
                            for (wv, wk, bank, bk) in ((w1, k1, b1, bk1), (w2, k2, b2, bk2)):
                                for kc in range(DC):
                                    g.add("pe", lambda e, wv=wv, kc=kc, l0=l0, w=w, bank=bank: nc.tensor.matmul(
                                        bank[:, :w], lhsT=wv[:, kc * 128:(kc + 1) * 128], rhs=self.hT[:, kc, l0:l0 + w],
                                        start=(kc == 0), stop=(kc == DC - 1)),
                                        reads=[wk] + seg("hT", kc, l0, w), writes=[bk])
                            rk = ("rope", 0)
                            g.add("sp", lambda e, dc=dc, c0=c0, w=w: e.dma_start(
                                out=self.ropest[:, :, :w], in_=rope[:, :, dc, c0:c0 + w].rearrange("a p t -> p a t")),
                                writes=[rk], dma=True)
                            tv, tk = self.tmp.next()
                            g.add("dve", lambda e, tv=tv, b1=b1, dc=dc, w=w: nc.vector.tensor_tensor(
                                out=tv[:, :w], in0=b1[:, :w], in1=self.ropest[:, 0, :w], op=ALU.mult),
                                reads=[rk], writes=[tk, bk1])
                            sv, sk = self.sa.next()
                            g.add("dve", lambda e, sv=sv, b2=b2, dc=dc, w=w: nc.vector.tensor_tensor(
                                out=sv[:, :w], in0=b2[:, :w], in1=self.ropest[:, 1, :w], op=ALU.mult),
                                reads=[rk], writes=[sk, bk2])
                            g.add("dve", lambda e, tv=tv, sv=sv, dstT=dstT, dc=dc, c0=c0, w=w: nc.vector.tensor_tensor(
                                out=dstT[:, dc, c0:c0 + w], in0=tv[:, :w], in1=sv[:, :w], op=ALU.add),
                                reads=[tk, sk], writes=seg(dname, dc, c0, w))
                for ec in range(4):
                    wv, wk = self.wload(dr["ret_in"][r, h, 8 + ec])
                    for ti, (c0, w, is_ctx) in enumerate(tiles):
                        l0 = c0 - b0
                        bank, bk = self.psring.next()
                        for kc in range(DC):
                            g.add("pe", lambda e, wv=wv, kc=kc, l0=l0, w=w, bank=bank: nc.tensor.matmul(
                                bank[:, :w], lhsT=wv[:, kc * 128:(kc + 1) * 128], rhs=self.hT[:, kc, l0:l0 + w],
                                start=(kc == 0), stop=(kc == DC - 1)),
                                reads=[wk] + seg("hT", kc, l0, w), writes=[bk])
                        tv, tk = self.tmp.next()
                        g.add("act", lambda e, tv=tv, bank=bank, w=w: nc.scalar.activation(out=tv[:, :w], in_=bank[:, :w], func=AF.Silu),
                              writes=[tk, bk])
                        g.add("dve", lambda e, tv=tv, ec=ec, c0=c0, w=w, h=h: nc.vector.tensor_single_scalar(
                            out=sgT[:, ec, c0:c0 + w], in_=tv[:, :w], scalar=gng[:, h * 4 + ec:h * 4 + ec + 1],
                            op=ALU.mult), reads=[tk, "prm"], writes=seg("sgT", ec, c0, w))
                vw = [self.wload(dr["ret_v"][r, h, p4]) for p4 in range(4)]
                for j in range(b0 // 128, (b0 + bw) // 128):
                    l0 = j * 128 - b0
                    bank, bk = self.psring.next()
                    for kc in range(DC):
                        wv, wk = vw[kc // 4]
                        kl = kc % 4
                        g.add("pe", lambda e, wv=wv, kl=kl, kc=kc, l0=l0, bank=bank: nc.tensor.matmul(
                            bank[:, :], lhsT=self.hT[:, kc, l0:l0 + 128], rhs=wv[:, kl * 512:(kl + 1) * 512],
                            start=(kc == 0), stop=(kc == DC - 1)),
                            reads=[wk] + seg("hT", kc, l0, 128), writes=[bk])
                    g.add("act", lambda e, j=j, bank=bank: nc.scalar.copy(out=vtm[:, j, :], in_=bank[:, :]),
                          writes=[("vtok", j), bk])
            for j0 in range(0, NCH, 2):
                pb, pk = self.psring.next()
                pbb = pb[:].bitcast(BF16)
                for jj in range(2):
                    for dc in range(2):
                        j = j0 + jj
                        g.add("pe", lambda e, pbb=pbb, jj=jj, dc=dc, j=j: nc.tensor.transpose(
                            pbb[:, (jj * 2 + dc) * 128:(jj * 2 + dc + 1) * 128], kT[:, dc, j * 128:(j + 1) * 128], ident),
                            reads=seg("kT", dc, j * 128, 128) + ["prmb"], writes=[pk])
                for di in range(2):
                    g.add("dve", lambda e, di=di, pbb=pbb, j0=j0: nc.vector.tensor_single_scalar(
                        out=ktm[di][:, j0:j0 + 2, :], in_=pbb[:, 0:512].rearrange("p (j d) -> p j d", d=256),
                        scalar=sm[:, di:di + 1], op=ALU.mult),
                        reads=[("sm", di)], writes=[("ktm", di, j0), ("ktm", di, j0 + 1), pk])

            def kv_update(di, j):
                for dc in range(2):
                    bank, bk = self.psring.next()
                    g.add("pe", lambda e, bank=bank, dc=dc: nc.tensor.matmul(
                        bank[:, :], lhsT=ktm[di][:, j, dc * 128:(dc + 1) * 128], rhs=vtm[:, j, :], start=True, stop=True),
                        reads=[("ktm", di, j), ("vtok", j)], writes=[bk])
                    g.add("dve", lambda e, bank=bank, dc=dc: nc.vector.scalar_tensor_tensor(
                        out=S[di][:, dc, :], in0=S[di][:, dc, :], scalar=sm[:, 2 + di:3 + di], in1=bank[:, :],
                        op0=ALU.mult, op1=ALU.add),
                        reads=[("sm", 2 + di)], writes=[("S", di), bk])

            def zero_state(di):
                g.add("dve", lambda e: nc.vector.memset(S[di], 0.0), writes=[("S", di)])

            def bwd_store(j):
                sv, sk = self.sbin.next()
                g.add("act", lambda e, sv=sv: nc.scalar.copy(out=sv[:, :].rearrange("p (c e) -> p c e", e=512), in_=S[1]),
                      reads=[("S", 1)], writes=[sk])
                g.add("sp", lambda e, sv=sv, j=j: e.dma_start(out=sbs[j], in_=sv[:, :]), reads=[sk], writes=[("sbs", j)], dma=True)

            zero_state(1)
            for j in (1, 0):
                bwd_store(j)
                kv_update(1, j)
            for j in range(NCH - 1, 1, -1):
                bwd_store(j)
                kv_update(1, j)

            zero_state(0)
            for j in range(NCH):
                c0 = j * 128
                fb = self.sfb[:, j % 2, :].rearrange("p (c e) -> p c e", e=512)
                fk = ("sfb", j % 2)
                g.add("act", lambda e, fb=fb: nc.scalar.copy(out=fb, in_=S[0]), reads=[("S", 0)], writes=[fk])
                bv, bkk = self.sbin.next()
                g.add("sp", lambda e, bv=bv, j=j: e.dma_start(out=bv[:, :], in_=sbs[j]), reads=[("sbs", j)], writes=[bkk], dma=True)
                bvv = bv[:, :].rearrange("p (c e) -> p c e", e=512)
                pb, pk = self.psring.next()
                for dc in range(2):
                    g.add("pe", lambda e, pb=pb, dc=dc, c0=c0: nc.tensor.matmul(
                        pb[:, 0:128], lhsT=kT[:, dc, c0:c0 + 128], rhs=qT[:, dc, c0:c0 + 128], start=(dc == 0), stop=(dc == 1)),
                        reads=seg("kT", dc, c0, 128) + seg("qT", dc, c0, 128), writes=[pk])
                av, ak = self.vtm.next()
                for di in range(2):
                    g.add("dve", lambda e, av=av, pb=pb, di=di: nc.vector.tensor_tensor(
                        out=av[:, di * 128:(di + 1) * 128], in0=pb[:, 0:128], in1=DT[di], op=ALU.mult),
                        reads=[("DT", di)], writes=[ak, pk])
                qv, qk_ = self.sq.next()
                qdst = qv[:, :].rearrange("p (a c n) -> p a c n", a=2, c=2)
                for di in range(2):
                    qdv = bass.AP(self.prm, 1152 + di * 128, [[PW, 128], [0, 2], [1, 128]])
                    g.add("dve", lambda e, qdst=qdst, qdv=qdv, c0=c0, di=di: nc.vector.tensor_tensor(
                        out=qdst[:, di, :, :], in0=qT[:, :, c0:c0 + 128], in1=qdv, op=ALU.mult),
                        reads=seg("qT", 0, c0, 128) + seg("qT", 1, c0, 128) + [("QD", di)], writes=[qk_])
                qd = qv[:, :].rearrange("p (a c n) -> p a c n", a=2, c=2)
                ob, ok = self.psring.next()
                mms = [(av[:, 0:128], vtm[:, j, :], [ak, ("vtok", j)]), (av[:, 128:256], vtm[:, j, :], [ak, ("vtok", j)])]
                for dc in range(2):
                    mms.append((qd[:, 0, dc, :], fb[:, dc, :], [qk_, fk]))
                    mms.append((qd[:, 1, dc, :], bvv[:, dc, :], [qk_, bkk]))
                for i, (lh, rh, rd) in enumerate(mms):
                    g.add("pe", lambda e, ob=ob, lh=lh, rh=rh, i=i, n=len(mms): nc.tensor.matmul(
                        ob[:, :], lhsT=lh, rhs=rh, start=(i == 0), stop=(i == n - 1)), reads=rd, writes=[ok])
                ot, otk = self.yst.next()
                g.add("act", lambda e, ot=ot, ob=ob: nc.scalar.copy(out=ot[:, :512], in_=ob[:, :]), writes=[otk, ok])
                q2, q2k = self.xst.next()
                g.add("act", lambda e, ot=ot, q2=q2: nc.scalar.activation(out=q2[:, :512], in_=ot[:, :512], func=AF.Square),
                      reads=[otk], writes=[q2k])
                sj = 8 + (j % 2) * 8
                st_ = sm[:, sj:sj + 8]
                stk = ("sm", "st", j % 2)
                g.add("dve", lambda e, ot=ot, st_=st_: nc.vector.reduce_sum(out=st_[:, 0:1], in_=ot[:, :512], axis=mybir.AxisListType.X),
                      reads=[otk], writes=[stk])
                g.add("dve", lambda e, q2=q2, st_=st_: nc.vector.reduce_sum(out=st_[:, 1:2], in_=q2[:, :512], axis=mybir.AxisListType.X),
                      reads=[q2k], writes=[stk])
                g.add("dve", lambda e, st_=st_: nc.vector.tensor_single_scalar(out=st_[:, 2:3], in_=st_[:, 0:1], scalar=1.0 / 512,
                                                                               op=ALU.mult), writes=[stk])
                g.add("dve", lambda e, st_=st_: nc.vector.tensor_tensor(out=st_[:, 3:4], in0=st_[:, 2:3], in1=st_[:, 2:3], op=ALU.mult),
                      writes=[stk])
                g.add("dve", lambda e, st_=st_: nc.vector.scalar_tensor_tensor(out=st_[:, 4:5], in0=st_[:, 1:2], scalar=1.0 / 512,
                                                                               in1=st_[:, 3:4], op0=ALU.mult, op1=ALU.subtract),
                      writes=[stk])
                g.add("act", lambda e, st_=st_: nc.scalar.activation(out=st_[:, 5:6], in_=st_[:, 4:5], func=AF.Sqrt,
                                                                     bias=self.epsc[:, 0:1], scale=1.0),
                      reads=["epsc"], writes=[stk])
                g.add("dve", lambda e, st_=st_: nc.vector.reciprocal(out=st_[:, 6:7], in_=st_[:, 5:6]), writes=[stk])
                onv, onk = self.onst.next()
                g.add("dve", lambda e, ot=ot, onv=onv, st_=st_: nc.vector.tensor_scalar(
                    out=onv[:, :], in0=ot[:, :512], scalar1=st_[:, 2:3], scalar2=st_[:, 6:7], op0=ALU.subtract, op1=ALU.mult),
                    reads=[otk, stk], writes=[onk])
                tb, tbk = self.psring.next()
                tbb = tb[:].bitcast(BF16)
                for ec in range(4):
                    g.add("pe", lambda e, tbb=tbb, onv=onv, ec=ec: nc.tensor.transpose(
                        tbb[:, ec * 128:(ec + 1) * 128], onv[:, ec * 128:(ec + 1) * 128], ident),
                        reads=[onk, "prmb"], writes=[tbk])
                gv, gk = self.ogst.next()
                g.add("dve", lambda e, gv=gv, tbb=tbb, c0=c0: nc.vector.tensor_tensor(
                    out=gv[:, :].rearrange("p (c n) -> p c n", n=128), in0=tbb[:, 0:512].rearrange("p (c n) -> p c n", n=128),
                    in1=sgT[:, :, c0:c0 + 128], op=ALU.mult),
                    reads=[k for ec in range(4) for k in seg("sgT", ec, c0, 128)], writes=[gk, tbk])
                g.add("sp", lambda e, gv=gv, c0=c0, h=h: e.dma_start(
                    out=ogs[h * 4:(h + 1) * 4, :, c0:c0 + 128].rearrange("c p n -> p c n"),
                    in_=gv[:, :].rearrange("p (c n) -> p c n", n=128)),
                    reads=[gk], writes=[("ogs", h, j)], dma=True)
                if j < NCH - 1:
                    kv_update(0, j)

        ogv = ogs.rearrange("c p t -> p c t")
        inT = self.big[:, :32 * BW].rearrange("p (k t) -> p k t", t=BW)
        for b0, tiles in BLOCKS3:
            bw = sum(t[1] for t in tiles)
            otiles = [t for t in tiles if ctx_out or not t[2]]
            rd = [("ogs", h, j) for h in range(RET_H) for j in range(b0 // 128, (b0 + bw) // 128)]
            wr = [k for kc in range(32) for k in seg("big", kc, 0, bw)]
            allbig = [k for nm, n in (("qT", 2), ("kT", 2), ("sgT", 4)) for c in range(n) for k in seg(nm, c, 0, T)] + \
                     [("vtok", j) for j in range(NCH)] + [("ktm", d, j) for d in range(2) for j in range(NCH)]
            g.add("sp", lambda e, b0=b0, bw=bw: e.dma_start(out=inT[:, :, :bw], in_=ogv[:, :, b0:b0 + bw]),
                  reads=rd, writes=wr + allbig, dma=True)
            self.out_proj("ret_out", (r,), 32, b0, otiles, "big", inT=inT)
            self.epilogue(b0, otiles)
        self.g.add("dve", lambda e: nc.vector.memset(self.small[:, 32:33], 0.0),
                   reads=[k for kc in range(32) for k in seg("big", kc, 0, BW)], writes=allbig)

def lay_w_in(w, nchunk_cols=128):
    K, N = w.shape
    a = w.reshape(K // 128, 128, N // 128, 128)
    return np.ascontiguousarray(a.transpose(2, 1, 0, 3)).reshape(N // 128, 128, (K // 128) * 128)


def lay_vec(v):
    sh = v.shape[:-1]
    c = v.shape[-1] // 128
    a = v.reshape(sh + (c, 128))
    return np.ascontiguousarray(np.moveaxis(a, -1, 0))


def lay_xT(x):
    t, d = x.shape
    return np.ascontiguousarray(x.T.reshape(d // 128, 128, t))


def declare_dram(nc, shapes):
    out = {}
    for n, v in shapes.items():
        sh, kind = v[0], v[1]
        dt_ = v[2] if len(v) > 2 else F32
        out[n] = (nc.dram_tensor(n, list(sh), dt_, kind=kind) if kind else nc.dram_tensor(n, list(sh), dt_)).ap()
    return out


def copy_x0(p):
    g = p.g
    for kc in range(DC):
        g.add("sp", lambda e, kc=kc: e.dma_start(out=p.dr["xs"][kc], in_=p.dr["x0"][kc]),
              writes=seg("xs", kc, 0, T), dma=True)


def finish(p):
    p.bg_flush()
    g = p.g
    keys = [k for kc in range(DC) for k in seg("xs", kc, 0, T)]
    g.add("sp", lambda e: e.dma_start(out=p.dr["done"], in_=p.epsc[0:1, 0:1]), reads=keys + ["epsc"], dma=True, is_output=True)


def ret_consts():
    m = np.arange(128)[:, None]
    n = np.arange(128)[None, :]
    c = np.zeros((128, 770), np.float32)
    c[:, 0:128] = np.where(n >= m, n - m, 0)
    c[:, 128:256] = np.where(m > n, m - n, 0)
    c[:, 256:384] = np.where(n >= m, 1 / 16, 0)
    c[:, 384:512] = np.where(m > n, 1 / 16, 0)
    c[:, 512:640] = n + 1
    c[:, 640:768] = 128 - n
    c[:, 768] = 127 - np.arange(128)
    c[:, 769] = np.arange(128)
    return c


def rope_tables():
    half = 64
    inv = 10000.0 ** (-np.arange(half, dtype=np.float32) / half)
    pos = np.arange(SEQ)
    rows, cols = (pos // 64).astype(np.float32), (pos % 64).astype(np.float32)
    out = np.zeros((2, 128, 2, T), np.float32)
    out[0, :, :, :CTX] = 1.0
    for dc, pp in enumerate((rows, cols)):
        ang = (pp[None, :] * inv[:, None]).astype(np.float32)
        out[0, :64, dc, CTX:] = np.cos(ang)
        out[0, 64:, dc, CTX:] = np.cos(ang)
        out[1, :64, dc, CTX:] = -np.sin(ang)
        out[1, 64:, dc, CTX:] = np.sin(ang)
    return out


def lay_ret_in(w):
    sw = np.concatenate([np.arange(64, 128), np.arange(0, 64)])
    outs = []
    for h in range(RET_H):
        cols = []
        for base in (0, RET_QK):
            hb = base + h * RET_DK
            for dc in range(2):
                cols.append(np.arange(hb + dc * 128, hb + dc * 128 + 128))
            for dc in range(2):
                cols.append(hb + dc * 128 + sw)
        gb = 2 * RET_QK + RET_V + h * RET_DV
        for ec in range(4):
            cols.append(np.arange(gb + ec * 128, gb + ec * 128 + 128))
        outs.append(lay_w_in(w[:, np.concatenate(cols)]))
    vs = []
    for h in range(RET_H):
        wv = w[:, 2 * RET_QK + h * RET_DV:2 * RET_QK + (h + 1) * RET_DV]
        a = wv.reshape(4, 4, 128, RET_DV).transpose(0, 2, 1, 3)
        vs.append(np.ascontiguousarray(a).reshape(4, 128, 2048))
    return np.stack(outs), np.stack(vs)


SHAPES = {
    "x0": ((DC, 128, T), "ExternalInput"), "cT": ((128, DC, 2), "ExternalInput"),
    "normg": ((128, DEPTH * 6, DC), "ExternalInput"), "adab": ((DEPTH, 128, 144), "ExternalInput"),
    "adaw": ((DEPTH, 144, 128, 2048), "ExternalInput"),
    "ffn_in": ((DEPTH, 2, FC, 2, 128, 2048), "ExternalInput"), "ffn_out": ((DEPTH, 2, DC, 128, DFF), "ExternalInput"),
    "ret_logit": ((2, 128, 16), "ExternalInput"), "ret_gng": ((2, 128, 32), "ExternalInput"),
    "ret_const": ((128, 770), "ExternalInput"), "ident": ((128, 128), "ExternalInput"),
    "rope": ((2, 128, 2, T), "ExternalInput"),
    "ret_in": ((2, RET_H, 12, 128, 2048), "ExternalInput"), "ret_v": ((2, RET_H, 4, 128, 2048), "ExternalInput"),
    "ret_out": ((2, DC, 128, RET_V), "ExternalInput"),
    "gmlp_vec": ((2, 128, GE // 128), "ExternalInput"), "gmlp_bsb": ((128, GG * 128), "ExternalInput"),
    "gmlp_wsT": ((128, GG * 128), "ExternalInput"),
    "gmlp_in": ((2, GE // 128, 128, 2048), "ExternalInput"), "gmlp_out": ((DC, 128, GE), "ExternalInput"),
    "conv_wdw": ((128, CONV_K * DC), "ExternalInput"), "conv_vec": ((3, 128, DC), "ExternalInput"),
    "conv_pw1": ((DC, 2, 128, 2048), "ExternalInput"), "conv_pw2": ((DC, 128, 2048), "ExternalInput"),
    "xs": ((DC, 128, T), "ExternalOutput"), "done": ((1, 1), "ExternalOutput"),
    "ys": ((DC, 128, T), None), "zs": ((DC, 128, 2364), None),
    "hs": ((DC, 128, T), None, BF16), "sbs": ((T // 128, 128, 1024), None, BF16), "ogs": ((32, 128, T), None, BF16),
}


def build_program():
    nc = bass.Bass("TRN2", target_bir_lowering=False)
    with ExitStack() as st:
        dr = declare_dram(nc, SHAPES)
        p = Prog(nc, st, dr)
        copy_x0(p)
        for L in range(DEPTH):
            kind, inst = L % 3, L // 3
            ctx_out = L != DEPTH - 1
            ctx_needed = ctx_out or kind == 0
            p.ada_stage(L)
            p.ffn_stage(L, 0, 0, ctx_needed)
            if kind == 0:
                p.ret_stage(L, inst, ctx_out)
            elif kind == 1:
                p.gmlp_stage(L)
            else:
                p.conv_stage(L)
            p.ffn_stage(L, 1, 2, ctx_out)
        finish(p)
        p.g.emit(st)
    return nc


def host_layout(x, c, ctx, c_ctx, ada_w, ada_b, norm_g, ffn_w_in, ffn_w_out, ret_w_in, ret_w_out,
                ret_decay_logit, ret_gn_g, gmlp_w_in, gmlp_ln_g, gmlp_ln_b, gmlp_w_s, gmlp_b_s, gmlp_w_out,
                conv_w_pw1, conv_w_dw, conv_b_dw, conv_ln_g, conv_ln_b, conv_w_pw2):
    f = lambda a: np.asarray(a, dtype=np.float32)
    sh = {}
    sh["normg"] = lay_vec(f(norm_g).reshape(DEPTH * 6, D))
    sh["adab"] = np.ascontiguousarray(lay_vec(f(ada_b)).transpose(1, 0, 2))
    sh["adaw"] = np.stack([lay_w_in(f(ada_w[L])) for L in range(DEPTH)])
    fin = []
    for L in range(DEPTH):
        row = []
        for wch in range(2):
            wi = lay_w_in(f(ffn_w_in[L, wch]))
            row.append(np.stack([wi[:FC], wi[FC:]], 1))
        fin.append(np.stack(row))
    sh["ffn_in"] = np.stack(fin)
    sh["ffn_out"] = np.stack([np.stack([lay_w_in(f(ffn_w_out[L, wch])) for wch in range(2)]) for L in range(DEPTH)])
    lg = f(ret_decay_logit).reshape(2, 1, 16)
    sh["ret_logit"] = np.ascontiguousarray(np.broadcast_to(lg, (2, 128, 16)))
    sh["ret_gng"] = np.ascontiguousarray(lay_vec(f(ret_gn_g)).transpose(1, 0, 2))
    sh["ret_const"] = ret_consts()
    sh["ident"] = np.eye(128, dtype=np.float32)
    sh["rope"] = rope_tables()
    ri = [lay_ret_in(f(ret_w_in[r])) for r in range(2)]
    sh["ret_in"] = np.stack([a for a, _ in ri])
    sh["ret_v"] = np.stack([b for _, b in ri])
    sh["ret_out"] = np.stack([lay_w_in(f(ret_w_out[r])) for r in range(2)])
    NV = GE // 128
    sh["gmlp_vec"] = np.ascontiguousarray(lay_vec(np.stack([f(gmlp_ln_g[0]), f(gmlp_ln_b[0])])).transpose(1, 0, 2))
    sh["gmlp_bsb"] = np.ascontiguousarray(np.broadcast_to(f(gmlp_b_s[0]).reshape(1, GG * 128), (128, GG * 128)))
    sh["gmlp_wsT"] = np.ascontiguousarray(f(gmlp_w_s[0]).transpose(2, 0, 1)).reshape(128, GG * 128)
    wi = lay_w_in(f(gmlp_w_in[0]))
    sh["gmlp_in"] = np.ascontiguousarray(np.stack([wi[:NV], wi[NV:]], 0))
    sh["gmlp_out"] = lay_w_in(f(gmlp_w_out[0]))
    sh["conv_wdw"] = np.ascontiguousarray(lay_vec(f(conv_w_dw[0]))).reshape(128, CONV_K * DC)
    sh["conv_vec"] = np.ascontiguousarray(
        lay_vec(np.stack([f(conv_b_dw[0]), f(conv_ln_g[0]), f(conv_ln_b[0])])).transpose(1, 0, 2))
    wi = lay_w_in(f(conv_w_pw1[0]))
    sh["conv_pw1"] = np.ascontiguousarray(np.stack([wi[:DC], wi[DC:]], 1))
    sh["conv_pw2"] = lay_w_in(f(conv_w_pw2[0]))
    maps = []
    for b in range(NB):
        m = dict(sh)
        m["x0"] = lay_xT(np.concatenate([f(ctx[b]), f(x[b])], 0))
        m["cT"] = np.ascontiguousarray(lay_vec(np.stack([f(c[b]), f(c_ctx)])).transpose(0, 2, 1))
        maps.append(m)
    return maps


_NC_CACHE = {}


def kernel(**inputs):
    maps = host_layout(**inputs)
    for k, v in maps[0].items():
        assert tuple(v.shape) == tuple(SHAPES[k][0]), (k, v.shape, SHAPES[k][0])
    if "nc" not in _NC_CACHE:
        _NC_CACHE["nc"] = build_program()
    res = run_bass_kernel_spmd(_NC_CACHE["nc"], maps, core_ids=list(range(N_CORES)))
    out = np.empty((NB, SEQ, D), np.float32)
    for b in range(NB):
        xs = np.asarray(res.results[b]["xs"]).reshape(D, T)
        out[b] = xs[:, CTX:].T
    return out
```

```python
import numpy as np
from contextlib import ExitStack
import concourse.bass as bass
import concourse.mybir as mybir
from concourse.bass_utils import run_bass_kernel_spmd

F32 = mybir.dt.float32
BF16 = mybir.dt.bfloat16
AF = mybir.ActivationFunctionType
ALU = mybir.AluOpType

D = 2048
DC = D // 128
DEPTH = 4
NB = 4
SEQ = 2048
CTX = 256
T = CTX + SEQ
DFF = 5632
FC = DFF // 128
EPS = 1e-6
RET_H = 8
RET_DK = 256
RET_DV = 512
RET_QK = 2048
RET_V = 4096
GE = 6144
GG = 8
CONV_K = 31
N_CORES = 4

TILES3 = [(0, 256, True), (256, 512, False), (768, 384, False), (1152, 384, False), (1536, 384, False), (1920, 384, False)]
BLOCKS3 = [(0, TILES3[0:2]), (768, TILES3[2:4]), (1536, TILES3[4:6])]
TILES = TILES3
BLOCKS = BLOCKS3
BW = 768

ENGS = ("pe", "act", "dve", "pool", "sp")
N_DMA_SEMS = 8
NSLOT = 6


class Op:
    __slots__ = ("eng", "fn", "deps", "need_sig", "sigidx", "dma", "dma_sem", "dma_val")

    def __init__(self, eng, fn, dma):
        self.eng = eng
        self.fn = fn
        self.deps = []
        self.need_sig = False
        self.sigidx = 0
        self.dma = dma
        self.dma_sem = None
        self.dma_val = 0


class Gen:
    def __init__(self, nc):
        self.nc = nc
        self.ops = {e: [] for e in ENGS}
        self.lastw = {}
        self.readers = {}
        self.ndma = {e: 0 for e in ENGS}
        self.out_dmas = []

    def add(self, eng, fn, reads=(), writes=(), dma=False, is_output=False):
        op = Op(eng, fn, dma)
        deps = {}
        lastw = self.lastw
        readers = self.readers
        for k in reads:
            w = lastw.get(k)
            if w is not None:
                deps[id(w)] = w
            r = readers.get(k)
            if r is None:
                readers[k] = [op]
            else:
                r.append(op)
        for k in writes:
            w = lastw.get(k)
            if w is not None:
                deps[id(w)] = w
            r = readers.get(k)
            if r:
                for o in r:
                    if o is not op:
                        deps[id(o)] = o
            lastw[k] = op
            readers[k] = []
        if dma:
            j = self.ndma[eng]
            self.ndma[eng] = j + 1
            op.dma_sem = j % N_DMA_SEMS
            op.dma_val = 16 * (j // N_DMA_SEMS + 1)
            if is_output:
                self.out_dmas.append(op)
        for d in deps.values():
            if d.eng == eng and eng == "pe" and not d.dma:
                continue
            op.deps.append(d)
            if not d.dma:
                d.need_sig = True
        self.ops[eng].append(op)
        return op

    def emit(self, stack):
        nc = self.nc
        csem = {e: stack.enter_context(nc.semaphore("c_" + e)) for e in ENGS}
        dsem = {e: [stack.enter_context(nc.semaphore("d_%s_%d" % (e, i))) for i in range(N_DMA_SEMS)]
                for e in ENGS if self.ndma[e] > 0}
        for e in ENGS:
            n = 0
            for op in self.ops[e]:
                if op.need_sig and not op.dma:
                    n += 1
                    op.sigidx = n
        block = stack.enter_context(nc.Block())
        engobj = {"pe": "tensor", "act": "scalar", "dve": "vector", "pool": "gpsimd", "sp": "sync"}

        def make(e):
            def body(eng):
                waited = {}

                def wait(sem, key, val):
                    if waited.get(key, 0) >= val:
                        return
                    waited[key] = val
                    eng.wait_ge(sem, val)

                for op in self.ops[e]:
                    for d in op.deps:
                        if d.dma:
                            wait(dsem[d.eng][d.dma_sem], ("d", d.eng, d.dma_sem), d.dma_val)
                        else:
                            wait(csem[d.eng], ("c", d.eng), d.sigidx)
                    if op.dma:
                        if op.dma_val > 16:
                            wait(dsem[e][op.dma_sem], ("d", e, op.dma_sem), op.dma_val - 16)
                        ins = op.fn(eng)
                        ins.then_inc(dsem[e][op.dma_sem], 16)
                    else:
                        ins = op.fn(eng)
                        if op.need_sig:
                            ins.then_inc(csem[e], 1)
                for op in self.out_dmas:
                    if op.eng == e:
                        wait(dsem[e][op.dma_sem], ("d", e, op.dma_sem), op.dma_val)
            return body

        for e in ENGS:
            if self.ops[e]:
                getattr(block, engobj[e])(make(e))


class Ring:
    def __init__(self, name, views):
        self.name = name
        self.views = views
        self.i = 0

    def next(self):
        j = self.i % len(self.views)
        self.i += 1
        return self.views[j], (self.name, j)


def seg(name, a, c0, w):
    return [(name, a, s) for s in range(c0 // 256, (c0 + w + 255) // 256)]


class Prog:
    def __init__(self, nc, st, dram):
        self.nc = nc
        self.st = st
        self.dr = dram
        self.g = Gen(nc)
        sb = lambda n, s, d: st.enter_context(nc.sbuf_tensor("s_" + n, s, d))
        self.hT = sb("hT", [128, DC, BW], BF16)
        self.big = sb("big", [128, 48 * BW], BF16)
        self.wr = sb("wr", [128, NSLOT, 2048], BF16)
        self.wring = Ring("wr", [self.wr[:, i, :] for i in range(NSLOT)])
        xst = sb("xst", [128, 4, 544], F32)
        self.xst = Ring("xst", [xst[:, i, :] for i in range(4)])
        self.zin = self.xst
        yst = sb("yst", [128, 4, 512], F32)
        self.yst = Ring("yst", [yst[:, i, :] for i in range(4)])
        sq = sb("sq", [128, 3, 512], BF16)
        self.sq = Ring("sq", [sq[:, i, :] for i in range(3)])
        zbuf = sb("zbuf", [128, 2, 544], BF16)
        self.zb = Ring("zb", [zbuf[:, i, :] for i in range(2)])
        sqy = sb("sqy", [128, 3, 512], BF16)
        self.sqy = Ring("sqy", [sqy[:, i, :] for i in range(3)])
        tmp = sb("tmp", [128, 2, 512], F32)
        self.tmp = Ring("tmp", [tmp[:, i, :] for i in range(2)])
        sa = sb("sa", [128, 2, 512], F32)
        self.sa = Ring("sa", [sa[:, i, :] for i in range(2)])
        self.rstd = sb("rstd", [128, 4, 512], F32)
        self.zero = sb("zero", [128, DC, 16], F32)
        self.sfb = sb("sfb", [128, 2, 1024], BF16)
        sbin = sb("sbin", [128, 2, 1024], BF16)
        self.sbin = Ring("sbin", [sbin[:, i, :] for i in range(2)])
        self.ropest = sb("ropest", [128, 2, 512], F32)
        onst = sb("onst", [128, 2, 512], BF16)
        self.onst = Ring("onst", [onst[:, i, :] for i in range(2)])
        ogst = sb("ogst", [128, 2, 512], BF16)
        self.ogst = Ring("ogst", [ogst[:, i, :] for i in range(2)])
        self.small = sb("small", [128, 64], F32)
        self.prm_w = 1536
        self.prm = sb("prm", [128, self.prm_w], F32)
        self.ug = sb("ug", [128, 6, BW], BF16)
        vtm = sb("vtm", [128, 2, 512], BF16)
        self.vtm = Ring("vtm", [vtm[:, i, :] for i in range(2)])
        self.prmb = sb("prmb", [128, 1152], BF16)
        self.ones = sb("ones", [128, 128], BF16)
        self.mod = sb("mod", [128, 144, 2], F32)
        self.vecA = sb("vecA", [128, 2, DC, 2], F32)
        self.vecB = sb("vecB", [128, 2, DC, 2], F32)
        self.vecG = sb("vecG", [128, 2, DC, 2], F32)
        self.par = 0
        self.rstdE = sb("rstdE", [128, 2, 512], F32)
        self.bg = []
        self.normg = sb("normg", [128, DEPTH * 6, DC], F32)
        self.adab = sb("adab", [128, 144], F32)
        self.cT = sb("cT", [128, DC, 2], F32)
        self.scT = sb("scT", [128, DC, 2], BF16)
        self.epsc = sb("epsc", [128, 1], F32)
        self.ps = [st.enter_context(nc.psum_tensor("ps%d" % i, [128, 512], F32)) for i in range(8)]
        self.psring = Ring("ps", [self.ps[i] for i in range(5)])
        self.psring4 = Ring("ps", [self.ps[i] for i in range(4)])
        g = self.g
        nc_ = nc
        g.add("dve", lambda e: nc_.vector.memset(self.ones[:], 1.0), writes=["ones"])
        g.add("dve", lambda e: nc_.vector.memset(self.epsc[:], EPS), writes=["epsc"])
        g.add("dve", lambda e: nc_.vector.memset(self.zero[:], 0.0), writes=["zero"])
        g.add("sp", lambda e: e.dma_start(out=self.normg[:], in_=self.dr["normg"]), writes=["normg"], dma=True)

    def bg_step(self, n=1):
        for _ in range(n):
            if not self.bg:
                return
            try:
                next(self.bg[0][1])
            except StopIteration:
                self.bg.pop(0)

    def bg_flush(self, tag=None):
        keep = []
        for t, gen in self.bg:
            if tag is None or t == tag:
                for _ in gen:
                    pass
            else:
                keep.append((t, gen))
        self.bg = keep

    def wload(self, src_ap):
        self.bg_step(2)
        view, key = self.wring.next()
        self.g.add("pool", lambda e: e.dma_start(out=view, in_=src_ap), writes=[key], dma=True)
        return view, key

    def rstd_from(self, bank, bkey, slot, w, nfeat):
        nc, g = self.nc, self.g
        r = self.rstd[:, slot, :w]
        g.add("act", lambda e: nc.scalar.activation(out=r, in_=bank[:, :w], func=AF.Sqrt,
                                                    bias=self.epsc[:, 0:1], scale=1.0 / nfeat),
              reads=["epsc"], writes=[("rstd", slot), bkey])
        g.add("dve", lambda e: nc.vector.reciprocal(out=r, in_=r), reads=[("rstd", slot)], writes=[("rstd", slot)])
        return r

    def ada_stage(self, L):
        nc, g = self.nc, self.g
        if L == 0:
            g.add("sp", lambda e: e.dma_start(out=self.cT[:], in_=self.dr["cT"]), writes=["cT"], dma=True)
            g.add("act", lambda e: nc.scalar.activation(out=self.scT[:], in_=self.cT[:], func=AF.Silu),
                  reads=["cT"], writes=["scT"])
        g.add("sp", lambda e: e.dma_start(out=self.adab[:], in_=self.dr["adab"][L]), writes=["adab"], dma=True)
        bank, bkey = self.ps[7], ("ps", 7)
        for n in range(144):
            wv, wk = self.wload(self.dr["adaw"][L, n])
            for kc in range(DC):
                g.add("pe", lambda e, n=n, kc=kc, wv=wv: nc.tensor.matmul(
                    bank[:, 2 * n:2 * n + 2], lhsT=wv[:, kc * 128:(kc + 1) * 128], rhs=self.scT[:, kc, :],
                    start=(kc == 0), stop=(kc == DC - 1)),
                    reads=[wk, "scT"], writes=[bkey])
        pv = bank[:, 0:288].rearrange("p (n s) -> p n s", s=2)
        for s in range(2):
            g.add("dve", lambda e, s=s: nc.vector.tensor_tensor(out=self.mod[:, :, s], in0=pv[:, :, s],
                                                                in1=self.adab[:], op=ALU.add),
                  reads=["adab"], writes=["mod", bkey])

    def mod_vectors(self, L, s, weight):
        nc, g = self.nc, self.g
        self.par ^= 1
        pr = self.par
        gpre = self.normg[:, L * 6 + s * 2 + 0, :]
        gpost = self.normg[:, L * 6 + s * 2 + 1, :]
        for j in range(2):
            sh = self.mod[:, (3 * s) * DC:(3 * s + 1) * DC, j]
            sc = self.mod[:, (3 * s + 1) * DC:(3 * s + 2) * DC, j]
            gt = self.mod[:, (3 * s + 2) * DC:(3 * s + 3) * DC, j]
            g.add("dve", lambda e, j=j, sc=sc: nc.vector.scalar_tensor_tensor(
                out=self.vecA[:, pr, :, j], in0=sc, scalar=1.0, in1=gpre, op0=ALU.add, op1=ALU.mult),
                reads=["mod", "normg"], writes=[("vecA", pr)])
            g.add("dve", lambda e, j=j, sh=sh: nc.vector.tensor_copy(out=self.vecB[:, pr, :, j], in_=sh),
                  reads=["mod"], writes=[("vecB", pr)])
            g.add("dve", lambda e, j=j, gt=gt: nc.vector.scalar_tensor_tensor(
                out=self.vecG[:, pr, :, j], in0=gt, scalar=float(weight), in1=gpost, op0=ALU.mult, op1=ALU.mult),
                reads=["mod", "normg"], writes=[("vecG", pr)])

    def prologue(self, b0, tiles):
        for _ in self.prologue_gen(b0, tiles):
            pass

    def prologue_gen(self, b0, tiles):
        nc, g = self.nc, self.g
        xs = self.dr["xs"]
        pr = self.par
        self.bg_flush(tag=b0)
        sbanks = (4, 7)
        for ti, (c0, w, is_ctx) in enumerate(tiles):
            j = 1 if is_ctx else 0
            l0 = c0 - b0
            bank, bkey = self.ps[sbanks[ti]], ("ps", sbanks[ti])
            for kc in range(DC):
                xv, xk = self.xst.next()
                g.add("sp", lambda e, xv=xv, kc=kc, c0=c0, w=w: e.dma_start(out=xv[:, :w], in_=xs[kc, :, c0:c0 + w]),
                      reads=seg("xs", kc, c0, w), writes=[xk], dma=True)
                qv, qk = self.sq.next()
                g.add("act", lambda e, xv=xv, qv=qv, w=w: nc.scalar.activation(out=qv[:, :w], in_=xv[:, :w], func=AF.Square),
                      reads=[xk], writes=[qk])
                g.add("pe", lambda e, qv=qv, kc=kc, w=w, bank=bank: nc.tensor.matmul(
                    bank[:, :w], lhsT=self.ones[:], rhs=qv[:, :w], start=(kc == 0), stop=(kc == DC - 1)),
                    reads=[qk, "ones"], writes=[bkey])
                yield
            r = self.rstd_from(bank, bkey, 2 + ti, w, D)
            for kc in range(DC):
                xv, xk = self.xst.next()
                g.add("sp", lambda e, xv=xv, kc=kc, c0=c0, w=w: e.dma_start(out=xv[:, :w], in_=xs[kc, :, c0:c0 + w]),
                      reads=seg("xs", kc, c0, w), writes=[xk], dma=True)
                tv, tk = self.tmp.next()
                g.add("dve", lambda e, xv=xv, tv=tv, kc=kc, w=w, r=r, j=j: nc.vector.scalar_tensor_tensor(
                    out=tv[:, :w], in0=xv[:, :w], scalar=self.vecA[:, pr, kc, j:j + 1], in1=r, op0=ALU.mult, op1=ALU.mult),
                    reads=[xk, ("vecA", pr), ("rstd", 2 + ti)], writes=[tk])
                g.add("act", lambda e, tv=tv, kc=kc, w=w, l0=l0, j=j: nc.scalar.activation(
                    out=self.hT[:, kc, l0:l0 + w], in_=tv[:, :w], func=AF.Identity,
                    bias=self.vecB[:, pr, kc, j:j + 1], scale=1.0),
                    reads=[tk, ("vecB", pr)], writes=seg("hT", kc, l0, w))
                yield

    def epilogue(self, b0, tiles):
        self.bg_flush()
        gen = self.epilogue_gen(b0, tiles)
        next(gen)
        self.bg.append((b0, gen))

    def epilogue_gen(self, b0, tiles):
        nc, g = self.nc, self.g
        xs, ys = self.dr["xs"], self.dr["ys"]
        pr = self.par
        rs = []
        for ti, (c0, w, is_ctx) in enumerate(tiles):
            bank, bkey = self.ps[5 + ti], ("ps", 5 + ti)
            r = self.rstdE[:, ti, :w]
            g.add("act", lambda e, r=r, bank=bank, w=w: nc.scalar.activation(
                out=r, in_=bank[:, :w], func=AF.Sqrt, bias=self.epsc[:, 0:1], scale=1.0 / D),
                reads=["epsc"], writes=[("rstdE", ti), bkey])
            g.add("dve", lambda e, r=r: nc.vector.reciprocal(out=r, in_=r), writes=[("rstdE", ti)])
            rs.append(r)
        yield
        for ti, (c0, w, is_ctx) in enumerate(tiles):
            j = 1 if is_ctx else 0
            r = rs[ti]
            for d in range(DC):
                yv, yk = self.yst.next()
                g.add("sp", lambda e, yv=yv, d=d, c0=c0, w=w: e.dma_start(out=yv[:, :w], in_=ys[d, :, c0:c0 + w]),
                      reads=seg("ys", d, c0, w), writes=[yk], dma=True)
                xv, xk = self.xst.next()
                g.add("sp", lambda e, xv=xv, d=d, c0=c0, w=w: e.dma_start(out=xv[:, :w], in_=xs[d, :, c0:c0 + w]),
                      reads=seg("xs", d, c0, w), writes=[xk], dma=True)
                g.add("dve", lambda e, yv=yv, d=d, w=w, r=r, j=j: nc.vector.scalar_tensor_tensor(
                    out=yv[:, :w], in0=yv[:, :w], scalar=self.vecG[:, pr, d, j:j + 1], in1=r, op0=ALU.mult, op1=ALU.mult),
                    reads=[yk, ("vecG", pr), ("rstdE", ti)], writes=[yk])
                g.add("dve", lambda e, yv=yv, xv=xv, w=w: nc.vector.tensor_tensor(
                    out=xv[:, :w], in0=xv[:, :w], in1=yv[:, :w], op=ALU.add),
                    reads=[yk, xk], writes=[xk])
                g.add("sp", lambda e, xv=xv, d=d, c0=c0, w=w: e.dma_start(out=xs[d, :, c0:c0 + w], in_=xv[:, :w]),
                      reads=[xk], writes=seg("xs", d, c0, w), dma=True)
                yield

    def y_evac(self, bank, bkey, d, ti, c0, w, first, last, pending):
        nc, g = self.nc, self.g
        ys = self.dr["ys"]
        yv, yk = self.yst.next()
        g.add("dve", lambda e: nc.vector.tensor_copy(out=yv[:, :w], in_=bank[:, :w]), writes=[yk, bkey])
        qv, qk = self.sqy.next()
        g.add("act", lambda e: nc.scalar.activation(out=qv[:, :w], in_=yv[:, :w], func=AF.Square),
              reads=[yk], writes=[qk])
        g.add("sp", lambda e: e.dma_start(out=ys[d, :, c0:c0 + w], in_=yv[:, :w]),
              reads=[yk], writes=seg("ys", d, c0, w), dma=True)
        sbank, skey = self.ps[5 + ti], ("ps", 5 + ti)
        pending.append(lambda: g.add("pe", lambda e: nc.tensor.matmul(
            sbank[:, :w], lhsT=self.ones[:], rhs=qv[:, :w], start=first, stop=last),
            reads=[qk, "ones"], writes=[skey]))

    def out_proj(self, wname, widx, KC, b0, tiles, inkey, inT=None, psring=None, bg=False):
        nc, g = self.nc, self.g
        if inT is None:
            inT = self.big[:, :KC * BW].rearrange("p (k t) -> p k t", t=BW)
        psring = psring or self.psring
        wsrc = self.dr[wname]
        npieces = (KC + 15) // 16
        pending = []
        for d in range(DC):
            slots = []
            for pc in range(npieces):
                k0, k1 = pc * 16, min(KC, pc * 16 + 16)
                src = wsrc[tuple(widx) + (d,)][:, k0 * 128:k1 * 128]
                self.bg_step(2)
                view, key = self.wring.next()
                vv = view[:, :(k1 - k0) * 128]
                g.add("pool", lambda e, vv=vv, src=src: e.dma_start(out=vv, in_=src), writes=[key], dma=True)
                slots.append((view, key))
            for ti, (c0, w, is_ctx) in enumerate(tiles):
                l0 = c0 - b0
                bank, bkey = psring.next()
                for kc in range(KC):
                    view, key = slots[kc // 16]
                    kl = kc % 16
                    g.add("pe", lambda e, view=view, kl=kl, kc=kc, l0=l0, w=w, bank=bank: nc.tensor.matmul(
                        bank[:, :w], lhsT=view[:, kl * 128:(kl + 1) * 128], rhs=inT[:, kc, l0:l0 + w],
                        start=(kc == 0), stop=(kc == KC - 1)),
                        reads=[key] + seg(inkey, kc, l0, w), writes=[bkey])
                while pending:
                    pending.pop(0)()
                self.y_evac(bank, bkey, d, ti, c0, w, d == 0, d == DC - 1, pending)
        while pending:
            pending.pop(0)()

    def ffn_stage(self, L, which, s, do_ctx):
        nc, g = self.nc, self.g
        self.mod_vectors(L, s, 0.5)
        hid = self.big[:, :FC * BW].rearrange("p (k t) -> p k t", t=BW)
        blocks = [(b0, [t for t in tl if do_ctx or not t[2]]) for b0, tl in BLOCKS]
        self.prologue(*blocks[0])
        for bi, (b0, tiles) in enumerate(blocks):
            self.bg_flush(tag=("P", b0))
            for fc in range(FC):
                wa, ka = self.wload(self.dr["ffn_in"][L, which, fc, 0])
                wb, kb = self.wload(self.dr["ffn_in"][L, which, fc, 1])
                for ti, (c0, w, is_ctx) in enumerate(tiles):
                    l0 = c0 - b0
                    ba, bak = self.psring4.next()
                    bb, bbk = self.psring4.next()
                    for (wv, wk, bank, bk) in ((wa, ka, ba, bak), (wb, kb, bb, bbk)):
                        for kc in range(DC):
                            g.add("pe", lambda e, wv=wv, kc=kc, l0=l0, w=w, bank=bank: nc.tensor.matmul(
                                bank[:, :w], lhsT=wv[:, kc * 128:(kc + 1) * 128], rhs=self.hT[:, kc, l0:l0 + w],
                                start=(kc == 0), stop=(kc == DC - 1)),
                                reads=[wk] + seg("hT", kc, l0, w), writes=[bk])
                    sv, sk = self.sa.next()
                    g.add("act", lambda e, sv=sv, ba=ba, w=w: nc.scalar.activation(out=sv[:, :w], in_=ba[:, :w], func=AF.Silu),
                          writes=[sk, bak])
                    g.add("dve", lambda e, sv=sv, bb=bb, fc=fc, l0=l0, w=w: nc.vector.tensor_tensor(
                        out=hid[:, fc, l0:l0 + w], in0=sv[:, :w], in1=bb[:, :w], op=ALU.mult),
                        reads=[sk], writes=seg("big", fc, l0, w) + [bbk])
            self.bg_flush()
            if bi + 1 < len(blocks):
                nb0, ntiles = blocks[bi + 1]
                self.bg.append((("P", nb0), self.prologue_gen(nb0, ntiles)))
            self.out_proj("ffn_out", (L, which), FC, b0, tiles, "big", psring=self.psring4, bg=True)
            self.bg_flush(tag=("P", blocks[bi + 1][0]) if bi + 1 < len(blocks) else "none")
            self.epilogue(b0, tiles)

    def ln_stats_add(self, src, skey, ti, w, first, last, pending, bf_dst=None, bf_key=None):
        nc, g = self.nc, self.g
        if bf_dst is None:
            bv, bk = self.sq.next()
            bk = [bk]
        else:
            bv, bk = bf_dst, bf_key
        g.add("dve", lambda e: nc.vector.tensor_copy(out=bv[:, :w], in_=src), reads=skey, writes=bk)
        qv, qk = self.sq.next()
        g.add("act", lambda e: nc.scalar.activation(out=qv[:, :w], in_=src, func=AF.Square), reads=skey, writes=[qk])
        b1, k1 = self.ps[4 + 2 * ti], ("ps", 4 + 2 * ti)
        b2, k2 = self.ps[5 + 2 * ti], ("ps", 5 + 2 * ti)

        def emit():
            g.add("pe", lambda e: nc.tensor.matmul(b1[:, :w], lhsT=self.ones[:], rhs=bv[:, :w], start=first, stop=last),
                  reads=bk + ["ones"], writes=[k1])
            g.add("pe", lambda e: nc.tensor.matmul(b2[:, :w], lhsT=self.ones[:], rhs=qv[:, :w], start=first, stop=last),
                  reads=[qk, "ones"], writes=[k2])
        pending.append(emit)

    def ln_finish(self, ti, w, nfeat):
        nc, g = self.nc, self.g
        b1, k1 = self.ps[4 + 2 * ti], ("ps", 4 + 2 * ti)
        b2, k2 = self.ps[5 + 2 * ti], ("ps", 5 + 2 * ti)
        mean = self.rstd[:, 2 * ti, :w]
        rs = self.rstd[:, 2 * ti + 1, :w]
        mk, rk = ("rstd", 2 * ti), ("rstd", 2 * ti + 1)
        g.add("act", lambda e: nc.scalar.activation(out=mean, in_=b1[:, :w], func=AF.Copy, scale=1.0 / nfeat),
              writes=[mk, k1])
        g.add("dve", lambda e: nc.vector.tensor_tensor(out=rs, in0=mean, in1=mean, op=ALU.mult), reads=[mk], writes=[rk])
        g.add("dve", lambda e: nc.vector.scalar_tensor_tensor(out=rs, in0=b2[:, :w], scalar=1.0 / nfeat, in1=rs,
                                                              op0=ALU.mult, op1=ALU.subtract),
              writes=[rk, k2])
        g.add("act", lambda e: nc.scalar.activation(out=rs, in_=rs, func=AF.Sqrt, bias=self.epsc[:, 0:1], scale=1.0),
              reads=["epsc"], writes=[rk])
        g.add("dve", lambda e: nc.vector.reciprocal(out=rs, in_=rs), writes=[rk])
        return mean, rs, mk, rk

    def load_prm(self, src, off, n, bf=False):
        dst = (self.prmb if bf else self.prm)[:, off:off + n]
        if bf:
            self.g.add("pool", lambda e: e.dma_start(out=dst, in_=src), writes=["prmb"], dma=True)
        else:
            self.g.add("sp", lambda e: e.dma_start(out=dst, in_=src), writes=["prm"], dma=True)
        return dst

    def conv_stage(self, L):
        nc, g = self.nc, self.g
        dr = self.dr
        self.mod_vectors(L, 1, 1.0)
        wdw = self.load_prm(dr["conv_wdw"], 0, CONV_K * DC).rearrange("p (k c) -> p k c", c=DC)
        bdw = self.load_prm(dr["conv_vec"][0], 512, DC)
        lng = self.load_prm(dr["conv_vec"][1], 528, DC)
        lnb = self.load_prm(dr["conv_vec"][2], 544, DC)
        zs = dr["zs"]
        zsv = zs.rearrange("c p t -> p c t")
        for a in (0, 271, 286, 2349):
            g.add("sp", lambda e, a=a: e.dma_start(out=zsv[:, :, a:a + 15], in_=self.zero[:, :, 0:15]),
                  reads=["zero"], writes=["zpad"], dma=True)

        def zcol(c0, is_ctx):
            return 15 + c0 if is_ctx else 301 + (c0 - CTX)

        for b0, tiles in BLOCKS3:
            self.prologue(b0, tiles)
            for fc in range(DC):
                wa, ka = self.wload(dr["conv_pw1"][fc, 0])
                wb, kb = self.wload(dr["conv_pw1"][fc, 1])
                for ti, (c0, w, is_ctx) in enumerate(tiles):
                    l0 = c0 - b0
                    ba, bak = self.psring.next()
                    bb, bbk = self.psring.next()
                    for (wv, wk, bank, bk) in ((wa, ka, ba, bak), (wb, kb, bb, bbk)):
                        for kc in range(DC):
                            g.add("pe", lambda e, wv=wv, kc=kc, l0=l0, w=w, bank=bank: nc.tensor.matmul(
                                bank[:, :w], lhsT=wv[:, kc * 128:(kc + 1) * 128], rhs=self.hT[:, kc, l0:l0 + w],
                                start=(kc == 0), stop=(kc == DC - 1)),
                                reads=[wk] + seg("hT", kc, l0, w), writes=[bk])
                    sv, sk = self.sa.next()
                    g.add("act", lambda e, sv=sv, bb=bb, w=w: nc.scalar.activation(out=sv[:, :w], in_=bb[:, :w], func=AF.Sigmoid),
                          writes=[sk, bbk])
                    yv, yk = self.yst.next()
                    g.add("dve", lambda e, sv=sv, ba=ba, yv=yv, w=w: nc.vector.tensor_tensor(
                        out=yv[:, :w], in0=sv[:, :w], in1=ba[:, :w], op=ALU.mult), reads=[sk], writes=[yk, bak])
                    z0 = zcol(c0, is_ctx)
                    g.add("sp", lambda e, yv=yv, fc=fc, z0=z0, w=w: e.dma_start(out=zs[fc, :, z0:z0 + w], in_=yv[:, :w]),
                          reads=[yk], writes=[("zs", fc, c0)], dma=True)
        zc = self.big[:].bitcast(F32)[:, :DC * 768].rearrange("p (c t) -> p c t", t=768)
        ident = self.load_prm(dr["ident"], 0, 128, bf=True)
        dg = self.ug[:].rearrange("p a b -> p (a b)")[:, :CONV_K * 128].rearrange("p (k n) -> p k n", n=128)
        PBW = 1152
        for b0, tiles in BLOCKS3:
            pending = []
            for fc in range(DC):
                idb = bass.AP(self.prmb, 0, [[PBW, 128], [0, CONV_K], [1, 128]])
                wdb = bass.AP(self.prm, fc, [[self.prm_w, 128], [DC, CONV_K], [0, 128]])
                g.add("dve", lambda e, idb=idb, wdb=wdb: nc.vector.tensor_tensor(out=dg, in0=idb, in1=wdb, op=ALU.mult),
                      reads=["prm", "prmb"], writes=["dg"])
                for ti, (c0, w, is_ctx) in enumerate(tiles):
                    l0 = c0 - b0
                    z0 = zcol(c0, is_ctx)
                    zv, zk = self.zin.next()
                    allz = ["zpad"] + [("zs", fc, t[0]) for t in TILES3]
                    g.add("sp", lambda e, zv=zv, fc=fc, z0=z0, w=w: e.dma_start(out=zv[:, :w + 30], in_=zs[fc, :, z0 - 15:z0 + w + 15]),
                          reads=allz, writes=[zk], dma=True)
                    zb, zbk = self.zb.next()
                    g.add("act", lambda e, zv=zv, zb=zb, w=w: nc.scalar.copy(out=zb[:, :w + 30], in_=zv[:, :w + 30]),
                          reads=[zk], writes=[zbk])
                    bank, bk = self.psring4.next()
                    for k in range(CONV_K):
                        g.add("pe", lambda e, bank=bank, zb=zb, k=k, w=w: nc.tensor.matmul(
                            bank[:, :w], lhsT=dg[:, k, :], rhs=zb[:, k:k + w], start=(k == 0), stop=(k == CONV_K - 1)),
                            reads=["dg", zbk], writes=[bk])
                    acc = zc[:, fc, l0:l0 + w]
                    ak = seg("big", fc, l0, w)
                    g.add("act", lambda e, bank=bank, acc=acc, fc=fc, w=w: nc.scalar.activation(
                        out=acc, in_=bank[:, :w], func=AF.Identity, bias=bdw[:, fc:fc + 1], scale=1.0),
                        reads=["prm"], writes=ak + [bk])
                    while pending:
                        pending.pop(0)()
                    self.ln_stats_add(acc, ak, ti, w, fc == 0, fc == DC - 1, pending)
            while pending:
                pending.pop(0)()
            for ti, (c0, w, is_ctx) in enumerate(tiles):
                l0 = c0 - b0
                mean, rs, mk, rk = self.ln_finish(ti, w, D)
                for fc in range(DC):
                    acc = zc[:, fc, l0:l0 + w]
                    ak = seg("big", fc, l0, w)
                    tv, tk = self.tmp.next()
                    g.add("dve", lambda e, tv=tv, acc=acc, mean=mean, w=w: nc.vector.tensor_tensor(
                        out=tv[:, :w], in0=acc, in1=mean, op=ALU.subtract), reads=ak + [mk], writes=[tk])
                    g.add("dve", lambda e, tv=tv, fc=fc, rs=rs, w=w: nc.vector.scalar_tensor_tensor(
                        out=tv[:, :w], in0=tv[:, :w], scalar=lng[:, fc:fc + 1], in1=rs, op0=ALU.mult, op1=ALU.mult),
                        reads=[rk, "prm"], writes=[tk])
                    g.add("act", lambda e, tv=tv, fc=fc, l0=l0, w=w: nc.scalar.activation(
                        out=self.hT[:, fc, l0:l0 + w], in_=tv[:, :w], func=AF.Silu, bias=lnb[:, fc:fc + 1], scale=1.0),
                        reads=[tk, "prm"], writes=seg("hT", fc, l0, w))
            self.out_proj("conv_pw2", (), DC, b0, tiles, "hT", inT=self.hT, psring=self.psring4)
            self.epilogue(b0, tiles)

    def gmlp_stage(self, L):
        nc, g = self.nc, self.g
        dr = self.dr
        self.mod_vectors(L, 1, 1.0)
        NV = GE // 128
        lng = self.load_prm(dr["gmlp_vec"][0], 0, NV)
        lnb = self.load_prm(dr["gmlp_vec"][1], NV, NV)
        self.load_prm(dr["gmlp_bsb"], 128, GG * 128)
        wsT = self.load_prm(dr["gmlp_wsT"], 0, GG * 128, bf=True).rearrange("p (g n) -> p g n", n=128)
        ident = self.load_prm(dr["ident"], GG * 128, 128, bf=True)
        vT = self.big[:, :NV * BW].rearrange("p (k t) -> p k t", t=BW)
        ug = self.ug
        for b0, tiles in BLOCKS3:
            self.prologue(b0, tiles)
            pending = []
            for vc in range(NV):
                wv, wk = self.wload(dr["gmlp_in"][1, vc])
                for ti, (c0, w, is_ctx) in enumerate(tiles):
                    l0 = c0 - b0
                    bank, bk = self.psring4.next()
                    for kc in range(DC):
                        g.add("pe", lambda e, wv=wv, kc=kc, l0=l0, w=w, bank=bank: nc.tensor.matmul(
                            bank[:, :w], lhsT=wv[:, kc * 128:(kc + 1) * 128], rhs=self.hT[:, kc, l0:l0 + w],
                            start=(kc == 0), stop=(kc == DC - 1)),
                            reads=[wk] + seg("hT", kc, l0, w), writes=[bk])
                    while pending:
                        pending.pop(0)()
                    sv, sk = self.sa.next()
                    g.add("act", lambda e, sv=sv, bank=bank, w=w: nc.scalar.activation(
                        out=sv[:, :w], in_=bank[:, :w], func=AF.Gelu_apprx_tanh), writes=[sk, bk])
                    self.ln_stats_add(sv[:, :w], [sk], ti, w, vc == 0, vc == NV - 1, pending,
                                      bf_dst=vT[:, vc, l0:l0 + w], bf_key=seg("big", vc, l0, w))
            while pending:
                pending.pop(0)()
            for ti, (c0, w, is_ctx) in enumerate(tiles):
                l0 = c0 - b0
                mean, rs, mk, rk = self.ln_finish(ti, w, GE)
                for vc in range(NV):
                    vv = vT[:, vc, l0:l0 + w]
                    vk = seg("big", vc, l0, w)
                    tv, tk = self.tmp.next()
                    g.add("dve", lambda e, tv=tv, vv=vv, mean=mean, w=w: nc.vector.tensor_tensor(
                        out=tv[:, :w], in0=vv, in1=mean, op=ALU.subtract), reads=vk + [mk], writes=[tk])
                    g.add("dve", lambda e, tv=tv, rs=rs, w=w: nc.vector.tensor_tensor(
                        out=tv[:, :w], in0=tv[:, :w], in1=rs, op=ALU.mult), reads=[rk], writes=[tk])
                    g.add("act", lambda e, tv=tv, vv=vv, vc=vc, w=w: nc.scalar.activation(
                        out=vv, in_=tv[:, :w], func=AF.Identity, bias=lnb[:, vc:vc + 1], scale=lng[:, vc:vc + 1]),
                        reads=[tk, "prm"], writes=vk)
            bw = sum(t[1] for t in tiles)
            nck = bw // 128
            for gi in range(GG):
                for j in range(6):
                    uc = gi * 6 + j
                    wv, wk = self.wload(dr["gmlp_in"][0, uc])
                    for ti, (c0, w, is_ctx) in enumerate(tiles):
                        l0 = c0 - b0
                        bank, bk = self.psring4.next()
                        for kc in range(DC):
                            g.add("pe", lambda e, wv=wv, kc=kc, l0=l0, w=w, bank=bank: nc.tensor.matmul(
                                bank[:, :w], lhsT=wv[:, kc * 128:(kc + 1) * 128], rhs=self.hT[:, kc, l0:l0 + w],
                                start=(kc == 0), stop=(kc == DC - 1)),
                                reads=[wk] + seg("hT", kc, l0, w), writes=[bk])
                        g.add("act", lambda e, j=j, l0=l0, bank=bank, w=w: nc.scalar.activation(
                            out=ug[:, j, l0:l0 + w], in_=bank[:, :w], func=AF.Gelu_apprx_tanh),
                            writes=seg("ug", j, l0, w) + [bk])
                for j in range(6):
                    vc = gi * 6 + j
                    for ck0 in range(0, nck, 4):
                        nb = min(4, nck - ck0)
                        cl0, cw = ck0 * 128, nb * 128
                        vk = seg("big", vc, cl0, cw)
                        pb, pk = self.psring4.next()
                        pbb = pb[:].bitcast(BF16)
                        for i in range(nb):
                            g.add("pe", lambda e, pbb=pbb, i=i, vc=vc, cl0=cl0: nc.tensor.transpose(
                                pbb[:, i * 128:(i + 1) * 128], vT[:, vc, cl0 + i * 128:cl0 + (i + 1) * 128], ident),
                                reads=vk + ["prmb"], writes=[pk])
                        mv, mkk = self.vtm.next()
                        g.add("act", lambda e, mv=mv, pbb=pbb, cw=cw: nc.scalar.copy(out=mv[:, :cw], in_=pbb[:, :cw]),
                              writes=[mkk, pk])
                        mb, mbk = self.psring4.next()
                        for i in range(nb):
                            g.add("pe", lambda e, mb=mb, mv=mv, i=i, gi=gi: nc.tensor.matmul(
                                mb[:, i * 128:(i + 1) * 128], lhsT=mv[:, i * 128:(i + 1) * 128], rhs=wsT[:, gi, :],
                                start=True, stop=True), reads=[mkk, "prmb"], writes=[mbk])
                        bsv = bass.AP(self.prm, 128 + gi * 128, [[self.prm_w, 128], [0, nb], [1, 128]])
                        tv, tk = self.tmp.next()
                        g.add("dve", lambda e, tv=tv, mb=mb, bsv=bsv, nb=nb, cw=cw: nc.vector.tensor_tensor(
                            out=tv[:, :cw].rearrange("p (b n) -> p b n", n=128),
                            in0=mb[:, :cw].rearrange("p (b n) -> p b n", n=128), in1=bsv, op=ALU.add),
                            reads=["prm"], writes=[tk, mbk])
                        g.add("dve", lambda e, tv=tv, j=j, vc=vc, cl0=cl0, cw=cw: nc.vector.tensor_tensor(
                            out=vT[:, vc, cl0:cl0 + cw], in0=tv[:, :cw], in1=ug[:, j, cl0:cl0 + cw], op=ALU.mult),
                            reads=[tk] + seg("ug", j, cl0, cw), writes=vk)
            self.out_proj("gmlp_out", (), NV, b0, tiles, "big", inT=vT, psring=self.psring4)
            self.epilogue(b0, tiles)

    def prologue_to_dram(self):
        g = self.g
        hs = self.dr["hs"]
        hsv = hs.rearrange("c p t -> p c t")
        for b0, tiles in BLOCKS3:
            self.prologue(b0, tiles)
            bw = sum(t[1] for t in tiles)
            rk = [k for kc in range(DC) for k in seg("hT", kc, 0, bw)]
            g.add("sp", lambda e, b0=b0, bw=bw: e.dma_start(out=hsv[:, :, b0:b0 + bw], in_=self.hT[:, :, :bw]),
                  reads=rk, writes=[("hs", b0)], dma=True)

    def ret_stage(self, L, r, ctx_out):
        nc, g = self.nc, self.g
        dr = self.dr
        NCH = T // 128
        self.mod_vectors(L, 1, 1.0)
        P = self.prm
        PW = self.prm_w
        self.load_prm(dr["ret_logit"][r], 0, 16)
        gng = self.load_prm(dr["ret_gng"][r], 16, 32)
        self.load_prm(dr["ret_const"], 64, 770)
        ident = self.load_prm(dr["ident"], 0, 128, bf=True)
        lg = P[:, 0:16]
        posd = [P[:, 64:192], P[:, 192:320]]
        msk = [P[:, 320:448], P[:, 448:576]]
        colc = [P[:, 576:704], P[:, 704:832]]
        pcol = [P[:, 832:833], P[:, 833:834]]
        DT = [P[:, 896:1024], P[:, 1024:1152]]
        QD = P[:, 1152:1408]
        sm = self.small
        g.add("act", lambda e: nc.scalar.activation(out=lg, in_=lg, func=AF.Exp, scale=-1.0), reads=["prm"], writes=["prm"])
        g.add("act", lambda e: nc.scalar.activation(out=lg, in_=lg, func=AF.Ln, bias=1.0, scale=1.0), reads=["prm"], writes=["prm"])
        g.add("dve", lambda e: nc.vector.tensor_single_scalar(out=lg, in_=lg, scalar=-1.0, op=ALU.mult),
              reads=["prm"], writes=["prm"])
        self.prologue_to_dram()

        big = self.big
        qT = big[:, 0:2 * T].rearrange("p (c t) -> p c t", t=T)
        kT = big[:, 2 * T:4 * T].rearrange("p (c t) -> p c t", t=T)
        sgT = big[:, 4 * T:8 * T].rearrange("p (c t) -> p c t", t=T)
        vtm = big[:, 8 * T:12 * T].rearrange("p (j e) -> p j e", e=512)
        ktm = [big[:, 12 * T:14 * T].rearrange("p (j d) -> p j d", d=256),
               big[:, 14 * T:16 * T].rearrange("p (j d) -> p j d", d=256)]
        ugf = self.ug[:].rearrange("p a b -> p (a b)").bitcast(F32)
        S = [ugf[:, 0:1024].rearrange("p (c e) -> p c e", e=512), ugf[:, 1024:2048].rearrange("p (c e) -> p c e", e=512)]
        hsv = dr["hs"].rearrange("c p t -> p c t")
        sbs = dr["sbs"]
        ogs = dr["ogs"]
        rope = dr["rope"]

        for h in range(RET_H):
            for di in range(2):
                col = di * 8 + h
                g.add("act", lambda e, di=di, col=col: nc.scalar.activation(out=DT[di], in_=posd[di], func=AF.Exp,
                                                                            scale=lg[:, col:col + 1]),
                      reads=["prm"], writes=[("DT", di)])
                g.add("dve", lambda e, di=di: nc.vector.tensor_tensor(out=DT[di], in0=DT[di], in1=msk[di], op=ALU.mult),
                      reads=["prm"], writes=[("DT", di)])
                g.add("act", lambda e, di=di, col=col: nc.scalar.activation(out=QD[:, di * 128:(di + 1) * 128], in_=colc[di],
                                                                            func=AF.Exp, scale=lg[:, col:col + 1]),
                      reads=["prm"], writes=[("QD", di)])
                g.add("act", lambda e, di=di, col=col: nc.scalar.activation(out=sm[:, di:di + 1], in_=pcol[di], func=AF.Exp,
                                                                            scale=lg[:, col:col + 1]),
                      reads=["prm"], writes=[("sm", di)])
                g.add("dve", lambda e, di=di: nc.vector.tensor_single_scalar(out=sm[:, di:di + 1], in_=sm[:, di:di + 1],
                                                                             scalar=1.0 / 16.0, op=ALU.mult),
                      writes=[("sm", di)])
                g.add("act", lambda e, di=di, col=col: nc.scalar.activation(out=sm[:, 2 + di:3 + di], in_=lg[:, col:col + 1],
                                                                            func=AF.Exp, scale=128.0),
                      reads=["prm"], writes=[("sm", 2 + di)])
            for b0, tiles in BLOCKS3:
                bw = sum(t[1] for t in tiles)
                g.add("sp", lambda e, b0=b0, bw=bw: e.dma_start(out=self.hT[:, :, :bw], in_=hsv[:, :, b0:b0 + bw]),
                      reads=[("hs", b0)], writes=[k for kc in range(DC) for k in seg("hT", kc, 0, bw)], dma=True)
                for qk in range(2):
                    dstT = qT if qk == 0 else kT
                    dname = "qT" if qk == 0 else "kT"
                    for dc in range(2):
                        w1, k1 = self.wload(dr["ret_in"][r, h, qk * 4 + dc])
                        w2, k2 = self.wload(dr["ret_in"][r, h, qk * 4 + 2 + dc])
                        for ti, (c0, w, is_ctx) in enumerate(tiles):
                            l0 = c0 - b0
                            b1, bk1 = self.psring.next()
                            b2, bk2 = self.psring.next()
                            for (wv, wk, bank, bk) in ((w1, k1, b1, bk1), (w2, k2, b2, bk2)):
                                for kc in range(DC):
                                    g.add("pe", lambda e, wv=wv, kc=kc, l0=l0, w=w, bank=bank: nc.tensor.matmul(
                                        bank[:, :w], lhsT=wv[:, kc * 128:(kc + 1) * 128], rhs=self.hT[:, kc, l0:l0 + w],
                                        start=(kc == 0), stop=(kc == DC - 1)),
                                        reads=[wk] + seg("hT", kc, l0, w), writes=[bk])
                            rk = ("rope", 0)
                            g.add("sp", lambda e, dc=dc, c0=c0, w=w: e.dma_start(
                                out=self.ropest[:, :, :w], in_=rope[:, :, dc, c0:c0 + w].rearrange("a p t -> p a t")),
                                writes=[rk], dma=True)
                            tv, tk = self.tmp.next()
                            g.add("dve", lambda e, tv=tv, b1=b1, dc=dc, w=w: nc.vector.tensor_tensor(
                                out=tv[:, :w], in0=b1[:, :w], in1=self.ropest[:, 0, :w], op=ALU.mult),
                                reads=[rk], writes=[tk, bk1])
                            sv, sk = self.sa.next()
                            g.add("dve", lambda e, sv=sv, b2=b2, dc=dc, w=w: nc.vector.tensor_tensor(
                                out=sv[:, :w], in0=b2[:, :w], in1=self.ropest[:, 1, :w], op=ALU.mult),
                                reads=[rk], writes=[sk, bk2])
                            g.add("dve", lambda e, tv=tv, sv=sv, dstT=dstT, dc=dc, c0=c0, w=w: nc.vector.tensor_tensor(
                                out=dstT[:, dc, c0:c0 + w], in0=tv[:, :w], in1=sv[:, :w], op=ALU.add),
                                reads=[tk, sk], writes=seg(dname, dc, c0, w))
                for ec in range(4):
                    wv, wk = self.wload(dr["ret_in"][r, h, 8 + ec])
                    for ti, (c0, w, is_ctx) in enumerate(tiles):
                        l0 = c0 - b0
                        bank, bk = self.psring.next()
                        for kc in range(DC):
                            g.add("pe", lambda e, wv=wv, kc=kc, l0=l0, w=w, bank=bank: nc.tensor.matmul(
                                bank[:, :w], lhsT=wv[:, kc * 128:(kc + 1) * 128], rhs=self.hT[:, kc, l0:l0 + w],
                                start=(kc == 0), stop=(kc == DC - 1)),
                                reads=[wk] + seg("hT", kc, l0, w), writes=[bk])
                        tv, tk = self.tmp.next()
                        g.add("act", lambda e, tv=tv, bank=bank, w=w: nc.scalar.activation(out=tv[:, :w], in_=bank[:, :w], func=AF.Silu),
                              writes=[tk, bk])
                        g.add("dve", lambda e, tv=tv, ec=ec, c0=c0, w=w, h=h: nc.vector.tensor_single_scalar(
                            out=sgT[:, ec, c0:c0 + w], in_=tv[:, :w], scalar=gng[:, h * 4 + ec:h * 4 + ec + 1],
                            op=ALU.mult), reads=[tk, "prm"], writes=seg("sgT", ec, c0, w))
                vw = [self.wload(dr["ret_v"][r, h, p4]) for p4 in range(4)]
                for j in range(b0 // 128, (b0 + bw) // 128):
                    l0 = j * 128 - b0
                    bank, bk = self.psring.next()
                    for kc in range(DC):
                        wv, wk = vw[kc // 4]
                        kl = kc % 4
                        g.add("pe", lambda e, wv=wv, kl=kl, kc=kc, l0=l0, bank=bank: nc.tensor.matmul(
                            bank[:, :], lhsT=self.hT[:, kc, l0:l0 + 128], rhs=wv[:, kl * 512:(kl + 1) * 512],
                            start=(kc == 0), stop=(kc == DC - 1)),
                            reads=[wk] + seg("hT", kc, l0, 128), writes=[bk])
                    g.add("act", lambda e, j=j, bank=bank: nc.scalar.copy(out=vtm[:, j, :], in_=bank[:, :]),
                          writes=[("vtok", j), bk])
            for j0 in range(0, NCH, 2):
                pb, pk = self.psring.next()
                pbb = pb[:].bitcast(BF16)
                for jj in range(2):
                    for dc in range(2):
                        j = j0 + jj
                        g.add("pe", lambda e, pbb=pbb, jj=jj, dc=dc, j=j: nc.tensor.transpose(
                            pbb[:, (jj * 2 + dc) * 128:(jj * 2 + dc + 1) * 128], kT[:, dc, j * 128:(j + 1) * 128], ident),
                            reads=seg("kT", dc, j * 128, 128) + ["prmb"], writes=[pk])
                for di in range(2):
                    g.add("dve", lambda e, di=di, pbb=pbb, j0=j0: nc.vector.tensor_single_scalar(
                        out=ktm[di][:, j0:j0 + 2, :], in_=pbb[:, 0:512].rearrange("p (j d) -> p j d", d=256),
                        scalar=sm[:, di:di + 1], op=ALU.mult),
                        reads=[("sm", di)], writes=[("ktm", di, j0), ("ktm", di, j0 + 1), pk])

            def kv_update(di, j):
                for dc in range(2):
                    bank, bk = self.psring.next()
                    g.add("pe", lambda e, bank=bank, dc=dc: nc.tensor.matmul(
                        bank[:, :], lhsT=ktm[di][:, j, dc * 128:(dc + 1) * 128], rhs=vtm[:, j, :], start=True, stop=True),
                        reads=[("ktm", di, j), ("vtok", j)], writes=[bk])
                    g.add("dve", lambda e, bank=bank, dc=dc: nc.vector.scalar_tensor_tensor(
                        out=S[di][:, dc, :], in0=S[di][:, dc, :], scalar=sm[:, 2 + di:3 + di], in1=bank[:, :],
                        op0=ALU.mult, op1=ALU.add),
                        reads=[("sm", 2 + di)], writes=[("S", di), bk])

            def zero_state(di):
                g.add("dve", lambda e: nc.vector.memset(S[di], 0.0), writes=[("S", di)])

            def bwd_store(j):
                sv, sk = self.sbin.next()
                g.add("act", lambda e, sv=sv: nc.scalar.copy(out=sv[:, :].rearrange("p (c e) -> p c e", e=512), in_=S[1]),
                      reads=[("S", 1)], writes=[sk])
                g.add("sp", lambda e, sv=sv, j=j: e.dma_start(out=sbs[j], in_=sv[:, :]), reads=[sk], writes=[("sbs", j)], dma=True)

            zero_state(1)
            for j in (1, 0):
                bwd_store(j)
                kv_update(1, j)
            for j in range(NCH - 1, 1, -1):
                bwd_store(j)
                kv_update(1, j)

            zero_state(0)
            for j in range(NCH):
                c0 = j * 128
                fb = self.sfb[:, j % 2, :].rearrange("p (c e) -> p c e", e=512)
                fk = ("sfb", j % 2)
                g.add("act", lambda e, fb=fb: nc.scalar.copy(out=fb, in_=S[0]), reads=[("S", 0)], writes=[fk])
                bv, bkk = self.sbin.next()
                g.add("sp", lambda e, bv=bv, j=j: e.dma_start(out=bv[:, :], in_=sbs[j]), reads=[("sbs", j)], writes=[bkk], dma=True)
                bvv = bv[:, :].rearrange("p (c e) -> p c e", e=512)
                pb, pk = self.psring.next()
                for dc in range(2):
                    g.add("pe", lambda e, pb=pb, dc=dc, c0=c0: nc.tensor.matmul(
                        pb[:, 0:128], lhsT=kT[:, dc, c0:c0 + 128], rhs=qT[:, dc, c0:c0 + 128], start=(dc == 0), stop=(dc == 1)),
                        reads=seg("kT", dc, c0, 128) + seg("qT", dc, c0, 128), writes=[pk])
                av, ak = self.vtm.next()
                for di in range(2):
                    g.add("dve", lambda e, av=av, pb=pb, di=di: nc.vector.tensor_tensor(
                        out=av[:, di * 128:(di + 1) * 128], in0=pb[:, 0:128], in1=DT[di], op=ALU.mult),
                        reads=[("DT", di)], writes=[ak, pk])
                qv, qk_ = self.sq.next()
                qdst = qv[:, :].rearrange("p (a c n) -> p a c n", a=2, c=2)
                for di in range(2):
                    qdv = bass.AP(self.prm, 1152 + di * 128, [[PW, 128], [0, 2], [1, 128]])
                    g.add("dve", lambda e, qdst=qdst, qdv=qdv, c0=c0, di=di: nc.vector.tensor_tensor(
                        out=qdst[:, di, :, :], in0=qT[:, :, c0:c0 + 128], in1=qdv, op=ALU.mult),
                        reads=seg("qT", 0, c0, 128) + seg("qT", 1, c0, 128) + [("QD", di)], writes=[qk_])
                qd = qv[:, :].rearrange("p (a c n) -> p a c n", a=2, c=2)
                ob, ok = self.psring.next()
                mms = [(av[:, 0:128], vtm[:, j, :], [ak, ("vtok", j)]), (av[:, 128:256], vtm[:, j, :], [ak, ("vtok", j)])]
                for dc in range(2):
                    mms.append((qd[:, 0, dc, :], fb[:, dc, :], [qk_, fk]))
                    mms.append((qd[:, 1, dc, :], bvv[:, dc, :], [qk_, bkk]))
                for i, (lh, rh, rd) in enumerate(mms):
                    g.add("pe", lambda e, ob=ob, lh=lh, rh=rh, i=i, n=len(mms): nc.tensor.matmul(
                        ob[:, :], lhsT=lh, rhs=rh, start=(i == 0), stop=(i == n - 1)), reads=rd, writes=[ok])
                ot, otk = self.yst.next()
                g.add("act", lambda e, ot=ot, ob=ob: nc.scalar.copy(out=ot[:, :512], in_=ob[:, :]), writes=[otk, ok])
                q2, q2k = self.xst.next()
                g.add("act", lambda e, ot=ot, q2=q2: nc.scalar.activation(out=q2[:, :512], in_=ot[:, :512], func=AF.Square),
                      reads=[otk], writes=[q2k])
                sj = 8 + (j % 2) * 8
                st_ = sm[:, sj:sj + 8]
                stk = ("sm", "st", j % 2)
                g.add("dve", lambda e, ot=ot, st_=st_: nc.vector.reduce_sum(out=st_[:, 0:1], in_=ot[:, :512], axis=mybir.AxisListType.X),
                      reads=[otk], writes=[stk])
                g.add("dve", lambda e, q2=q2, st_=st_: nc.vector.reduce_sum(out=st_[:, 1:2], in_=q2[:, :512], axis=mybir.AxisListType.X),
                      reads=[q2k], writes=[stk])
                g.add("dve", lambda e, st_=st_: nc.vector.tensor_single_scalar(out=st_[:, 2:3], in_=st_[:, 0:1], scalar=1.0 / 512,
                                                                               op=ALU.mult), writes=[stk])
                g.add("dve", lambda e, st_=st_: nc.vector.tensor_tensor(out=st_[:, 3:4], in0=st_[:, 2:3], in1=st_[:, 2:3], op=ALU.mult),
                      writes=[stk])
                g.add("dve", lambda e, st_=st_: nc.vector.scalar_tensor_tensor(out=st_[:, 4:5], in0=st_[:, 1:2], scalar=1.0 / 512,
                                                                               in1=st_[:, 3:4], op0=ALU.mult, op1=ALU.subtract),
                      writes=[stk])
                g.add("act", lambda e, st_=st_: nc.scalar.activation(out=st_[:, 5:6], in_=st_[:, 4:5], func=AF.Sqrt,
                                                                     bias=self.epsc[:, 0:1], scale=1.0),
                      reads=["epsc"], writes=[stk])
                g.add("dve", lambda e, st_=st_: nc.vector.reciprocal(out=st_[:, 6:7], in_=st_[:, 5:6]), writes=[stk])
                onv, onk = self.onst.next()
                g.add("dve", lambda e, ot=ot, onv=onv, st_=st_: nc.vector.tensor_scalar(
                    out=onv[:, :], in0=ot[:, :512], scalar1=st_[:, 2:3], scalar2=st_[:, 6:7], op0=ALU.subtract, op1=ALU.mult),
                    reads=[otk, stk], writes=[onk])
                tb, tbk = self.psring.next()
                tbb = tb[:].bitcast(BF16)
                for ec in range(4):
                    g.add("pe", lambda e, tbb=tbb, onv=onv, ec=ec: nc.tensor.transpose(
                        tbb[:, ec * 128:(ec + 1) * 128], onv[:, ec * 128:(ec + 1) * 128], ident),
                        reads=[onk, "prmb"], writes=[tbk])
                gv, gk = self.ogst.next()
                g.add("dve", lambda e, gv=gv, tbb=tbb, c0=c0: nc.vector.tensor_tensor(
                    out=gv[:, :].rearrange("p (c n) -> p c n", n=128), in0=tbb[:, 0:512].rearrange("p (c n) -> p c n", n=128),
                    in1=sgT[:, :, c0:c0 + 128], op=ALU.mult),
                    reads=[k for ec in range(4) for k in seg("sgT", ec, c0, 128)], writes=[gk, tbk])
                g.add("sp", lambda e, gv=gv, c0=c0, h=h: e.dma_start(
                    out=ogs[h * 4:(h + 1) * 4, :, c0:c0 + 128].rearrange("c p n -> p c n"),
                    in_=gv[:, :].rearrange("p (c n) -> p c n", n=128)),
                    reads=[gk], writes=[("ogs", h, j)], dma=True)
                if j < NCH - 1:
                    kv_update(0, j)

        ogv = ogs.rearrange("c p t -> p c t")
        inT = self.big[:, :32 * BW].rearrange("p (k t) -> p k t", t=BW)
        for b0, tiles in BLOCKS3:
            bw = sum(t[1] for t in tiles)
            otiles = [t for t in tiles if ctx_out or not t[2]]
            rd = [("ogs", h, j) for h in range(RET_H) for j in range(b0 // 128, (b0 + bw) // 128)]
            wr = [k for kc in range(32) for k in seg("big", kc, 0, bw)]
            allbig = [k for nm, n in (("qT", 2), ("kT", 2), ("sgT", 4)) for c in range(n) for k in seg(nm, c, 0, T)] + \
                     [("vtok", j) for j in range(NCH)] + [("ktm", d, j) for d in range(2) for j in range(NCH)]
            g.add("sp", lambda e, b0=b0, bw=bw: e.dma_start(out=inT[:, :, :bw], in_=ogv[:, :, b0:b0 + bw]),
                  reads=rd, writes=wr + allbig, dma=True)
            self.out_proj("ret_out", (r,), 32, b0, otiles, "big", inT=inT)
            self.epilogue(b0, otiles)
        self.g.add("dve", lambda e: nc.vector.memset(self.small[:, 32:33], 0.0),
                   reads=[k for kc in range(32) for k in seg("big", kc, 0, BW)], writes=allbig)

def lay_w_in(w, nchunk_cols=128):
    K, N = w.shape
    a = w.reshape(K // 128, 128, N // 128, 128)
    return np.ascontiguousarray(a.transpose(2, 1, 0, 3)).reshape(N // 128, 128, (K // 128) * 128)


def lay_vec(v):
    sh = v.shape[:-1]
    c = v.shape[-1] // 128
    a = v.reshape(sh + (c, 128))
    return np.ascontiguousarray(np.moveaxis(a, -1, 0))


def lay_xT(x):
    t, d = x.shape
    return np.ascontiguousarray(x.T.reshape(d // 128, 128, t))


def declare_dram(nc, shapes):
    out = {}
    for n, v in shapes.items():
        sh, kind = v[0], v[1]
        dt_ = v[2] if len(v) > 2 else F32
        out[n] = (nc.dram_tensor(n, list(sh), dt_, kind=kind) if kind else nc.dram_tensor(n, list(sh), dt_)).ap()
    return out


def copy_x0(p):
    g = p.g
    for kc in range(DC):
        g.add("sp", lambda e, kc=kc: e.dma_start(out=p.dr["xs"][kc], in_=p.dr["x0"][kc]),
              writes=seg("xs", kc, 0, T), dma=True)


def finish(p):
    p.bg_flush()
    g = p.g
    keys = [k for kc in range(DC) for k in seg("xs", kc, 0, T)]
    g.add("sp", lambda e: e.dma_start(out=p.dr["done"], in_=p.epsc[0:1, 0:1]), reads=keys + ["epsc"], dma=True, is_output=True)


def ret_consts():
    m = np.arange(128)[:, None]
    n = np.arange(128)[None, :]
    c = np.zeros((128, 770), np.float32)
    c[:, 0:128] = np.where(n >= m, n - m, 0)
    c[:, 128:256] = np.where(m > n, m - n, 0)
    c[:, 256:384] = np.where(n >= m, 1 / 16, 0)
    c[:, 384:512] = np.where(m > n, 1 / 16, 0)
    c[:, 512:640] = n + 1
    c[:, 640:768] = 128 - n
    c[:, 768] = 127 - np.arange(128)
    c[:, 769] = np.arange(128)
    return c


def rope_tables():
    half = 64
    inv = 10000.0 ** (-np.arange(half, dtype=np.float32) / half)
    pos = np.arange(SEQ)
    rows, cols = (pos // 64).astype(np.float32), (pos % 64).astype(np.float32)
    out = np.zeros((2, 128, 2, T), np.float32)
    out[0, :, :, :CTX] = 1.0
    for dc, pp in enumerate((rows, cols)):
        ang = (pp[None, :] * inv[:, None]).astype(np.float32)
        out[0, :64, dc, CTX:] = np.cos(ang)
        out[0, 64:, dc, CTX:] = np.cos(ang)
        out[1, :64, dc, CTX:] = -np.sin(ang)
        out[1, 64:, dc, CTX:] = np.sin(ang)
    return out


def lay_ret_in(w):
    sw = np.concatenate([np.arange(64, 128), np.arange(0, 64)])
    outs = []
    for h in range(RET_H):
        cols = []
        for base in (0, RET_QK):
            hb = base + h * RET_DK
            for dc in range(2):
                cols.append(np.arange(hb + dc * 128, hb + dc * 128 + 128))
            for dc in range(2):
                cols.append(hb + dc * 128 + sw)
        gb = 2 * RET_QK + RET_V + h * RET_DV
        for ec in range(4):
            cols.append(np.arange(gb + ec * 128, gb + ec * 128 + 128))
        outs.append(lay_w_in(w[:, np.concatenate(cols)]))
    vs = []
    for h in range(RET_H):
        wv = w[:, 2 * RET_QK + h * RET_DV:2 * RET_QK + (h + 1) * RET_DV]
        a = wv.reshape(4, 4, 128, RET_DV).transpose(0, 2, 1, 3)
        vs.append(np.ascontiguousarray(a).reshape(4, 128, 2048))
    return np.stack(outs), np.stack(vs)


SHAPES = {
    "x0": ((DC, 128, T), "ExternalInput"), "cT": ((128, DC, 2), "ExternalInput"),
    "normg": ((128, DEPTH * 6, DC), "ExternalInput"), "adab": ((DEPTH, 128, 144), "ExternalInput"),
    "adaw": ((DEPTH, 144, 128, 2048), "ExternalInput"),
    "ffn_in": ((DEPTH, 2, FC, 2, 128, 2048), "ExternalInput"), "ffn_out": ((DEPTH, 2, DC, 128, DFF), "ExternalInput"),
    "ret_logit": ((2, 128, 16), "ExternalInput"), "ret_gng": ((2, 128, 32), "ExternalInput"),
    "ret_const": ((128, 770), "ExternalInput"), "ident": ((128, 128), "ExternalInput"),
    "rope": ((2, 128, 2, T), "ExternalInput"),
    "ret_in": ((2, RET_H, 12, 128, 2048), "ExternalInput"), "ret_v": ((2, RET_H, 4, 128, 2048), "ExternalInput"),
    "ret_out": ((2, DC, 128, RET_V), "ExternalInput"),
    "gmlp_vec": ((2, 128, GE // 128), "ExternalInput"), "gmlp_bsb": ((128, GG * 128), "ExternalInput"),
    "gmlp_wsT": ((128, GG * 128), "ExternalInput"),
    "gmlp_in": ((2, GE // 128, 128, 2048), "ExternalInput"), "gmlp_out": ((DC, 128, GE), "ExternalInput"),
    "conv_wdw": ((128, CONV_K * DC), "ExternalInput"), "conv_vec": ((3, 128, DC), "ExternalInput"),
    "conv_pw1": ((DC, 2, 128, 2048), "ExternalInput"), "conv_pw2": ((DC, 128, 2048), "ExternalInput"),
    "xs": ((DC, 128, T), "ExternalOutput"), "done": ((1, 1), "ExternalOutput"),
    "ys": ((DC, 128, T), None), "zs": ((DC, 128, 2364), None),
    "hs": ((DC, 128, T), None, BF16), "sbs": ((T // 128, 128, 1024), None, BF16), "ogs": ((32, 128, T), None, BF16),
}


def build_program():
    nc = bass.Bass("TRN2", target_bir_lowering=False)
    with ExitStack() as st:
        dr = declare_dram(nc, SHAPES)
        p = Prog(nc, st, dr)
        copy_x0(p)
        for L in range(DEPTH):
            kind, inst = L % 3, L // 3
            ctx_out = L != DEPTH - 1
            ctx_needed = ctx_out or kind == 0
            p.ada_stage(L)
            p.ffn_stage(L, 0, 0, ctx_needed)
            if kind == 0:
                p.ret_stage(L, inst, ctx_out)
            elif kind == 1:
                p.gmlp_stage(L)
            else:
                p.conv_stage(L)
            p.ffn_stage(L, 1, 2, ctx_out)
        finish(p)
        p.g.emit(st)
    return nc


def host_layout(x, c, ctx, c_ctx, ada_w, ada_b, norm_g, ffn_w_in, ffn_w_out, ret_w_in, ret_w_out,
                ret_decay_logit, ret_gn_g, gmlp_w_in, gmlp_ln_g, gmlp_ln_b, gmlp_w_s, gmlp_b_s, gmlp_w_out,
                conv_w_pw1, conv_w_dw, conv_b_dw, conv_ln_g, conv_ln_b, conv_w_pw2):
    f = lambda a: np.asarray(a, dtype=np.float32)
    sh = {}
    sh["normg"] = lay_vec(f(norm_g).reshape(DEPTH * 6, D))
    sh["adab"] = np.ascontiguousarray(lay_vec(f(ada_b)).transpose(1, 0, 2))
    sh["adaw"] = np.stack([lay_w_in(f(ada_w[L])) for L in range(DEPTH)])
    fin = []
    for L in range(DEPTH):
        row = []
        for wch in range(2):
            wi = lay_w_in(f(ffn_w_in[L, wch]))
            row.append(np.stack([wi[:FC], wi[FC:]], 1))
        fin.append(np.stack(row))
    sh["ffn_in"] = np.stack(fin)
    sh["ffn_out"] = np.stack([np.stack([lay_w_in(f(ffn_w_out[L, wch])) for wch in range(2)]) for L in range(DEPTH)])
    lg = f(ret_decay_logit).reshape(2, 1, 16)
    sh["ret_logit"] = np.ascontiguousarray(np.broadcast_to(lg, (2, 128, 16)))
    sh["ret_gng"] = np.ascontiguousarray(lay_vec(f(ret_gn_g)).transpose(1, 0, 2))
    sh["ret_const"] = ret_consts()
    sh["ident"] = np.eye(128, dtype=np.float32)
    sh["rope"] = rope_tables()
    ri = [lay_ret_in(f(ret_w_in[r])) for r in range(2)]
    sh["ret_in"] = np.stack([a for a, _ in ri])
    sh["ret_v"] = np.stack([b for _, b in ri])
    sh["ret_out"] = np.stack([lay_w_in(f(ret_w_out[r])) for r in range(2)])
    NV = GE // 128
    sh["gmlp_vec"] = np.ascontiguousarray(lay_vec(np.stack([f(gmlp_ln_g[0]), f(gmlp_ln_b[0])])).transpose(1, 0, 2))
    sh["gmlp_bsb"] = np.ascontiguousarray(np.broadcast_to(f(gmlp_b_s[0]).reshape(1, GG * 128), (128, GG * 128)))
    sh["gmlp_wsT"] = np.ascontiguousarray(f(gmlp_w_s[0]).transpose(2, 0, 1)).reshape(128, GG * 128)
    wi = lay_w_in(f(gmlp_w_in[0]))
    sh["gmlp_in"] = np.ascontiguousarray(np.stack([wi[:NV], wi[NV:]], 0))
    sh["gmlp_out"] = lay_w_in(f(gmlp_w_out[0]))
    sh["conv_wdw"] = np.ascontiguousarray(lay_vec(f(conv_w_dw[0]))).reshape(128, CONV_K * DC)
    sh["conv_vec"] = np.ascontiguousarray(
        lay_vec(np.stack([f(conv_b_dw[0]), f(conv_ln_g[0]), f(conv_ln_b[0])])).transpose(1, 0, 2))
    wi = lay_w_in(f(conv_w_pw1[0]))
    sh["conv_pw1"] = np.ascontiguousarray(np.stack([wi[:DC], wi[DC:]], 1))
    sh["conv_pw2"] = lay_w_in(f(conv_w_pw2[0]))
    maps = []
    for b in range(NB):
        m = dict(sh)
        m["x0"] = lay_xT(np.concatenate([f(ctx[b]), f(x[b])], 0))
        m["cT"] = np.ascontiguousarray(lay_vec(np.stack([f(c[b]), f(c_ctx)])).transpose(0, 2, 1))
        maps.append(m)
    return maps


_NC_CACHE = {}


PLACE = [0, 1, 4, 5]
N_LAUNCH = 8


def kernel(**inputs):
    maps = host_layout(**inputs)
    for k, v in maps[0].items():
        assert tuple(v.shape) == tuple(SHAPES[k][0]), (k, v.shape, SHAPES[k][0])
    if "nc" not in _NC_CACHE:
        _NC_CACHE["nc"] = build_program()
    zero = {k: np.zeros_like(v) for k, v in maps[0].items()}
    in_maps = [zero] * N_LAUNCH
    in_maps = list(in_maps)
    for b, cidx in enumerate(PLACE):
        in_maps[cidx] = maps[b]
    res = run_bass_kernel_spmd(_NC_CACHE["nc"], in_maps, core_ids=list(range(N_LAUNCH)))
    out = np.empty((NB, SEQ, D), np.float32)
    for b, cidx in enumerate(PLACE):
        xs = np.asarray(res.results[cidx]["xs"]).reshape(D, T)
        out[b] = xs[:, CTX:].T
    return out
```

```python
import numpy as np
from contextlib import ExitStack
import concourse.bass as bass
import concourse.mybir as mybir
from concourse.bass_utils import run_bass_kernel_spmd

F32 = mybir.dt.float32
BF16 = mybir.dt.bfloat16
AF = mybir.ActivationFunctionType
ALU = mybir.AluOpType

D = 2048
DC = D // 128
DEPTH = 4
NB = 4
SEQ = 2048
CTX = 256
T = CTX + SEQ
DFF = 5632
FC = DFF // 128
EPS = 1e-6
RET_H = 8
RET_DK = 256
RET_DV = 512
RET_QK = 2048
RET_V = 4096
GE = 6144
GG = 8
CONV_K = 31
N_CORES = 4

TILES3 = [(0, 256, True), (256, 512, False), (768, 384, False), (1152, 384, False), (1536, 384, False), (1920, 384, False)]
BLOCKS3 = [(0, TILES3[0:2]), (768, TILES3[2:4]), (1536, TILES3[4:6])]
TILES = TILES3
BLOCKS = BLOCKS3
BW = 768

ENGS = ("pe", "act", "dve", "pool", "sp")
N_DMA_SEMS = 8
NSLOT = 6


class Op:
    __slots__ = ("eng", "fn", "deps", "need_sig", "sigidx", "dma", "dma_sem", "dma_val")

    def __init__(self, eng, fn, dma):
        self.eng = eng
        self.fn = fn
        self.deps = []
        self.need_sig = False
        self.sigidx = 0
        self.dma = dma
        self.dma_sem = None
        self.dma_val = 0


class Gen:
    def __init__(self, nc):
        self.nc = nc
        self.ops = {e: [] for e in ENGS}
        self.lastw = {}
        self.readers = {}
        self.ndma = {e: 0 for e in ENGS}
        self.out_dmas = []

    def add(self, eng, fn, reads=(), writes=(), dma=False, is_output=False):
        op = Op(eng, fn, dma)
        deps = {}
        lastw = self.lastw
        readers = self.readers
        for k in reads:
            w = lastw.get(k)
            if w is not None:
                deps[id(w)] = w
            r = readers.get(k)
            if r is None:
                readers[k] = [op]
            else:
                r.append(op)
        for k in writes:
            w = lastw.get(k)
            if w is not None:
                deps[id(w)] = w
            r = readers.get(k)
            if r:
                for o in r:
                    if o is not op:
                        deps[id(o)] = o
            lastw[k] = op
            readers[k] = []
        if dma:
            j = self.ndma[eng]
            self.ndma[eng] = j + 1
            op.dma_sem = j % N_DMA_SEMS
            op.dma_val = 16 * (j // N_DMA_SEMS + 1)
            if is_output:
                self.out_dmas.append(op)
        for d in deps.values():
            if d.eng == eng and eng == "pe" and not d.dma:
                continue
            op.deps.append(d)
            if not d.dma:
                d.need_sig = True
        self.ops[eng].append(op)
        return op

    def emit(self, stack):
        nc = self.nc
        csem = {e: stack.enter_context(nc.semaphore("c_" + e)) for e in ENGS}
        dsem = {e: [stack.enter_context(nc.semaphore("d_%s_%d" % (e, i))) for i in range(N_DMA_SEMS)]
                for e in ENGS if self.ndma[e] > 0}
        for e in ENGS:
            n = 0
            for op in self.ops[e]:
                if op.need_sig and not op.dma:
                    n += 1
                    op.sigidx = n
        block = stack.enter_context(nc.Block())
        engobj = {"pe": "tensor", "act": "scalar", "dve": "vector", "pool": "gpsimd", "sp": "sync"}

        def make(e):
            def body(eng):
                waited = {}

                def wait(sem, key, val):
                    if waited.get(key, 0) >= val:
                        return
                    waited[key] = val
                    eng.wait_ge(sem, val)

                for op in self.ops[e]:
                    for d in op.deps:
                        if d.dma:
                            wait(dsem[d.eng][d.dma_sem], ("d", d.eng, d.dma_sem), d.dma_val)
                        else:
                            wait(csem[d.eng], ("c", d.eng), d.sigidx)
                    if op.dma:
                        if op.dma_val > 16:
                            wait(dsem[e][op.dma_sem], ("d", e, op.dma_sem), op.dma_val - 16)
                        ins = op.fn(eng)
                        ins.then_inc(dsem[e][op.dma_sem], 16)
                    else:
                        ins = op.fn(eng)
                        if op.need_sig:
                            ins.then_inc(csem[e], 1)
                for op in self.out_dmas:
                    if op.eng == e:
                        wait(dsem[e][op.dma_sem], ("d", e, op.dma_sem), op.dma_val)
            return body

        for e in ENGS:
            if self.ops[e]:
                getattr(block, engobj[e])(make(e))


class Ring:
    def __init__(self, name, views):
        self.name = name
        self.views = views
        self.i = 0

    def next(self):
        j = self.i % len(self.views)
        self.i += 1
        return self.views[j], (self.name, j)


def seg(name, a, c0, w):
    return [(name, a, s) for s in range(c0 // 256, (c0 + w + 255) // 256)]


class Prog:
    def __init__(self, nc, st, dram):
        self.nc = nc
        self.st = st
        self.dr = dram
        self.g = Gen(nc)
        sb = lambda n, s, d: st.enter_context(nc.sbuf_tensor("s_" + n, s, d))
        self.hT = sb("hT", [128, DC, BW], BF16)
        self.big = sb("big", [128, 48 * BW], BF16)
        self.wr = sb("wr", [128, NSLOT, 2048], BF16)
        self.wring = Ring("wr", [self.wr[:, i, :] for i in range(NSLOT)])
        xst = sb("xst", [128, 4, 544], F32)
        self.xst = Ring("xst", [xst[:, i, :] for i in range(4)])
        self.zin = self.xst
        yst = sb("yst", [128, 4, 512], F32)
        self.yst = Ring("yst", [yst[:, i, :] for i in range(4)])
        sq = sb("sq", [128, 3, 512], BF16)
        self.sq = Ring("sq", [sq[:, i, :] for i in range(3)])
        zbuf = sb("zbuf", [128, 2, 544], BF16)
        self.zb = Ring("zb", [zbuf[:, i, :] for i in range(2)])
        sqy = sb("sqy", [128, 3, 512], BF16)
        self.sqy = Ring("sqy", [sqy[:, i, :] for i in range(3)])
        tmp = sb("tmp", [128, 2, 512], F32)
        self.tmp = Ring("tmp", [tmp[:, i, :] for i in range(2)])
        sa = sb("sa", [128, 2, 512], F32)
        self.sa = Ring("sa", [sa[:, i, :] for i in range(2)])
        self.rstd = sb("rstd", [128, 4, 512], F32)
        self.zero = sb("zero", [128, DC, 16], F32)
        self.sfb = sb("sfb", [128, 2, 1024], BF16)
        sbin = sb("sbin", [128, 2, 1024], BF16)
        self.sbin = Ring("sbin", [sbin[:, i, :] for i in range(2)])
        self.ropest = sb("ropest", [128, 2, 512], F32)
        onst = sb("onst", [128, 2, 512], BF16)
        self.onst = Ring("onst", [onst[:, i, :] for i in range(2)])
        ogst = sb("ogst", [128, 2, 512], BF16)
        self.ogst = Ring("ogst", [ogst[:, i, :] for i in range(2)])
        self.small = sb("small", [128, 64], F32)
        self.prm_w = 1536
        self.prm = sb("prm", [128, self.prm_w], F32)
        self.ug = sb("ug", [128, 6, BW], BF16)
        vtm = sb("vtm", [128, 2, 512], BF16)
        self.vtm = Ring("vtm", [vtm[:, i, :] for i in range(2)])
        self.prmb = sb("prmb", [128, 1152], BF16)
        self.ones = sb("ones", [128, 128], BF16)
        self.mod = sb("mod", [128, 144, 2], F32)
        self.vecA = sb("vecA", [128, 2, DC, 2], F32)
        self.vecB = sb("vecB", [128, 2, DC, 2], F32)
        self.vecG = sb("vecG", [128, 2, DC, 2], F32)
        self.par = 0
        self.rstdE = sb("rstdE", [128, 2, 512], F32)
        self.bg = []
        self.normg = sb("normg", [128, DEPTH * 6, DC], F32)
        self.adab = sb("adab", [128, 144], F32)
        self.cT = sb("cT", [128, DC, 2], F32)
        self.scT = sb("scT", [128, DC, 2], BF16)
        self.epsc = sb("epsc", [128, 1], F32)
        self.ps = [st.enter_context(nc.psum_tensor("ps%d" % i, [128, 512], F32)) for i in range(8)]
        self.psring = Ring("ps", [self.ps[i] for i in range(5)])
        self.psring4 = Ring("ps", [self.ps[i] for i in range(4)])
        g = self.g
        nc_ = nc
        g.add("dve", lambda e: nc_.vector.memset(self.ones[:], 1.0), writes=["ones"])
        g.add("dve", lambda e: nc_.vector.memset(self.epsc[:], EPS), writes=["epsc"])
        g.add("dve", lambda e: nc_.vector.memset(self.zero[:], 0.0), writes=["zero"])
        g.add("sp", lambda e: e.dma_start(out=self.normg[:], in_=self.dr["normg"]), writes=["normg"], dma=True)

    def bg_step(self, n=1):
        for _ in range(n):
            if not self.bg:
                return
            try:
                next(self.bg[0][1])
            except StopIteration:
                self.bg.pop(0)

    def bg_flush(self, tag=None):
        keep = []
        for t, gen in self.bg:
            if tag is None or t == tag:
                for _ in gen:
                    pass
            else:
                keep.append((t, gen))
        self.bg = keep

    def wload(self, src_ap):
        self.bg_step(1)
        view, key = self.wring.next()
        self.g.add("pool", lambda e: e.dma_start(out=view, in_=src_ap), writes=[key], dma=True)
        return view, key

    def rstd_from(self, bank, bkey, slot, w, nfeat):
        nc, g = self.nc, self.g
        r = self.rstd[:, slot, :w]
        g.add("act", lambda e: nc.scalar.activation(out=r, in_=bank[:, :w], func=AF.Sqrt,
                                                    bias=self.epsc[:, 0:1], scale=1.0 / nfeat),
              reads=["epsc"], writes=[("rstd", slot), bkey])
        g.add("dve", lambda e: nc.vector.reciprocal(out=r, in_=r), reads=[("rstd", slot)], writes=[("rstd", slot)])
        return r

    def ada_stage(self, L):
        nc, g = self.nc, self.g
        if L == 0:
            g.add("sp", lambda e: e.dma_start(out=self.cT[:], in_=self.dr["cT"]), writes=["cT"], dma=True)
            g.add("act", lambda e: nc.scalar.activation(out=self.scT[:], in_=self.cT[:], func=AF.Silu),
                  reads=["cT"], writes=["scT"])
        g.add("sp", lambda e: e.dma_start(out=self.adab[:], in_=self.dr["adab"][L]), writes=["adab"], dma=True)
        bank, bkey = self.ps[7], ("ps", 7)
        for n in range(144):
            wv, wk = self.wload(self.dr["adaw"][L, n])
            for kc in range(DC):
                g.add("pe", lambda e, n=n, kc=kc, wv=wv: nc.tensor.matmul(
                    bank[:, 2 * n:2 * n + 2], lhsT=wv[:, kc * 128:(kc + 1) * 128], rhs=self.scT[:, kc, :],
                    start=(kc == 0), stop=(kc == DC - 1)),
                    reads=[wk, "scT"], writes=[bkey])
        pv = bank[:, 0:288].rearrange("p (n s) -> p n s", s=2)
        for s in range(2):
            g.add("dve", lambda e, s=s: nc.vector.tensor_tensor(out=self.mod[:, :, s], in0=pv[:, :, s],
                                                                in1=self.adab[:], op=ALU.add),
                  reads=["adab"], writes=["mod", bkey])

    def mod_vectors(self, L, s, weight):
        nc, g = self.nc, self.g
        self.par ^= 1
        pr = self.par
        gpre = self.normg[:, L * 6 + s * 2 + 0, :]
        gpost = self.normg[:, L * 6 + s * 2 + 1, :]
        for j in range(2):
            sh = self.mod[:, (3 * s) * DC:(3 * s + 1) * DC, j]
            sc = self.mod[:, (3 * s + 1) * DC:(3 * s + 2) * DC, j]
            gt = self.mod[:, (3 * s + 2) * DC:(3 * s + 3) * DC, j]
            g.add("dve", lambda e, j=j, sc=sc: nc.vector.scalar_tensor_tensor(
                out=self.vecA[:, pr, :, j], in0=sc, scalar=1.0, in1=gpre, op0=ALU.add, op1=ALU.mult),
                reads=["mod", "normg"], writes=[("vecA", pr)])
            g.add("dve", lambda e, j=j, sh=sh: nc.vector.tensor_copy(out=self.vecB[:, pr, :, j], in_=sh),
                  reads=["mod"], writes=[("vecB", pr)])
            g.add("dve", lambda e, j=j, gt=gt: nc.vector.scalar_tensor_tensor(
                out=self.vecG[:, pr, :, j], in0=gt, scalar=float(weight), in1=gpost, op0=ALU.mult, op1=ALU.mult),
                reads=["mod", "normg"], writes=[("vecG", pr)])

    def prologue(self, b0, tiles):
        for _ in self.prologue_gen(b0, tiles):
            pass

    def prologue_gen(self, b0, tiles):
        nc, g = self.nc, self.g
        xs = self.dr["xs"]
        pr = self.par
        self.bg_flush(tag=b0)
        sbanks = (4, 7)
        for ti, (c0, w, is_ctx) in enumerate(tiles):
            j = 1 if is_ctx else 0
            l0 = c0 - b0
            bank, bkey = self.ps[sbanks[ti]], ("ps", sbanks[ti])
            for kc in range(DC):
                xv, xk = self.xst.next()
                g.add("sp", lambda e, xv=xv, kc=kc, c0=c0, w=w: e.dma_start(out=xv[:, :w], in_=xs[kc, :, c0:c0 + w]),
                      reads=seg("xs", kc, c0, w), writes=[xk], dma=True)
                qv, qk = self.sq.next()
                g.add("act", lambda e, xv=xv, qv=qv, w=w: nc.scalar.activation(out=qv[:, :w], in_=xv[:, :w], func=AF.Square),
                      reads=[xk], writes=[qk])
                g.add("pe", lambda e, qv=qv, kc=kc, w=w, bank=bank: nc.tensor.matmul(
                    bank[:, :w], lhsT=self.ones[:], rhs=qv[:, :w], start=(kc == 0), stop=(kc == DC - 1)),
                    reads=[qk, "ones"], writes=[bkey])
                yield
            r = self.rstd_from(bank, bkey, 2 + ti, w, D)
            for kc in range(DC):
                xv, xk = self.xst.next()
                g.add("sp", lambda e, xv=xv, kc=kc, c0=c0, w=w: e.dma_start(out=xv[:, :w], in_=xs[kc, :, c0:c0 + w]),
                      reads=seg("xs", kc, c0, w), writes=[xk], dma=True)
                tv, tk = self.tmp.next()
                g.add("dve", lambda e, xv=xv, tv=tv, kc=kc, w=w, r=r, j=j: nc.vector.scalar_tensor_tensor(
                    out=tv[:, :w], in0=xv[:, :w], scalar=self.vecA[:, pr, kc, j:j + 1], in1=r, op0=ALU.mult, op1=ALU.mult),
                    reads=[xk, ("vecA", pr), ("rstd", 2 + ti)], writes=[tk])
                g.add("act", lambda e, tv=tv, kc=kc, w=w, l0=l0, j=j: nc.scalar.activation(
                    out=self.hT[:, kc, l0:l0 + w], in_=tv[:, :w], func=AF.Identity,
                    bias=self.vecB[:, pr, kc, j:j + 1], scale=1.0),
                    reads=[tk, ("vecB", pr)], writes=seg("hT", kc, l0, w))
                yield

    def epilogue(self, b0, tiles):
        self.bg_flush()
        gen = self.epilogue_gen(b0, tiles)
        next(gen)
        self.bg.append((b0, gen))

    def epilogue_gen(self, b0, tiles):
        nc, g = self.nc, self.g
        xs, ys = self.dr["xs"], self.dr["ys"]
        pr = self.par
        rs = []
        for ti, (c0, w, is_ctx) in enumerate(tiles):
            bank, bkey = self.ps[5 + ti], ("ps", 5 + ti)
            r = self.rstdE[:, ti, :w]
            g.add("act", lambda e, r=r, bank=bank, w=w: nc.scalar.activation(
                out=r, in_=bank[:, :w], func=AF.Sqrt, bias=self.epsc[:, 0:1], scale=1.0 / D),
                reads=["epsc"], writes=[("rstdE", ti), bkey])
            g.add("dve", lambda e, r=r: nc.vector.reciprocal(out=r, in_=r), writes=[("rstdE", ti)])
            rs.append(r)
        yield
        for ti, (c0, w, is_ctx) in enumerate(tiles):
            j = 1 if is_ctx else 0
            r = rs[ti]
            for d in range(DC):
                yv, yk = self.yst.next()
                g.add("sp", lambda e, yv=yv, d=d, c0=c0, w=w: e.dma_start(out=yv[:, :w], in_=ys[d, :, c0:c0 + w]),
                      reads=seg("ys", d, c0, w), writes=[yk], dma=True)
                xv, xk = self.xst.next()
                g.add("sp", lambda e, xv=xv, d=d, c0=c0, w=w: e.dma_start(out=xv[:, :w], in_=xs[d, :, c0:c0 + w]),
                      reads=seg("xs", d, c0, w), writes=[xk], dma=True)
                g.add("dve", lambda e, yv=yv, d=d, w=w, r=r, j=j: nc.vector.scalar_tensor_tensor(
                    out=yv[:, :w], in0=yv[:, :w], scalar=self.vecG[:, pr, d, j:j + 1], in1=r, op0=ALU.mult, op1=ALU.mult),
                    reads=[yk, ("vecG", pr), ("rstdE", ti)], writes=[yk])
                g.add("dve", lambda e, yv=yv, xv=xv, w=w: nc.vector.tensor_tensor(
                    out=xv[:, :w], in0=xv[:, :w], in1=yv[:, :w], op=ALU.add),
                    reads=[yk, xk], writes=[xk])
                g.add("sp", lambda e, xv=xv, d=d, c0=c0, w=w: e.dma_start(out=xs[d, :, c0:c0 + w], in_=xv[:, :w]),
                      reads=[xk], writes=seg("xs", d, c0, w), dma=True)
                yield

    def y_evac(self, bank, bkey, d, ti, c0, w, first, last, pending):
        nc, g = self.nc, self.g
        ys = self.dr["ys"]
        yv, yk = self.yst.next()
        g.add("dve", lambda e: nc.vector.tensor_copy(out=yv[:, :w], in_=bank[:, :w]), writes=[yk, bkey])
        qv, qk = self.sqy.next()
        g.add("act", lambda e: nc.scalar.activation(out=qv[:, :w], in_=yv[:, :w], func=AF.Square),
              reads=[yk], writes=[qk])
        g.add("sp", lambda e: e.dma_start(out=ys[d, :, c0:c0 + w], in_=yv[:, :w]),
              reads=[yk], writes=seg("ys", d, c0, w), dma=True)
        sbank, skey = self.ps[5 + ti], ("ps", 5 + ti)
        pending.append(lambda: g.add("pe", lambda e: nc.tensor.matmul(
            sbank[:, :w], lhsT=self.ones[:], rhs=qv[:, :w], start=first, stop=last),
            reads=[qk, "ones"], writes=[skey]))

    def out_proj(self, wname, widx, KC, b0, tiles, inkey, inT=None, psring=None, bg=False):
        nc, g = self.nc, self.g
        if inT is None:
            inT = self.big[:, :KC * BW].rearrange("p (k t) -> p k t", t=BW)
        psring = psring or self.psring
        wsrc = self.dr[wname]
        npieces = (KC + 15) // 16
        pending = []
        for d in range(DC):
            slots = []
            for pc in range(npieces):
                k0, k1 = pc * 16, min(KC, pc * 16 + 16)
                src = wsrc[tuple(widx) + (d,)][:, k0 * 128:k1 * 128]
                self.bg_step(2)
                view, key = self.wring.next()
                vv = view[:, :(k1 - k0) * 128]
                g.add("pool", lambda e, vv=vv, src=src: e.dma_start(out=vv, in_=src), writes=[key], dma=True)
                slots.append((view, key))
            for ti, (c0, w, is_ctx) in enumerate(tiles):
                l0 = c0 - b0
                bank, bkey = psring.next()
                for kc in range(KC):
                    view, key = slots[kc // 16]
                    kl = kc % 16
                    g.add("pe", lambda e, view=view, kl=kl, kc=kc, l0=l0, w=w, bank=bank: nc.tensor.matmul(
                        bank[:, :w], lhsT=view[:, kl * 128:(kl + 1) * 128], rhs=inT[:, kc, l0:l0 + w],
                        start=(kc == 0), stop=(kc == KC - 1)),
                        reads=[key] + seg(inkey, kc, l0, w), writes=[bkey])
                while pending:
                    pending.pop(0)()
                self.y_evac(bank, bkey, d, ti, c0, w, d == 0, d == DC - 1, pending)
        while pending:
            pending.pop(0)()

    def ffn_stage(self, L, which, s, do_ctx):
        nc, g = self.nc, self.g
        self.mod_vectors(L, s, 0.5)
        hid = self.big[:, :FC * BW].rearrange("p (k t) -> p k t", t=BW)
        blocks = [(b0, [t for t in tl if do_ctx or not t[2]]) for b0, tl in BLOCKS]
        self.prologue(*blocks[0])
        for bi, (b0, tiles) in enumerate(blocks):
            self.bg_flush(tag=("P", b0))
            for fc in range(FC):
                wa, ka = self.wload(self.dr["ffn_in"][L, which, fc, 0])
                wb, kb = self.wload(self.dr["ffn_in"][L, which, fc, 1])
                for ti, (c0, w, is_ctx) in enumerate(tiles):
                    l0 = c0 - b0
                    ba, bak = self.psring4.next()
                    bb, bbk = self.psring4.next()
                    for (wv, wk, bank, bk) in ((wa, ka, ba, bak), (wb, kb, bb, bbk)):
                        for kc in range(DC):
                            g.add("pe", lambda e, wv=wv, kc=kc, l0=l0, w=w, bank=bank: nc.tensor.matmul(
                                bank[:, :w], lhsT=wv[:, kc * 128:(kc + 1) * 128], rhs=self.hT[:, kc, l0:l0 + w],
                                start=(kc == 0), stop=(kc == DC - 1)),
                                reads=[wk] + seg("hT", kc, l0, w), writes=[bk])
                    sv, sk = self.sa.next()
                    g.add("act", lambda e, sv=sv, ba=ba, w=w: nc.scalar.activation(out=sv[:, :w], in_=ba[:, :w], func=AF.Silu),
                          writes=[sk, bak])
                    g.add("dve", lambda e, sv=sv, bb=bb, fc=fc, l0=l0, w=w: nc.vector.tensor_tensor(
                        out=hid[:, fc, l0:l0 + w], in0=sv[:, :w], in1=bb[:, :w], op=ALU.mult),
                        reads=[sk], writes=seg("big", fc, l0, w) + [bbk])
            self.bg_flush()
            if bi + 1 < len(blocks):
                nb0, ntiles = blocks[bi + 1]
                self.bg.append((("P", nb0), self.prologue_gen(nb0, ntiles)))
            self.out_proj("ffn_out", (L, which), FC, b0, tiles, "big", psring=self.psring4, bg=True)
            self.bg_flush(tag=("P", blocks[bi + 1][0]) if bi + 1 < len(blocks) else "none")
            self.epilogue(b0, tiles)

    def ln_stats_add(self, src, skey, ti, w, first, last, pending, bf_dst=None, bf_key=None):
        nc, g = self.nc, self.g
        if bf_dst is None:
            bv, bk = self.sq.next()
            bk = [bk]
        else:
            bv, bk = bf_dst, bf_key
        g.add("dve", lambda e: nc.vector.tensor_copy(out=bv[:, :w], in_=src), reads=skey, writes=bk)
        qv, qk = self.sq.next()
        g.add("act", lambda e: nc.scalar.activation(out=qv[:, :w], in_=src, func=AF.Square), reads=skey, writes=[qk])
        b1, k1 = self.ps[4 + 2 * ti], ("ps", 4 + 2 * ti)
        b2, k2 = self.ps[5 + 2 * ti], ("ps", 5 + 2 * ti)

        def emit():
            g.add("pe", lambda e: nc.tensor.matmul(b1[:, :w], lhsT=self.ones[:], rhs=bv[:, :w], start=first, stop=last),
                  reads=bk + ["ones"], writes=[k1])
            g.add("pe", lambda e: nc.tensor.matmul(b2[:, :w], lhsT=self.ones[:], rhs=qv[:, :w], start=first, stop=last),
                  reads=[qk, "ones"], writes=[k2])
        pending.append(emit)

    def ln_finish(self, ti, w, nfeat):
        nc, g = self.nc, self.g
        b1, k1 = self.ps[4 + 2 * ti], ("ps", 4 + 2 * ti)
        b2, k2 = self.ps[5 + 2 * ti], ("ps", 5 + 2 * ti)
        mean = self.rstd[:, 2 * ti, :w]
        rs = self.rstd[:, 2 * ti + 1, :w]
        mk, rk = ("rstd", 2 * ti), ("rstd", 2 * ti + 1)
        g.add("act", lambda e: nc.scalar.activation(out=mean, in_=b1[:, :w], func=AF.Copy, scale=1.0 / nfeat),
              writes=[mk, k1])
        g.add("dve", lambda e: nc.vector.tensor_tensor(out=rs, in0=mean, in1=mean, op=ALU.mult), reads=[mk], writes=[rk])
        g.add("dve", lambda e: nc.vector.scalar_tensor_tensor(out=rs, in0=b2[:, :w], scalar=1.0 / nfeat, in1=rs,
                                                              op0=ALU.mult, op1=ALU.subtract),
              writes=[rk, k2])
        g.add("act", lambda e: nc.scalar.activation(out=rs, in_=rs, func=AF.Sqrt, bias=self.epsc[:, 0:1], scale=1.0),
              reads=["epsc"], writes=[rk])
        g.add("dve", lambda e: nc.vector.reciprocal(out=rs, in_=rs), writes=[rk])
        return mean, rs, mk, rk

    def load_prm(self, src, off, n, bf=False):
        dst = (self.prmb if bf else self.prm)[:, off:off + n]
        if bf:
            self.g.add("pool", lambda e: e.dma_start(out=dst, in_=src), writes=["prmb"], dma=True)
        else:
            self.g.add("sp", lambda e: e.dma_start(out=dst, in_=src), writes=["prm"], dma=True)
        return dst

    def conv_stage(self, L):
        nc, g = self.nc, self.g
        dr = self.dr
        self.mod_vectors(L, 1, 1.0)
        wdw = self.load_prm(dr["conv_wdw"], 0, CONV_K * DC).rearrange("p (k c) -> p k c", c=DC)
        bdw = self.load_prm(dr["conv_vec"][0], 512, DC)
        lng = self.load_prm(dr["conv_vec"][1], 528, DC)
        lnb = self.load_prm(dr["conv_vec"][2], 544, DC)
        zs = dr["zs"]
        zsv = zs.rearrange("c p t -> p c t")
        for a in (0, 271, 286, 2349):
            g.add("sp", lambda e, a=a: e.dma_start(out=zsv[:, :, a:a + 15], in_=self.zero[:, :, 0:15]),
                  reads=["zero"], writes=["zpad"], dma=True)

        def zcol(c0, is_ctx):
            return 15 + c0 if is_ctx else 301 + (c0 - CTX)

        for b0, tiles in BLOCKS3:
            self.prologue(b0, tiles)
            for fc in range(DC):
                wa, ka = self.wload(dr["conv_pw1"][fc, 0])
                wb, kb = self.wload(dr["conv_pw1"][fc, 1])
                for ti, (c0, w, is_ctx) in enumerate(tiles):
                    l0 = c0 - b0
                    ba, bak = self.psring.next()
                    bb, bbk = self.psring.next()
                    for (wv, wk, bank, bk) in ((wa, ka, ba, bak), (wb, kb, bb, bbk)):
                        for kc in range(DC):
                            g.add("pe", lambda e, wv=wv, kc=kc, l0=l0, w=w, bank=bank: nc.tensor.matmul(
                                bank[:, :w], lhsT=wv[:, kc * 128:(kc + 1) * 128], rhs=self.hT[:, kc, l0:l0 + w],
                                start=(kc == 0), stop=(kc == DC - 1)),
                                reads=[wk] + seg("hT", kc, l0, w), writes=[bk])
                    sv, sk = self.sa.next()
                    g.add("act", lambda e, sv=sv, bb=bb, w=w: nc.scalar.activation(out=sv[:, :w], in_=bb[:, :w], func=AF.Sigmoid),
                          writes=[sk, bbk])
                    yv, yk = self.yst.next()
                    g.add("dve", lambda e, sv=sv, ba=ba, yv=yv, w=w: nc.vector.tensor_tensor(
                        out=yv[:, :w], in0=sv[:, :w], in1=ba[:, :w], op=ALU.mult), reads=[sk], writes=[yk, bak])
                    z0 = zcol(c0, is_ctx)
                    g.add("sp", lambda e, yv=yv, fc=fc, z0=z0, w=w: e.dma_start(out=zs[fc, :, z0:z0 + w], in_=yv[:, :w]),
                          reads=[yk], writes=[("zs", fc, c0)], dma=True)
        zc = self.big[:].bitcast(F32)[:, :DC * 768].rearrange("p (c t) -> p c t", t=768)
        ident = self.load_prm(dr["ident"], 0, 128, bf=True)
        dg = self.ug[:].rearrange("p a b -> p (a b)")[:, :CONV_K * 128].rearrange("p (k n) -> p k n", n=128)
        PBW = 1152
        for b0, tiles in BLOCKS3:
            pending = []
            for fc in range(DC):
                idb = bass.AP(self.prmb, 0, [[PBW, 128], [0, CONV_K], [1, 128]])
                wdb = bass.AP(self.prm, fc, [[self.prm_w, 128], [DC, CONV_K], [0, 128]])
                g.add("dve", lambda e, idb=idb, wdb=wdb: nc.vector.tensor_tensor(out=dg, in0=idb, in1=wdb, op=ALU.mult),
                      reads=["prm", "prmb"], writes=["dg"])
                for ti, (c0, w, is_ctx) in enumerate(tiles):
                    l0 = c0 - b0
                    z0 = zcol(c0, is_ctx)
                    zv, zk = self.zin.next()
                    allz = ["zpad"] + [("zs", fc, t[0]) for t in TILES3]
                    g.add("sp", lambda e, zv=zv, fc=fc, z0=z0, w=w: e.dma_start(out=zv[:, :w + 30], in_=zs[fc, :, z0 - 15:z0 + w + 15]),
                          reads=allz, writes=[zk], dma=True)
                    zb, zbk = self.zb.next()
                    g.add("act", lambda e, zv=zv, zb=zb, w=w: nc.scalar.copy(out=zb[:, :w + 30], in_=zv[:, :w + 30]),
                          reads=[zk], writes=[zbk])
                    bank, bk = self.psring4.next()
                    for k in range(CONV_K):
                        g.add("pe", lambda e, bank=bank, zb=zb, k=k, w=w: nc.tensor.matmul(
                            bank[:, :w], lhsT=dg[:, k, :], rhs=zb[:, k:k + w], start=(k == 0), stop=(k == CONV_K - 1)),
                            reads=["dg", zbk], writes=[bk])
                    acc = zc[:, fc, l0:l0 + w]
                    ak = seg("big", fc, l0, w)
                    g.add("act", lambda e, bank=bank, acc=acc, fc=fc, w=w: nc.scalar.activation(
                        out=acc, in_=bank[:, :w], func=AF.Identity, bias=bdw[:, fc:fc + 1], scale=1.0),
                        reads=["prm"], writes=ak + [bk])
                    while pending:
                        pending.pop(0)()
                    self.ln_stats_add(acc, ak, ti, w, fc == 0, fc == DC - 1, pending)
            while pending:
                pending.pop(0)()
            for ti, (c0, w, is_ctx) in enumerate(tiles):
                l0 = c0 - b0
                mean, rs, mk, rk = self.ln_finish(ti, w, D)
                for fc in range(DC):
                    acc = zc[:, fc, l0:l0 + w]
                    ak = seg("big", fc, l0, w)
                    tv, tk = self.tmp.next()
                    g.add("dve", lambda e, tv=tv, acc=acc, mean=mean, w=w: nc.vector.tensor_tensor(
                        out=tv[:, :w], in0=acc, in1=mean, op=ALU.subtract), reads=ak + [mk], writes=[tk])
                    g.add("dve", lambda e, tv=tv, fc=fc, rs=rs, w=w: nc.vector.scalar_tensor_tensor(
                        out=tv[:, :w], in0=tv[:, :w], scalar=lng[:, fc:fc + 1], in1=rs, op0=ALU.mult, op1=ALU.mult),
                        reads=[rk, "prm"], writes=[tk])
                    g.add("act", lambda e, tv=tv, fc=fc, l0=l0, w=w: nc.scalar.activation(
                        out=self.hT[:, fc, l0:l0 + w], in_=tv[:, :w], func=AF.Silu, bias=lnb[:, fc:fc + 1], scale=1.0),
                        reads=[tk, "prm"], writes=seg("hT", fc, l0, w))
            self.out_proj("conv_pw2", (), DC, b0, tiles, "hT", inT=self.hT, psring=self.psring4)
            self.epilogue(b0, tiles)

    def gmlp_stage(self, L):
        nc, g = self.nc, self.g
        dr = self.dr
        self.mod_vectors(L, 1, 1.0)
        NV = GE // 128
        lng = self.load_prm(dr["gmlp_vec"][0], 0, NV)
        lnb = self.load_prm(dr["gmlp_vec"][1], NV, NV)
        self.load_prm(dr["gmlp_bsb"], 128, GG * 128)
        wsT = self.load_prm(dr["gmlp_wsT"], 0, GG * 128, bf=True).rearrange("p (g n) -> p g n", n=128)
        ident = self.load_prm(dr["ident"], GG * 128, 128, bf=True)
        vT = self.big[:, :NV * BW].rearrange("p (k t) -> p k t", t=BW)
        ug = self.ug
        for b0, tiles in BLOCKS3:
            self.prologue(b0, tiles)
            pending = []
            for vc in range(NV):
                wv, wk = self.wload(dr["gmlp_in"][1, vc])
                for ti, (c0, w, is_ctx) in enumerate(tiles):
                    l0 = c0 - b0
                    bank, bk = self.psring4.next()
                    for kc in range(DC):
                        g.add("pe", lambda e, wv=wv, kc=kc, l0=l0, w=w, bank=bank: nc.tensor.matmul(
                            bank[:, :w], lhsT=wv[:, kc * 128:(kc + 1) * 128], rhs=self.hT[:, kc, l0:l0 + w],
                            start=(kc == 0), stop=(kc == DC - 1)),
                            reads=[wk] + seg("hT", kc, l0, w), writes=[bk])
                    while pending:
                        pending.pop(0)()
                    sv, sk = self.sa.next()
                    g.add("act", lambda e, sv=sv, bank=bank, w=w: nc.scalar.activation(
                        out=sv[:, :w], in_=bank[:, :w], func=AF.Gelu_apprx_tanh), writes=[sk, bk])
                    self.ln_stats_add(sv[:, :w], [sk], ti, w, vc == 0, vc == NV - 1, pending,
                                      bf_dst=vT[:, vc, l0:l0 + w], bf_key=seg("big", vc, l0, w))
            while pending:
                pending.pop(0)()
            for ti, (c0, w, is_ctx) in enumerate(tiles):
                l0 = c0 - b0
                mean, rs, mk, rk = self.ln_finish(ti, w, GE)
                for vc in range(NV):
                    vv = vT[:, vc, l0:l0 + w]
                    vk = seg("big", vc, l0, w)
                    tv, tk = self.tmp.next()
                    g.add("dve", lambda e, tv=tv, vv=vv, mean=mean, w=w: nc.vector.tensor_tensor(
                        out=tv[:, :w], in0=vv, in1=mean, op=ALU.subtract), reads=vk + [mk], writes=[tk])
                    g.add("dve", lambda e, tv=tv, rs=rs, w=w: nc.vector.tensor_tensor(
                        out=tv[:, :w], in0=tv[:, :w], in1=rs, op=ALU.mult), reads=[rk], writes=[tk])
                    g.add("act", lambda e, tv=tv, vv=vv, vc=vc, w=w: nc.scalar.activation(
                        out=vv, in_=tv[:, :w], func=AF.Identity, bias=lnb[:, vc:vc + 1], scale=lng[:, vc:vc + 1]),
                        reads=[tk, "prm"], writes=vk)
            bw = sum(t[1] for t in tiles)
            nck = bw // 128
            for gi in range(GG):
                for j in range(6):
                    uc = gi * 6 + j
                    wv, wk = self.wload(dr["gmlp_in"][0, uc])
                    for ti, (c0, w, is_ctx) in enumerate(tiles):
                        l0 = c0 - b0
                        bank, bk = self.psring4.next()
                        for kc in range(DC):
                            g.add("pe", lambda e, wv=wv, kc=kc, l0=l0, w=w, bank=bank: nc.tensor.matmul(
                                bank[:, :w], lhsT=wv[:, kc * 128:(kc + 1) * 128], rhs=self.hT[:, kc, l0:l0 + w],
                                start=(kc == 0), stop=(kc == DC - 1)),
                                reads=[wk] + seg("hT", kc, l0, w), writes=[bk])
                        g.add("act", lambda e, j=j, l0=l0, bank=bank, w=w: nc.scalar.activation(
                            out=ug[:, j, l0:l0 + w], in_=bank[:, :w], func=AF.Gelu_apprx_tanh),
                            writes=seg("ug", j, l0, w) + [bk])
                for j in range(6):
                    vc = gi * 6 + j
                    for ck0 in range(0, nck, 4):
                        nb = min(4, nck - ck0)
                        cl0, cw = ck0 * 128, nb * 128
                        vk = seg("big", vc, cl0, cw)
                        pb, pk = self.psring4.next()
                        pbb = pb[:].bitcast(BF16)
                        for i in range(nb):
                            g.add("pe", lambda e, pbb=pbb, i=i, vc=vc, cl0=cl0: nc.tensor.transpose(
                                pbb[:, i * 128:(i + 1) * 128], vT[:, vc, cl0 + i * 128:cl0 + (i + 1) * 128], ident),
                                reads=vk + ["prmb"], writes=[pk])
                        mv, mkk = self.vtm.next()
                        g.add("act", lambda e, mv=mv, pbb=pbb, cw=cw: nc.scalar.copy(out=mv[:, :cw], in_=pbb[:, :cw]),
                              writes=[mkk, pk])
                        mb, mbk = self.psring4.next()
                        for i in range(nb):
                            g.add("pe", lambda e, mb=mb, mv=mv, i=i, gi=gi: nc.tensor.matmul(
                                mb[:, i * 128:(i + 1) * 128], lhsT=mv[:, i * 128:(i + 1) * 128], rhs=wsT[:, gi, :],
                                start=True, stop=True), reads=[mkk, "prmb"], writes=[mbk])
                        bsv = bass.AP(self.prm, 128 + gi * 128, [[self.prm_w, 128], [0, nb], [1, 128]])
                        tv, tk = self.tmp.next()
                        g.add("dve", lambda e, tv=tv, mb=mb, bsv=bsv, nb=nb, cw=cw: nc.vector.tensor_tensor(
                            out=tv[:, :cw].rearrange("p (b n) -> p b n", n=128),
                            in0=mb[:, :cw].rearrange("p (b n) -> p b n", n=128), in1=bsv, op=ALU.add),
                            reads=["prm"], writes=[tk, mbk])
                        g.add("dve", lambda e, tv=tv, j=j, vc=vc, cl0=cl0, cw=cw: nc.vector.tensor_tensor(
                            out=vT[:, vc, cl0:cl0 + cw], in0=tv[:, :cw], in1=ug[:, j, cl0:cl0 + cw], op=ALU.mult),
                            reads=[tk] + seg("ug", j, cl0, cw), writes=vk)
            self.out_proj("gmlp_out", (), NV, b0, tiles, "big", inT=vT, psring=self.psring4)
            self.epilogue(b0, tiles)

    def prologue_to_dram(self):
        g = self.g
        hs = self.dr["hs"]
        hsv = hs.rearrange("c p t -> p c t")
        for b0, tiles in BLOCKS3:
            self.prologue(b0, tiles)
            bw = sum(t[1] for t in tiles)
            rk = [k for kc in range(DC) for k in seg("hT", kc, 0, bw)]
            g.add("sp", lambda e, b0=b0, bw=bw: e.dma_start(out=hsv[:, :, b0:b0 + bw], in_=self.hT[:, :, :bw]),
                  reads=rk, writes=[("hs", b0)], dma=True)

    def ret_stage(self, L, r, ctx_out):
        nc, g = self.nc, self.g
        dr = self.dr
        NCH = T // 128
        self.mod_vectors(L, 1, 1.0)
        P = self.prm
        PW = self.prm_w
        self.load_prm(dr["ret_logit"][r], 0, 16)
        gng = self.load_prm(dr["ret_gng"][r], 16, 32)
        self.load_prm(dr["ret_const"], 64, 770)
        ident = self.load_prm(dr["ident"], 0, 128, bf=True)
        lg = P[:, 0:16]
        posd = [P[:, 64:192], P[:, 192:320]]
        msk = [P[:, 320:448], P[:, 448:576]]
        colc = [P[:, 576:704], P[:, 704:832]]
        pcol = [P[:, 832:833], P[:, 833:834]]
        DT = [P[:, 896:1024], P[:, 1024:1152]]
        QD = P[:, 1152:1408]
        sm = self.small
        g.add("act", lambda e: nc.scalar.activation(out=lg, in_=lg, func=AF.Exp, scale=-1.0), reads=["prm"], writes=["prm"])
        g.add("act", lambda e: nc.scalar.activation(out=lg, in_=lg, func=AF.Ln, bias=1.0, scale=1.0), reads=["prm"], writes=["prm"])
        g.add("dve", lambda e: nc.vector.tensor_single_scalar(out=lg, in_=lg, scalar=-1.0, op=ALU.mult),
              reads=["prm"], writes=["prm"])
        self.prologue_to_dram()

        big = self.big
        qT = big[:, 0:2 * T].rearrange("p (c t) -> p c t", t=T)
        kT = big[:, 2 * T:4 * T].rearrange("p (c t) -> p c t", t=T)
        sgT = big[:, 4 * T:8 * T].rearrange("p (c t) -> p c t", t=T)
        vtm = big[:, 8 * T:12 * T].rearrange("p (j e) -> p j e", e=512)
        ktm = [big[:, 12 * T:14 * T].rearrange("p (j d) -> p j d", d=256),
               big[:, 14 * T:16 * T].rearrange("p (j d) -> p j d", d=256)]
        ugf = self.ug[:].rearrange("p a b -> p (a b)").bitcast(F32)
        S = [ugf[:, 0:1024].rearrange("p (c e) -> p c e", e=512), ugf[:, 1024:2048].rearrange("p (c e) -> p c e", e=512)]
        hsv = dr["hs"].rearrange("c p t -> p c t")
        sbs = dr["sbs"]
        ogs = dr["ogs"]
        rope = dr["rope"]

        for h in range(RET_H):
            for di in range(2):
                col = di * 8 + h
                g.add("act", lambda e, di=di, col=col: nc.scalar.activation(out=DT[di], in_=posd[di], func=AF.Exp,
                                                                            scale=lg[:, col:col + 1]),
                      reads=["prm"], writes=[("DT", di)])
                g.add("dve", lambda e, di=di: nc.vector.tensor_tensor(out=DT[di], in0=DT[di], in1=msk[di], op=ALU.mult),
                      reads=["prm"], writes=[("DT", di)])
                g.add("act", lambda e, di=di, col=col: nc.scalar.activation(out=QD[:, di * 128:(di + 1) * 128], in_=colc[di],
                                                                            func=AF.Exp, scale=lg[:, col:col + 1]),
                      reads=["prm"], writes=[("QD", di)])
                g.add("act", lambda e, di=di, col=col: nc.scalar.activation(out=sm[:, di:di + 1], in_=pcol[di], func=AF.Exp,
                                                                            scale=lg[:, col:col + 1]),
                      reads=["prm"], writes=[("sm", di)])
                g.add("dve", lambda e, di=di: nc.vector.tensor_single_scalar(out=sm[:, di:di + 1], in_=sm[:, di:di + 1],
                                                                             scalar=1.0 / 16.0, op=ALU.mult),
                      writes=[("sm", di)])
                g.add("act", lambda e, di=di, col=col: nc.scalar.activation(out=sm[:, 2 + di:3 + di], in_=lg[:, col:col + 1],
                                                                            func=AF.Exp, scale=128.0),
                      reads=["prm"], writes=[("sm", 2 + di)])
            for b0, tiles in BLOCKS3:
                bw = sum(t[1] for t in tiles)
                g.add("sp", lambda e, b0=b0, bw=bw: e.dma_start(out=self.hT[:, :, :bw], in_=hsv[:, :, b0:b0 + bw]),
                      reads=[("hs", b0)], writes=[k for kc in range(DC) for k in seg("hT", kc, 0, bw)], dma=True)
                for qk in range(2):
                    dstT = qT if qk == 0 else kT
                    dname = "qT" if qk == 0 else "kT"
                    for dc in range(2):
                        w1, k1 = self.wload(dr["ret_in"][r, h, qk * 4 + dc])
                        w2, k2 = self.wload(dr["ret_in"][r, h, qk * 4 + 2 + dc])
                        for ti, (c0, w, is_ctx) in enumerate(tiles):
                            l0 = c0 - b0
                            b1, bk1 = self.psring.next()
                            b2, bk2 = self.psring.next()
                            for (wv, wk, bank, bk) in ((w1, k1, b1, bk1), (w2, k2, b2, bk2)):
                                for kc in range(DC):
                                    g.add("pe", lambda e, wv=wv, kc=kc, l0=l0, w=w, bank=bank: nc.tensor.matmul(
                                        bank[:, :w], lhsT=wv[:, kc * 128:(kc + 1) * 128], rhs=self.hT[:, kc, l0:l0 + w],
                                        start=(kc == 0), stop=(kc == DC - 1)),
                                        reads=[wk] + seg("hT", kc, l0, w), writes=[bk])
                            rk = ("rope", 0)
                            g.add("sp", lambda e, dc=dc, c0=c0, w=w: e.dma_start(
                                out=self.ropest[:, :, :w], in_=rope[:, :, dc, c0:c0 + w].rearrange("a p t -> p a t")),
                                writes=[rk], dma=True)
                            tv, tk = self.tmp.next()
                            g.add("dve", lambda e, tv=tv, b1=b1, dc=dc, w=w: nc.vector.tensor_tensor(
                                out=tv[:, :w], in0=b1[:, :w], in1=self.ropest[:, 0, :w], op=ALU.mult),
                                reads=[rk], writes=[tk, bk1])
                            sv, sk = self.sa.next()
                            g.add("dve", lambda e, sv=sv, b2=b2, dc=dc, w=w: nc.vector.tensor_tensor(
                                out=sv[:, :w], in0=b2[:, :w], in1=self.ropest[:, 1, :w], op=ALU.mult),
                                reads=[rk], writes=[sk, bk2])
                            g.add("dve", lambda e, tv=tv, sv=sv, dstT=dstT, dc=dc, c0=c0, w=w: nc.vector.tensor_tensor(
                                out=dstT[:, dc, c0:c0 + w], in0=tv[:, :w], in1=sv[:, :w], op=ALU.add),
                                reads=[tk, sk], writes=seg(dname, dc, c0, w))
                for ec in range(4):
                    wv, wk = self.wload(dr["ret_in"][r, h, 8 + ec])
                    for ti, (c0, w, is_ctx) in enumerate(tiles):
                        l0 = c0 - b0
                        bank, bk = self.psring.next()
                        for kc in range(DC):
                            g.add("pe", lambda e, wv=wv, kc=kc, l0=l0, w=w, bank=bank: nc.tensor.matmul(
                                bank[:, :w], lhsT=wv[:, kc * 128:(kc + 1) * 128], rhs=self.hT[:, kc, l0:l0 + w],
                                start=(kc == 0), stop=(kc == DC - 1)),
                                reads=[wk] + seg("hT", kc, l0, w), writes=[bk])
                        tv, tk = self.tmp.next()
                        g.add("act", lambda e, tv=tv, bank=bank, w=w: nc.scalar.activation(out=tv[:, :w], in_=bank[:, :w], func=AF.Silu),
                              writes=[tk, bk])
                        g.add("dve", lambda e, tv=tv, ec=ec, c0=c0, w=w, h=h: nc.vector.tensor_single_scalar(
                            out=sgT[:, ec, c0:c0 + w], in_=tv[:, :w], scalar=gng[:, h * 4 + ec:h * 4 + ec + 1],
                            op=ALU.mult), reads=[tk, "prm"], writes=seg("sgT", ec, c0, w))
                vw = [self.wload(dr["ret_v"][r, h, p4]) for p4 in range(4)]
                for j in range(b0 // 128, (b0 + bw) // 128):
                    l0 = j * 128 - b0
                    bank, bk = self.psring.next()
                    for kc in range(DC):
                        wv, wk = vw[kc // 4]
                        kl = kc % 4
                        g.add("pe", lambda e, wv=wv, kl=kl, kc=kc, l0=l0, bank=bank: nc.tensor.matmul(
                            bank[:, :], lhsT=self.hT[:, kc, l0:l0 + 128], rhs=wv[:, kl * 512:(kl + 1) * 512],
                            start=(kc == 0), stop=(kc == DC - 1)),
                            reads=[wk] + seg("hT", kc, l0, 128), writes=[bk])
                    g.add("act", lambda e, j=j, bank=bank: nc.scalar.copy(out=vtm[:, j, :], in_=bank[:, :]),
                          writes=[("vtok", j), bk])
            for j0 in range(0, NCH, 2):
                pb, pk = self.psring.next()
                pbb = pb[:].bitcast(BF16)
                for jj in range(2):
                    for dc in range(2):
                        j = j0 + jj
                        g.add("pe", lambda e, pbb=pbb, jj=jj, dc=dc, j=j: nc.tensor.transpose(
                            pbb[:, (jj * 2 + dc) * 128:(jj * 2 + dc + 1) * 128], kT[:, dc, j * 128:(j + 1) * 128], ident),
                            reads=seg("kT", dc, j * 128, 128) + ["prmb"], writes=[pk])
                for di in range(2):
                    g.add("dve", lambda e, di=di, pbb=pbb, j0=j0: nc.vector.tensor_single_scalar(
                        out=ktm[di][:, j0:j0 + 2, :], in_=pbb[:, 0:512].rearrange("p (j d) -> p j d", d=256),
                        scalar=sm[:, di:di + 1], op=ALU.mult),
                        reads=[("sm", di)], writes=[("ktm", di, j0), ("ktm", di, j0 + 1), pk])

            def kv_update(di, j):
                for dc in range(2):
                    bank, bk = self.psring.next()
                    g.add("pe", lambda e, bank=bank, dc=dc: nc.tensor.matmul(
                        bank[:, :], lhsT=ktm[di][:, j, dc * 128:(dc + 1) * 128], rhs=vtm[:, j, :], start=True, stop=True),
                        reads=[("ktm", di, j), ("vtok", j)], writes=[bk])
                    g.add("dve", lambda e, bank=bank, dc=dc: nc.vector.scalar_tensor_tensor(
                        out=S[di][:, dc, :], in0=S[di][:, dc, :], scalar=sm[:, 2 + di:3 + di], in1=bank[:, :],
                        op0=ALU.mult, op1=ALU.add),
                        reads=[("sm", 2 + di)], writes=[("S", di), bk])

            def zero_state(di):
                g.add("dve", lambda e: nc.vector.memset(S[di], 0.0), writes=[("S", di)])

            def bwd_store(j):
                sv, sk = self.sbin.next()
                g.add("act", lambda e, sv=sv: nc.scalar.copy(out=sv[:, :].rearrange("p (c e) -> p c e", e=512), in_=S[1]),
                      reads=[("S", 1)], writes=[sk])
                g.add("sp", lambda e, sv=sv, j=j: e.dma_start(out=sbs[j], in_=sv[:, :]), reads=[sk], writes=[("sbs", j)], dma=True)

            zero_state(1)
            for j in (1, 0):
                bwd_store(j)
                kv_update(1, j)
            for j in range(NCH - 1, 1, -1):
                bwd_store(j)
                kv_update(1, j)

            zero_state(0)
            for j in range(NCH):
                c0 = j * 128
                fb = self.sfb[:, j % 2, :].rearrange("p (c e) -> p c e", e=512)
                fk = ("sfb", j % 2)
                g.add("act", lambda e, fb=fb: nc.scalar.copy(out=fb, in_=S[0]), reads=[("S", 0)], writes=[fk])
                bv, bkk = self.sbin.next()
                g.add("sp", lambda e, bv=bv, j=j: e.dma_start(out=bv[:, :], in_=sbs[j]), reads=[("sbs", j)], writes=[bkk], dma=True)
                bvv = bv[:, :].rearrange("p (c e) -> p c e", e=512)
                pb, pk = self.psring.next()
                for dc in range(2):
                    g.add("pe", lambda e, pb=pb, dc=dc, c0=c0: nc.tensor.matmul(
                        pb[:, 0:128], lhsT=kT[:, dc, c0:c0 + 128], rhs=qT[:, dc, c0:c0 + 128], start=(dc == 0), stop=(dc == 1)),
                        reads=seg("kT", dc, c0, 128) + seg("qT", dc, c0, 128), writes=[pk])
                av, ak = self.vtm.next()
                for di in range(2):
                    g.add("dve", lambda e, av=av, pb=pb, di=di: nc.vector.tensor_tensor(
                        out=av[:, di * 128:(di + 1) * 128], in0=pb[:, 0:128], in1=DT[di], op=ALU.mult),
                        reads=[("DT", di)], writes=[ak, pk])
                qv, qk_ = self.sq.next()
                qdst = qv[:, :].rearrange("p (a c n) -> p a c n", a=2, c=2)
                for di in range(2):
                    qdv = bass.AP(self.prm, 1152 + di * 128, [[PW, 128], [0, 2], [1, 128]])
                    g.add("dve", lambda e, qdst=qdst, qdv=qdv, c0=c0, di=di: nc.vector.tensor_tensor(
                        out=qdst[:, di, :, :], in0=qT[:, :, c0:c0 + 128], in1=qdv, op=ALU.mult),
                        reads=seg("qT", 0, c0, 128) + seg("qT", 1, c0, 128) + [("QD", di)], writes=[qk_])
                qd = qv[:, :].rearrange("p (a c n) -> p a c n", a=2, c=2)
                ob, ok = self.psring.next()
                mms = [(av[:, 0:128], vtm[:, j, :], [ak, ("vtok", j)]), (av[:, 128:256], vtm[:, j, :], [ak, ("vtok", j)])]
                for dc in range(2):
                    mms.append((qd[:, 0, dc, :], fb[:, dc, :], [qk_, fk]))
                    mms.append((qd[:, 1, dc, :], bvv[:, dc, :], [qk_, bkk]))
                for i, (lh, rh, rd) in enumerate(mms):
                    g.add("pe", lambda e, ob=ob, lh=lh, rh=rh, i=i, n=len(mms): nc.tensor.matmul(
                        ob[:, :], lhsT=lh, rhs=rh, start=(i == 0), stop=(i == n - 1)), reads=rd, writes=[ok])
                ot, otk = self.yst.next()
                g.add("act", lambda e, ot=ot, ob=ob: nc.scalar.copy(out=ot[:, :512], in_=ob[:, :]), writes=[otk, ok])
                q2, q2k = self.xst.next()
                g.add("act", lambda e, ot=ot, q2=q2: nc.scalar.activation(out=q2[:, :512], in_=ot[:, :512], func=AF.Square),
                      reads=[otk], writes=[q2k])
                sj = 8 + (j % 2) * 8
                st_ = sm[:, sj:sj + 8]
                stk = ("sm", "st", j % 2)
                g.add("dve", lambda e, ot=ot, st_=st_: nc.vector.reduce_sum(out=st_[:, 0:1], in_=ot[:, :512], axis=mybir.AxisListType.X),
                      reads=[otk], writes=[stk])
                g.add("dve", lambda e, q2=q2, st_=st_: nc.vector.reduce_sum(out=st_[:, 1:2], in_=q2[:, :512], axis=mybir.AxisListType.X),
                      reads=[q2k], writes=[stk])
                g.add("dve", lambda e, st_=st_: nc.vector.tensor_single_scalar(out=st_[:, 2:3], in_=st_[:, 0:1], scalar=1.0 / 512,
                                                                               op=ALU.mult), writes=[stk])
                g.add("dve", lambda e, st_=st_: nc.vector.tensor_tensor(out=st_[:, 3:4], in0=st_[:, 2:3], in1=st_[:, 2:3], op=ALU.mult),
                      writes=[stk])
                g.add("dve", lambda e, st_=st_: nc.vector.scalar_tensor_tensor(out=st_[:, 4:5], in0=st_[:, 1:2], scalar=1.0 / 512,
                                                                               in1=st_[:, 3:4], op0=ALU.mult, op1=ALU.subtract),
                      writes=[stk])
                g.add("act", lambda e, st_=st_: nc.scalar.activation(out=st_[:, 5:6], in_=st_[:, 4:5], func=AF.Sqrt,
                                                                     bias=self.epsc[:, 0:1], scale=1.0),
                      reads=["epsc"], writes=[stk])
                g.add("dve", lambda e, st_=st_: nc.vector.reciprocal(out=st_[:, 6:7], in_=st_[:, 5:6]), writes=[stk])
                onv, onk = self.onst.next()
                g.add("dve", lambda e, ot=ot, onv=onv, st_=st_: nc.vector.tensor_scalar(
                    out=onv[:, :], in0=ot[:, :512], scalar1=st_[:, 2:3], scalar2=st_[:, 6:7], op0=ALU.subtract, op1=ALU.mult),
                    reads=[otk, stk], writes=[onk])
                tb, tbk = self.psring.next()
                tbb = tb[:].bitcast(BF16)
                for ec in range(4):
                    g.add("pe", lambda e, tbb=tbb, onv=onv, ec=ec: nc.tensor.transpose(
                        tbb[:, ec * 128:(ec + 1) * 128], onv[:, ec * 128:(ec + 1) * 128], ident),
                        reads=[onk, "prmb"], writes=[tbk])
                gv, gk = self.ogst.next()
                g.add("dve", lambda e, gv=gv, tbb=tbb, c0=c0: nc.vector.tensor_tensor(
                    out=gv[:, :].rearrange("p (c n) -> p c n", n=128), in0=tbb[:, 0:512].rearrange("p (c n) -> p c n", n=128),
                    in1=sgT[:, :, c0:c0 + 128], op=ALU.mult),
                    reads=[k for ec in range(4) for k in seg("sgT", ec, c0, 128)], writes=[gk, tbk])
                g.add("sp", lambda e, gv=gv, c0=c0, h=h: e.dma_start(
                    out=ogs[h * 4:(h + 1) * 4, :, c0:c0 + 128].rearrange("c p n -> p c n"),
                    in_=gv[:, :].rearrange("p (c n) -> p c n", n=128)),
                    reads=[gk], writes=[("ogs", h, j)], dma=True)
                if j < NCH - 1:
                    kv_update(0, j)

        ogv = ogs.rearrange("c p t -> p c t")
        inT = self.big[:, :32 * BW].rearrange("p (k t) -> p k t", t=BW)
        for b0, tiles in BLOCKS3:
            bw = sum(t[1] for t in tiles)
            otiles = [t for t in tiles if ctx_out or not t[2]]
            rd = [("ogs", h, j) for h in range(RET_H) for j in range(b0 // 128, (b0 + bw) // 128)]
            wr = [k for kc in range(32) for k in seg("big", kc, 0, bw)]
            allbig = [k for nm, n in (("qT", 2), ("kT", 2), ("sgT", 4)) for c in range(n) for k in seg(nm, c, 0, T)] + \
                     [("vtok", j) for j in range(NCH)] + [("ktm", d, j) for d in range(2) for j in range(NCH)]
            g.add("sp", lambda e, b0=b0, bw=bw: e.dma_start(out=inT[:, :, :bw], in_=ogv[:, :, b0:b0 + bw]),
                  reads=rd, writes=wr + allbig, dma=True)
            self.out_proj("ret_out", (r,), 32, b0, otiles, "big", inT=inT)
            self.epilogue(b0, otiles)
        self.g.add("dve", lambda e: nc.vector.memset(self.small[:, 32:33], 0.0),
                   reads=[k for kc in range(32) for k in seg("big", kc, 0, BW)], writes=allbig)

def lay_w_in(w, nchunk_cols=128):
    K, N = w.shape
    a = w.reshape(K // 128, 128, N // 128, 128)
    return np.ascontiguousarray(a.transpose(2, 1, 0, 3)).reshape(N // 128, 128, (K // 128) * 128)


def lay_vec(v):
    sh = v.shape[:-1]
    c = v.shape[-1] // 128
    a = v.reshape(sh + (c, 128))
    return np.ascontiguousarray(np.moveaxis(a, -1, 0))


def lay_xT(x):
    t, d = x.shape
    return np.ascontiguousarray(x.T.reshape(d // 128, 128, t))


def declare_dram(nc, shapes):
    out = {}
    for n, v in shapes.items():
        sh, kind = v[0], v[1]
        dt_ = v[2] if len(v) > 2 else F32
        out[n] = (nc.dram_tensor(n, list(sh), dt_, kind=kind) if kind else nc.dram_tensor(n, list(sh), dt_)).ap()
    return out


def copy_x0(p):
    g = p.g
    for kc in range(DC):
        g.add("sp", lambda e, kc=kc: e.dma_start(out=p.dr["xs"][kc], in_=p.dr["x0"][kc]),
              writes=seg("xs", kc, 0, T), dma=True)


def finish(p):
    p.bg_flush()
    g = p.g
    keys = [k for kc in range(DC) for k in seg("xs", kc, 0, T)]
    g.add("sp", lambda e: e.dma_start(out=p.dr["done"], in_=p.epsc[0:1, 0:1]), reads=keys + ["epsc"], dma=True, is_output=True)


def ret_consts():
    m = np.arange(128)[:, None]
    n = np.arange(128)[None, :]
    c = np.zeros((128, 770), np.float32)
    c[:, 0:128] = np.where(n >= m, n - m, 0)
    c[:, 128:256] = np.where(m > n, m - n, 0)
    c[:, 256:384] = np.where(n >= m, 1 / 16, 0)
    c[:, 384:512] = np.where(m > n, 1 / 16, 0)
    c[:, 512:640] = n + 1
    c[:, 640:768] = 128 - n
    c[:, 768] = 127 - np.arange(128)
    c[:, 769] = np.arange(128)
    return c


def rope_tables():
    half = 64
    inv = 10000.0 ** (-np.arange(half, dtype=np.float32) / half)
    pos = np.arange(SEQ)
    rows, cols = (pos // 64).astype(np.float32), (pos % 64).astype(np.float32)
    out = np.zeros((2, 128, 2, T), np.float32)
    out[0, :, :, :CTX] = 1.0
    for dc, pp in enumerate((rows, cols)):
        ang = (pp[None, :] * inv[:, None]).astype(np.float32)
        out[0, :64, dc, CTX:] = np.cos(ang)
        out[0, 64:, dc, CTX:] = np.cos(ang)
        out[1, :64, dc, CTX:] = -np.sin(ang)
        out[1, 64:, dc, CTX:] = np.sin(ang)
    return out


def lay_ret_in(w):
    sw = np.concatenate([np.arange(64, 128), np.arange(0, 64)])
    outs = []
    for h in range(RET_H):
        cols = []
        for base in (0, RET_QK):
            hb = base + h * RET_DK
            for dc in range(2):
                cols.append(np.arange(hb + dc * 128, hb + dc * 128 + 128))
            for dc in range(2):
                cols.append(hb + dc * 128 + sw)
        gb = 2 * RET_QK + RET_V + h * RET_DV
        for ec in range(4):
            cols.append(np.arange(gb + ec * 128, gb + ec * 128 + 128))
        outs.append(lay_w_in(w[:, np.concatenate(cols)]))
    vs = []
    for h in range(RET_H):
        wv = w[:, 2 * RET_QK + h * RET_DV:2 * RET_QK + (h + 1) * RET_DV]
        a = wv.reshape(4, 4, 128, RET_DV).transpose(0, 2, 1, 3)
        vs.append(np.ascontiguousarray(a).reshape(4, 128, 2048))
    return np.stack(outs), np.stack(vs)


SHAPES = {
    "x0": ((DC, 128, T), "ExternalInput"), "cT": ((128, DC, 2), "ExternalInput"),
    "normg": ((128, DEPTH * 6, DC), "ExternalInput"), "adab": ((DEPTH, 128, 144), "ExternalInput"),
    "adaw": ((DEPTH, 144, 128, 2048), "ExternalInput"),
    "ffn_in": ((DEPTH, 2, FC, 2, 128, 2048), "ExternalInput"), "ffn_out": ((DEPTH, 2, DC, 128, DFF), "ExternalInput"),
    "ret_logit": ((2, 128, 16), "ExternalInput"), "ret_gng": ((2, 128, 32), "ExternalInput"),
    "ret_const": ((128, 770), "ExternalInput"), "ident": ((128, 128), "ExternalInput"),
    "rope": ((2, 128, 2, T), "ExternalInput"),
    "ret_in": ((2, RET_H, 12, 128, 2048), "ExternalInput"), "ret_v": ((2, RET_H, 4, 128, 2048), "ExternalInput"),
    "ret_out": ((2, DC, 128, RET_V), "ExternalInput"),
    "gmlp_vec": ((2, 128, GE // 128), "ExternalInput"), "gmlp_bsb": ((128, GG * 128), "ExternalInput"),
    "gmlp_wsT": ((128, GG * 128), "ExternalInput"),
    "gmlp_in": ((2, GE // 128, 128, 2048), "ExternalInput"), "gmlp_out": ((DC, 128, GE), "ExternalInput"),
    "conv_wdw": ((128, CONV_K * DC), "ExternalInput"), "conv_vec": ((3, 128, DC), "ExternalInput"),
    "conv_pw1": ((DC, 2, 128, 2048), "ExternalInput"), "conv_pw2": ((DC, 128, 2048), "ExternalInput"),
    "xs": ((DC, 128, T), "ExternalOutput"), "done": ((1, 1), "ExternalOutput"),
    "ys": ((DC, 128, T), None), "zs": ((DC, 128, 2364), None),
    "hs": ((DC, 128, T), None, BF16), "sbs": ((T // 128, 128, 1024), None, BF16), "ogs": ((32, 128, T), None, BF16),
}


def build_program():
    nc = bass.Bass("TRN2", target_bir_lowering=False)
    with ExitStack() as st:
        dr = declare_dram(nc, SHAPES)
        p = Prog(nc, st, dr)
        copy_x0(p)
        for L in range(DEPTH):
            kind, inst = L % 3, L // 3
            ctx_out = L != DEPTH - 1
            ctx_needed = ctx_out or kind == 0
            p.ada_stage(L)
            p.ffn_stage(L, 0, 0, ctx_needed)
            if kind == 0:
                p.ret_stage(L, inst, ctx_out)
            elif kind == 1:
                p.gmlp_stage(L)
            else:
                p.conv_stage(L)
            p.ffn_stage(L, 1, 2, ctx_out)
        finish(p)
        p.g.emit(st)
    return nc


def host_layout(x, c, ctx, c_ctx, ada_w, ada_b, norm_g, ffn_w_in, ffn_w_out, ret_w_in, ret_w_out,
                ret_decay_logit, ret_gn_g, gmlp_w_in, gmlp_ln_g, gmlp_ln_b, gmlp_w_s, gmlp_b_s, gmlp_w_out,
                conv_w_pw1, conv_w_dw, conv_b_dw, conv_ln_g, conv_ln_b, conv_w_pw2):
    f = lambda a: np.asarray(a, dtype=np.float32)
    sh = {}
    sh["normg"] = lay_vec(f(norm_g).reshape(DEPTH * 6, D))
    sh["adab"] = np.ascontiguousarray(lay_vec(f(ada_b)).transpose(1, 0, 2))
    sh["adaw"] = np.stack([lay_w_in(f(ada_w[L])) for L in range(DEPTH)])
    fin = []
    for L in range(DEPTH):
        row = []
        for wch in range(2):
            wi = lay_w_in(f(ffn_w_in[L, wch]))
            row.append(np.stack([wi[:FC], wi[FC:]], 1))
        fin.append(np.stack(row))
    sh["ffn_in"] = np.stack(fin)
    sh["ffn_out"] = np.stack([np.stack([lay_w_in(f(ffn_w_out[L, wch])) for wch in range(2)]) for L in range(DEPTH)])
    lg = f(ret_decay_logit).reshape(2, 1, 16)
    sh["ret_logit"] = np.ascontiguousarray(np.broadcast_to(lg, (2, 128, 16)))
    sh["ret_gng"] = np.ascontiguousarray(lay_vec(f(ret_gn_g)).transpose(1, 0, 2))
    sh["ret_const"] = ret_consts()
    sh["ident"] = np.eye(128, dtype=np.float32)
    sh["rope"] = rope_tables()
    ri = [lay_ret_in(f(ret_w_in[r])) for r in range(2)]
    sh["ret_in"] = np.stack([a for a, _ in ri])
    sh["ret_v"] = np.stack([b for _, b in ri])
    sh["ret_out"] = np.stack([lay_w_in(f(ret_w_out[r])) for r in range(2)])
    NV = GE // 128
    sh["gmlp_vec"] = np.ascontiguousarray(lay_vec(np.stack([f(gmlp_ln_g[0]), f(gmlp_ln_b[0])])).transpose(1, 0, 2))
    sh["gmlp_bsb"] = np.ascontiguousarray(np.broadcast_to(f(gmlp_b_s[0]).reshape(1, GG * 128), (128, GG * 128)))
    sh["gmlp_wsT"] = np.ascontiguousarray(f(gmlp_w_s[0]).transpose(2, 0, 1)).reshape(128, GG * 128)
    wi = lay_w_in(f(gmlp_w_in[0]))
    sh["gmlp_in"] = np.ascontiguousarray(np.stack([wi[:NV], wi[NV:]], 0))
    sh["gmlp_out"] = lay_w_in(f(gmlp_w_out[0]))
    sh["conv_wdw"] = np.ascontiguousarray(lay_vec(f(conv_w_dw[0]))).reshape(128, CONV_K * DC)
    sh["conv_vec"] = np.ascontiguousarray(
        lay_vec(np.stack([f(conv_b_dw[0]), f(conv_ln_g[0]), f(conv_ln_b[0])])).transpose(1, 0, 2))
    wi = lay_w_in(f(conv_w_pw1[0]))
    sh["conv_pw1"] = np.ascontiguousarray(np.stack([wi[:DC], wi[DC:]], 1))
    sh["conv_pw2"] = lay_w_in(f(conv_w_pw2[0]))
    maps = []
    for b in range(NB):
        m = dict(sh)
        m["x0"] = lay_xT(np.concatenate([f(ctx[b]), f(x[b])], 0))
        m["cT"] = np.ascontiguousarray(lay_vec(np.stack([f(c[b]), f(c_ctx)])).transpose(0, 2, 1))
        maps.append(m)
    return maps


_NC_CACHE = {}


def kernel(**inputs):
    maps = host_layout(**inputs)
    for k, v in maps[0].items():
        assert tuple(v.shape) == tuple(SHAPES[k][0]), (k, v.shape, SHAPES[k][0])
    if "nc" not in _NC_CACHE:
        _NC_CACHE["nc"] = build_program()
    res = run_bass_kernel_spmd(_NC_CACHE["nc"], maps, core_ids=list(range(N_CORES)))
    out = np.empty((NB, SEQ, D), np.float32)
    for b in range(NB):
        xs = np.asarray(res.results[b]["xs"]).reshape(D, T)
        out[b] = xs[:, CTX:].T
    return out
```
